# Optimizing a Trainium2 kernel written in Bass

```python
import jax, jax.numpy as jnp
from jax import lax
import numpy as np

D_MODEL = 2048
BATCH = 2
SEQ = 4096
DEPTH = 1
DEC_BATCH = 8
DEC_SEQ = 4
PAST_LEN = 16384
PAGE_SIZE = 128

A_HEADS = 8
A_KV_HEADS = 4
A_HEAD_DIM = 128
A_WIDTH = A_HEADS * A_HEAD_DIM
A_KV_WIDTH = A_KV_HEADS * A_HEAD_DIM
IDX_HEADS = 16
IDX_DIM = 64
TOPK_MAX = 256
Q_BLOCK = 128
ROPE_THETA = 10000.0
CHUNK = 128
B_GROUPS = 8
B_GROUP_DIM = 128
B_WIDTH = B_GROUPS * B_GROUP_DIM
N_MEM = 256
M_HEADS = 4
M_HEAD_DIM = 256
M_WIDTH = M_HEADS * M_HEAD_DIM
EPS = 1e-6

SPLIT_SIZES = (A_WIDTH, A_KV_WIDTH, A_KV_WIDTH, IDX_HEADS * IDX_DIM, IDX_DIM, IDX_HEADS, A_WIDTH,
               B_WIDTH, B_WIDTH, B_WIDTH,
               M_WIDTH, M_WIDTH,
               D_MODEL, D_MODEL, D_MODEL)
D_IN = sum(SPLIT_SIZES)

kernel_name = "hybrid_dsa_chunkmlp_memxattn_step"


def _rmsnorm(x, g):
    x32 = x.astype(jnp.float32)
    y = x32 * lax.rsqrt(jnp.mean(x32 * x32, axis=-1, keepdims=True) + EPS)
    return y.astype(x.dtype) * g


def _rope(x, pos):
    half = x.shape[-1] // 2
    freq = ROPE_THETA ** (-jnp.arange(half, dtype=jnp.float32) / half)
    ang = pos.astype(jnp.float32)[:, None] * freq[None, :]
    cos = jnp.cos(ang)[None, :, None, :]
    sin = jnp.sin(ang)[None, :, None, :]
    x32 = x.astype(jnp.float32)
    x1, x2 = x32[..., :half], x32[..., half:]
    return jnp.concatenate([x1 * cos - x2 * sin, x2 * cos + x1 * sin], axis=-1).astype(x.dtype)


def _project(x, g_pre, w_in):
    z = _rmsnorm(x, g_pre) @ w_in
    return jnp.split(z, list(np.cumsum(SPLIT_SIZES)[:-1]), axis=-1)


def _prep_a(q, k, v, qi, ki, wi, pos, g_q, g_k):
    Bn, T = q.shape[:2]
    q = _rope(_rmsnorm(q.reshape(Bn, T, A_HEADS, A_HEAD_DIM), g_q), pos)
    k = _rope(_rmsnorm(k.reshape(Bn, T, A_KV_HEADS, A_HEAD_DIM), g_k), pos)
    v = v.reshape(Bn, T, A_KV_HEADS, A_HEAD_DIM)
    qi = _rope(qi.reshape(Bn, T, IDX_HEADS, IDX_DIM), pos)
    ki = _rope(ki[:, :, None, :], pos)[:, :, 0, :]
    wi = wi * (IDX_HEADS ** -0.5)
    return q, k, v, qi, ki, wi


def _index_scores(qi, ki, wi):
    dots = jnp.einsum('bthd,bsd->bths', qi, ki).astype(jnp.float32) * (IDX_DIM ** -0.5)
    return jnp.einsum('bths,bth->bts', jax.nn.relu(dots), wi.astype(jnp.float32))


def _sparse_attend(q, kg, vg, valid):
    Bn, T = q.shape[:2]
    qg = q.reshape(Bn, T, A_KV_HEADS, A_HEADS // A_KV_HEADS, A_HEAD_DIM)
    s = jnp.einsum('btgrd,btkgd->btgrk', qg, kg).astype(jnp.float32) * (A_HEAD_DIM ** -0.5)
    s = jnp.where(valid[:, :, None, None, :], s, -jnp.inf)
    p = jax.nn.softmax(s, axis=-1).astype(vg.dtype)
    o = jnp.einsum('btgrk,btkgd->btgrd', p, vg)
    return o.reshape(Bn, T, A_WIDTH)


_gather_rows = jax.vmap(lambda rows, idx: rows[idx])


def _dsa_prompt(q, k, v, qi, ki, wi, topk):
    Bn, T = q.shape[:2]
    key_pos = jnp.arange(T)

    def block(i):
        start = i * Q_BLOCK
        qb = lax.dynamic_slice_in_dim(q, start, Q_BLOCK, axis=1)
        qib = lax.dynamic_slice_in_dim(qi, start, Q_BLOCK, axis=1)
        wib = lax.dynamic_slice_in_dim(wi, start, Q_BLOCK, axis=1)
        pos = start + jnp.arange(Q_BLOCK)
        sc = _index_scores(qib, ki, wib)
        sc = jnp.where((key_pos[None, :] <= pos[:, None])[None], sc, -jnp.inf)
        _, idx = lax.top_k(sc, topk)
        valid = idx <= pos[None, :, None]
        return _sparse_attend(qb, _gather_rows(k, idx), _gather_rows(v, idx), valid)

    out = lax.map(block, jnp.arange(T // Q_BLOCK))
    return out.transpose(1, 0, 2, 3).reshape(Bn, T, A_WIDTH)


def _dsa_sample(q, k_new, v_new, qi, ki_new, wi, cache_k, cache_v, cache_kidx, page_table, topk):
    Bn, T = q.shape[:2]
    n_past = page_table.shape[1] * PAGE_SIZE
    ki_past = cache_kidx[page_table].reshape(Bn, n_past, IDX_DIM)
    ki_all = jnp.concatenate([ki_past, ki_new], axis=1)
    L = n_past + T
    pos = n_past + jnp.arange(T)
    sc = _index_scores(qi, ki_all, wi)
    sc = jnp.where((jnp.arange(L)[None, :] <= pos[:, None])[None], sc, -jnp.inf)
    _, idx = lax.top_k(sc, topk)
    valid = idx <= pos[None, :, None]
    in_past = idx < n_past
    pidx = jnp.minimum(idx, n_past - 1)
    phys = _gather_rows(page_table, pidx // PAGE_SIZE)
    off = pidx % PAGE_SIZE
    nidx = jnp.clip(idx - n_past, 0, T - 1)
    sel = in_past[..., None, None]
    kg = jnp.where(sel, cache_k[phys, off], _gather_rows(k_new, nidx))
    vg = jnp.where(sel, cache_v[phys, off], _gather_rows(v_new, nidx))
    return _sparse_attend(q, kg, vg, valid)


def _chunk_mix(u, v, g_sgu, w_s, b_s):
    Bn, T, _ = v.shape
    vn = _rmsnorm(v, g_sgu).reshape(Bn, T, B_GROUPS, B_GROUP_DIM)
    n_chunks = -(-T // CHUNK)
    vp = jnp.pad(vn, ((0, 0), (0, n_chunks * CHUNK - T), (0, 0), (0, 0)))
    vp = vp.reshape(Bn, n_chunks, CHUNK, B_GROUPS, B_GROUP_DIM)
    tril = jnp.tril(jnp.ones((CHUNK, CHUNK), dtype=bool))
    ws = jnp.where(tril[None], w_s, 0)
    s = jnp.einsum('gpq,bnqgc->bnpgc', ws, vp) + b_s.T[None, None, :, :, None]
    s = s.reshape(Bn, n_chunks * CHUNK, B_WIDTH)[:, :T]
    return u * s, vn


def _mem_kv(mem, g_mem, w_mem_kv, g_mk):
    Bn, M, _ = mem.shape
    k, v = jnp.split(_rmsnorm(mem, g_mem) @ w_mem_kv, 2, axis=-1)
    k = _rmsnorm(k.reshape(Bn, M, M_HEADS, M_HEAD_DIM), g_mk)
    return k, v.reshape(Bn, M, M_HEADS, M_HEAD_DIM)


def _mem_attend(q, g_mq, mk, mv):
    Bn, T = q.shape[:2]
    q = _rmsnorm(q.reshape(Bn, T, M_HEADS, M_HEAD_DIM), g_mq)
    s = jnp.einsum('bthd,bmhd->bhtm', q, mk).astype(jnp.float32) * (M_HEAD_DIM ** -0.5)
    p = jax.nn.softmax(s, axis=-1).astype(mv.dtype)
    return jnp.einsum('bhtm,bmhd->bthd', p, mv).reshape(Bn, T, M_WIDTH)


def _merge(x, oa, ga, ob, gb, oc, gc, ra, rb, rc, w_pa, w_pb, w_pc, w_out):
    m = (jax.nn.sigmoid(ra) * ((oa * jax.nn.silu(ga)) @ w_pa)
         + jax.nn.sigmoid(rb) * ((ob * jax.nn.silu(gb)) @ w_pb)
         + jax.nn.sigmoid(rc) * ((oc * jax.nn.silu(gc)) @ w_pc))
    return x + m @ w_out


def setup_inputs(seed: int = 0) -> dict:
    key = jax.random.key(seed)
    ks = jax.random.split(key, 26)
    n_pages = PAST_LEN // PAGE_SIZE
    n_used = DEC_BATCH * n_pages
    n_pool = n_used + max(1, n_used // 4)
    nrm = lambda k, shape, s=1.0: jax.random.normal(k, shape, jnp.float32) * s
    gain = lambda k, n: 1.0 + 0.01 * jax.random.normal(k, (n,), jnp.float32)
    page_table = jax.random.permutation(ks[0], n_pool)[:n_used].reshape(DEC_BATCH, n_pages).astype(jnp.int32)
    return {
        "x_prompt": nrm(ks[1], (BATCH, SEQ, D_MODEL)),
        "x_sample": nrm(ks[2], (DEC_BATCH, DEC_SEQ, D_MODEL)),
        "cache_k": nrm(ks[3], (n_pool, PAGE_SIZE, A_KV_HEADS, A_HEAD_DIM)),
        "cache_v": nrm(ks[4], (n_pool, PAGE_SIZE, A_KV_HEADS, A_HEAD_DIM)),
        "cache_kidx": nrm(ks[5], (n_pool, PAGE_SIZE, IDX_DIM)),
        "cache_mem_k": nrm(ks[6], (DEC_BATCH, N_MEM, M_HEADS, M_HEAD_DIM)),
        "cache_mem_v": nrm(ks[7], (DEC_BATCH, N_MEM, M_HEADS, M_HEAD_DIM)),
        "page_table": page_table,
        "mem_prompt": nrm(ks[8], (BATCH, N_MEM, D_MODEL)),
        "g_pre": gain(ks[9], D_MODEL),
        "w_in": nrm(ks[10], (D_MODEL, D_IN), D_MODEL ** -0.5),
        "g_q": gain(ks[11], A_HEAD_DIM),
        "g_k": gain(ks[12], A_HEAD_DIM),
        "g_mq": gain(ks[13], M_HEAD_DIM),
        "g_mk": gain(ks[14], M_HEAD_DIM),
        "g_mem": gain(ks[15], D_MODEL),
        "w_mem_kv": nrm(ks[16], (D_MODEL, 2 * M_WIDTH), D_MODEL ** -0.5),
        "g_sgu": gain(ks[17], B_WIDTH),
        "w_s": nrm(ks[18], (B_GROUPS, CHUNK, CHUNK), CHUNK ** -0.5),
        "b_s": 1.0 + 0.02 * jax.random.normal(ks[19], (B_GROUPS, CHUNK), jnp.float32),
        "w_pa": nrm(ks[20], (A_WIDTH, D_MODEL), A_WIDTH ** -0.5),
        "w_pb": nrm(ks[21], (B_WIDTH, D_MODEL), B_WIDTH ** -0.5),
        "w_pc": nrm(ks[22], (M_WIDTH, D_MODEL), M_WIDTH ** -0.5),
        "w_out": nrm(ks[23], (D_MODEL, D_MODEL), D_MODEL ** -0.5),
    }


def reference(x_prompt, x_sample, cache_k, cache_v, cache_kidx, cache_mem_k, cache_mem_v, page_table,
              mem_prompt, g_pre, w_in, g_q, g_k, g_mq, g_mk, g_mem, w_mem_kv, g_sgu, w_s, b_s,
              w_pa, w_pb, w_pc, w_out):
    T = x_prompt.shape[1]
    y_prompt = x_prompt
    for _ in range(DEPTH):
        (aq, ak, av, aqi, aki, awi, ag, bu, bv, bg, cq, cg, ra, rb, rc) = _project(y_prompt, g_pre, w_in)
        q, k_p, v_p, qi, ki_p, wi = _prep_a(aq, ak, av, aqi, aki, awi, jnp.arange(T), g_q, g_k)
        oa = _dsa_prompt(q, k_p, v_p, qi, ki_p, wi, min(TOPK_MAX, T // 4))
        ob, _ = _chunk_mix(bu, bv, g_sgu, w_s, b_s)
        mk_p, mv_p = _mem_kv(mem_prompt, g_mem, w_mem_kv, g_mk)
        oc = _mem_attend(cq, g_mq, mk_p, mv_p)
        y_prompt = _merge(y_prompt, oa, ag, ob, bg, oc, cg, ra, rb, rc, w_pa, w_pb, w_pc, w_out)

    Ts = x_sample.shape[1]
    n_past = page_table.shape[1] * PAGE_SIZE
    y_sample = x_sample
    for _ in range(DEPTH):
        (aq, ak, av, aqi, aki, awi, ag, bu, bv, bg, cq, cg, ra, rb, rc) = _project(y_sample, g_pre, w_in)
        q, k_s, v_s, qi, ki_s, wi = _prep_a(aq, ak, av, aqi, aki, awi, n_past + jnp.arange(Ts), g_q, g_k)
        oa = _dsa_sample(q, k_s, v_s, qi, ki_s, wi, cache_k, cache_v, cache_kidx, page_table,
                         min(TOPK_MAX, (n_past + Ts) // 4))
        ob, chunk_v_s = _chunk_mix(bu, bv, g_sgu, w_s, b_s)
        oc = _mem_attend(cq, g_mq, cache_mem_k, cache_mem_v)
        y_sample = _merge(y_sample, oa, ag, ob, bg, oc, cg, ra, rb, rc, w_pa, w_pb, w_pc, w_out)

    return (y_prompt, y_sample, k_p, v_p, ki_p, mk_p, mv_p, k_s, v_s, ki_s, chunk_v_s)
```

```python
import numpy as np
import concourse.bass as bass
import concourse.mybir as mybir
from concourse.bass_utils import run_bass_kernel_spmd

F32 = mybir.dt.float32
BF16 = mybir.dt.bfloat16
I32 = mybir.dt.int32
AF = mybir.ActivationFunctionType
ALU = mybir.AluOpType
AX = mybir.AxisListType

D = 2048
KC = 16
NB = 32
NG = 8
TOWN = 1024
NS = 4
EPS = 1e-6
NEG = -1.0e30
NPG = 128
C_AQ, C_AK, C_AV, C_AQI, C_AKI, C_AWI, C_AG = 0, 1024, 1536, 2048, 3072, 3136, 3152
C_BU, C_BV, C_BG, C_CQ, C_CG, C_RA, C_RB, C_RC = 4176, 5200, 6224, 7248, 8272, 9296, 11344, 13392


class Op:
    __slots__ = ("eng", "fn", "reads", "writes", "dma", "deps", "need_inc", "tick", "idx")


class Prog:
    ENGS = ("pe", "act", "dve", "pool", "sp")

    def __init__(self):
        self.ops = []
        self.last_w = {}
        self.readers = {}

    def add(self, eng, fn, reads=(), writes=(), dma=None):
        op = Op()
        op.eng, op.fn, op.dma = eng, fn, dma
        op.reads, op.writes = tuple(reads), tuple(writes)
        op.need_inc = dma is not None
        op.tick = None
        op.idx = len(self.ops)
        deps = []
        for r in op.reads:
            w = self.last_w.get(r)
            if w is not None:
                deps.append(w)
        for w_ in op.writes:
            w = self.last_w.get(w_)
            if w is not None:
                deps.append(w)
            deps.extend(self.readers.get(w_, ()))
        seen = set()
        op.deps = []
        for d in deps:
            if d.idx in seen or d is op:
                continue
            seen.add(d.idx)
            if d.dma is None and d.eng == "pe" and eng == "pe" and dma is None:
                continue
            op.deps.append(d)
            d.need_inc = True
        for r in op.reads:
            lst = self.readers.setdefault(r, [])
            if dma is None:
                lst[:] = [o for o in lst if not (o.dma is None and o.eng == eng)]
            lst.append(op)
        for w_ in op.writes:
            self.last_w[w_] = op
            self.readers[w_] = []
        self.ops.append(op)
        return op

    def emit(self, nc, stack):
        CH = 20000
        cnt = {e: 0 for e in self.ENGS}
        dma_cnt = {}
        for op in self.ops:
            if op.dma is not None:
                dma_cnt[op.dma] = dma_cnt.get(op.dma, 0) + 1
                op.tick = (("dma", op.dma), dma_cnt[op.dma] * 16)
            elif op.need_inc:
                cnt[op.eng] += 1
                t = cnt[op.eng] - 1
                op.tick = ((op.eng, t // CH), t % CH + 1)
        sems = {}

        def sem(key):
            if key not in sems:
                sems[key] = stack.enter_context(nc.semaphore("s%d" % len(sems)))
            return sems[key]

        for op in self.ops:
            if op.tick is not None:
                sem(op.tick[0])
        final_waits = [(("dma", k), v * 16) for k, v in dma_cnt.items()]
        block = stack.enter_context(nc.Block())
        prog = self

        def run(engname, eng):
            waited = {}
            for op in prog.ops:
                if op.eng != engname:
                    continue
                for d in op.deps:
                    key, val = d.tick
                    if waited.get(key, 0) >= val:
                        continue
                    waited[key] = val
                    eng.wait_ge(sems[key], val)
                ins = op.fn(eng)
                if op.tick is not None:
                    ins.then_inc(sems[op.tick[0]], 16 if op.dma is not None else 1)
            if engname == "sp":
                for key, val in final_waits:
                    if waited.get(key, 0) < val:
                        eng.wait_ge(sems[key], val)

        @block.sync
        def _(e):
            run("sp", e)

        @block.gpsimd
        def _(e):
            run("pool", e)

        @block.vector
        def _(e):
            run("dve", e)

        @block.scalar
        def _(e):
            run("act", e)

        @block.tensor
        def _(e):
            run("pe", e)
        print("ops:", len(self.ops), "sems:", len(sems), {e: cnt[e] for e in cnt})


class Ctx:
    def __init__(self, nc, stack):
        self.nc, self.stack = nc, stack
        self.P = Prog()
        self.rot = {}
        self.nuniq = 0

    def sb(self, name, shape, dt):
        self.nuniq += 1
        return self.stack.enter_context(self.nc.sbuf_tensor("s%d_%s" % (self.nuniq, name), list(shape), dt))

    def ps(self, name, shape, dt):
        self.nuniq += 1
        return self.stack.enter_context(self.nc.psum_tensor("p%d_%s" % (self.nuniq, name), list(shape), dt))

    def pool(self, name, n, shape, dt, psum=False):
        bufs = [(self.ps if psum else self.sb)("%s%d" % (name, i), shape, dt) for i in range(n)]
        self.rot[name] = [bufs, 0]

    def nxt(self, name):
        bufs, i = self.rot[name]
        self.rot[name][1] = i + 1
        k = i % len(bufs)
        return bufs[k], (name, k)

    def dma(self, out, in_, reads, writes, key, q="sp"):
        self.P.add(q, lambda e: e.dma_start(out=out, in_=in_), reads, writes, dma=key)

    def mm(self, out, lhsT, rhs, start, stop, reads, writes):
        self.P.add("pe", lambda e: e.matmul(out, lhsT, rhs, start=start, stop=stop), reads, writes)

    def tr(self, out, in_, ident, reads, writes):
        self.P.add("pe", lambda e: e.transpose(out, in_, ident), reads, writes)

    def op(self, eng, fn, reads, writes):
        self.P.add(eng, fn, reads, writes)


def _act(c, out, in_, func, reads, writes, **kw):
    c.op("act", lambda e: e.activation(out=out, in_=in_, func=func, **kw), reads, writes)


def _tt(c, eng, out, a, b, op, reads, writes):
    c.op(eng, lambda e: e.tensor_tensor(out=out, in0=a, in1=b, op=op), reads, writes)


def _ts(c, eng, out, a, s1, s2, op0, op1, reads, writes, accum_out=None):
    if accum_out is None:
        c.op(eng, lambda e: e.tensor_scalar(out=out, in0=a, scalar1=s1, scalar2=s2, op0=op0, op1=op1), reads, writes)
    else:
        c.op(eng, lambda e: e.tensor_scalar(out=out, in0=a, scalar1=s1, scalar2=s2, op0=op0, op1=op1,
                                            accum_out=accum_out), reads, writes)


def _copy(c, eng, out, in_, reads, writes):
    if eng == "act":
        c.op("act", lambda e: e.copy(out=out, in_=in_), reads, writes)
    else:
        c.op(eng, lambda e: e.tensor_copy(out=out, in_=in_), reads, writes)


def _recip(c, out, in_, reads, writes):
    c.op("dve", lambda e: e.reciprocal(out=out, in_=in_), reads, writes)


def _bc(ap2d, n):
    return ap2d.unsqueeze(2).to_broadcast([ap2d.shape[0], ap2d.shape[1], n])


class Pipe3:
    def __init__(self):
        import os
        self.on = os.environ.get("ROPE_PIPE", "1") == "1"
        self.items = []

    def push(self, fa, fb, fc):
        S = fa()
        if not self.on:
            fb(S)
            fc(S)
            return
        self.items.append((S, fb, fc))
        n = len(self.items)
        if n >= 3:
            S2, _, fc2 = self.items[n - 3]
            fc2(S2)
        if n >= 2:
            S1, fb1, _ = self.items[n - 2]
            fb1(S1)

    def flush(self):
        if not self.on:
            return
        n = len(self.items)
        if n >= 2:
            S2, _, fc2 = self.items[n - 2]
            fc2(S2)
        if n >= 1:
            S1, fb1, fc1 = self.items[n - 1]
            fb1(S1)
            fc1(S1)
        self.items = []

class K:
    def __init__(self, nc, stack):
        self.c = Ctx(nc, stack)
        self.nc = nc
        self.stack = stack
        self.dr = {}

    def din(self, name, shape, dt=F32):
        self.dr[name] = self.nc.dram_tensor(name, list(shape), dt, kind="ExternalInput").ap()
        return self.dr[name]

    def dout(self, name, shape, dt=F32):
        self.dr[name] = self.nc.dram_tensor(name, list(shape), dt, kind="ExternalOutput").ap()
        return self.dr[name]

    def consts(self):
        c, dr = self.c, self.dr
        self.identb = c.sb("identb", [128, 128], BF16)
        self.identf = c.sb("identf", [128, 128], F32)
        self.onesf = c.sb("onesf", [128, 128], F32)
        self.R128 = c.sb("R128", [128, 128], F32)
        self.R64 = c.sb("R64", [128, 128], F32)
        self.gpreT = c.sb("gpreT", [128, 16], F32)
        self.gk = c.sb("gk", [128, 1], F32)
        self.gq = c.sb("gq", [128, 1], F32)
        self.epsc = c.sb("epsc", [128, 1], F32)
        self.ckt = c.sb("ckt", [128, 32], F32)
        c.dma(self.ckt[:], self.din("ckt", [128, 32])[:, :], [], ["ckt"], "c11")
        cf = self.din("cf32", [128, 4, 128])
        c.dma(self.identf[:], cf[:, 0, :], [], ["identf"], "c0")
        c.dma(self.onesf[:], cf[:, 1, :], [], ["onesf"], "c1")
        c.dma(self.R128[:], cf[:, 2, :], [], ["R128"], "c2")
        c.dma(self.R64[:], cf[:, 3, :], [], ["R64"], "c3")
        c.dma(self.identb[:], cf[:, 0, :], [], ["identb"], "c4", q="pool")
        c.dma(self.gpreT[:], self.din("gpreT", [128, 16])[:, :], [], ["gpreT"], "c5")
        c.dma(self.gq[:], self.din("gq", [128, 1])[:, :], [], ["gq"], "c6")
        c.dma(self.gk[:], self.din("gk", [128, 1])[:, :], [], ["gk"], "c7")
        c.op("dve", lambda e: e.memset(self.epsc[:], EPS), [], ["epsc"])

    def norm_block(self, xrows, nt, dst_list, gT=None, gkey="gpreT"):
        c = self.c
        gT = self.gpreT if gT is None else gT
        xb, kx = c.nxt("xblk")
        c.dma(xb[0:nt, :], xrows, [], [kx], kx)
        ss, kss = c.nxt("ss")
        xn, kxn = c.nxt("xn")
        _act(c, xn[0:nt, :], xb[0:nt, :], AF.Square, [kx], [kxn, kss], accum_out=ss[0:nt, 0:1])
        _act(c, ss[0:nt, 1:2], ss[0:nt, 0:1], AF.Sqrt, [kss, "epsc"], [(kss, 1)], scale=1.0 / D, bias=self.epsc[0:nt, :])
        _recip(c, ss[0:nt, 2:3], ss[0:nt, 1:2], [(kss, 1)], [(kss, 2)])
        _ts(c, "dve", xn[0:nt, :], xb[0:nt, :], ss[0:nt, 2:3], None, ALU.mult, ALU.bypass, [kx, (kss, 2)], [kxn])
        for h in range(2):
            pt, kpt = c.nxt("pT")
            for k8 in range(8):
                kc = h * 8 + k8
                c.tr(pt[:, k8, 0:nt], xn[0:nt, kc * 128:(kc + 1) * 128], self.identb[0:nt, 0:nt],
                     [kxn, "identb"], [kpt])
            for (dst, kd) in dst_list:
                _tt(c, "dve", dst[:, h * 8:(h + 1) * 8, :], pt[:, :, 0:nt], _bc(gT[:, h * 8:(h + 1) * 8], nt),
                    ALU.mult, [kpt, gkey], [kd])

    def rope_B(self, S):
        c = self.c
        nt, ps, kps = S["nt"], S["ps"], S["kps"]
        kf, kkf = c.nxt("f_kf")
        _copy(c, "act", kf[:, 0:nt], ps, [kps], [kkf])
        S["kf"], S["kkf"] = kf, kkf
        if S["normalize"]:
            sq, ksq = c.nxt("f_sq")
            _act(c, sq[:, 0:nt], ps, AF.Square, [kps], [ksq])
            p2, kp2 = c.nxt("pB")
            c.mm(p2[:, 0:nt], self.onesf[:], sq[:, 0:nt], True, True, [ksq, "onesf"], [kp2])
            S["p2"], S["kp2"] = p2, kp2

    def rope_C(self, S):
        c = self.c
        nt, kf, kkf = S["nt"], S["kf"], S["kkf"]
        g_ap, gkey, R, cos, sin, krope, inv_d = S["g_ap"], S["gkey"], S["R"], S["cos"], S["sin"], S["krope"], S["inv_d"]
        if S["normalize"]:
            p2, kp2 = S["p2"], S["kp2"]
            rs, krs = c.nxt("f_rs")
            _act(c, rs[:, 0:nt], p2[:, 0:nt], AF.Sqrt, [kp2, "epsc"], [krs], scale=inv_d, bias=self.epsc[:])
            _recip(c, rs[:, 0:nt], rs[:, 0:nt], [krs], [krs])
            kn, kkn = c.nxt("f_kn")
            c.op("dve", lambda e: e.scalar_tensor_tensor(out=kn[:, 0:nt], in0=kf[:, 0:nt], scalar=g_ap, in1=rs[:, 0:nt],
                                                         op0=ALU.mult, op1=ALU.mult), [kkf, krs, gkey], [kkn])
        else:
            kn, kkn = kf, kkf
        p3, kp3 = c.nxt("pB")
        c.mm(p3[:, 0:nt], R[:], kn[:, 0:nt], True, True, [kkn, "R128", "R64"], [kp3])
        t1, kt1 = c.nxt("f_t1")
        _tt(c, "pool", t1[:, 0:nt], kn[:, 0:nt], cos, ALU.mult, [kkn, krope], [kt1])
        t2, kt2 = c.nxt("f_t2")
        _tt(c, "dve", t2[:, 0:nt], p3[:, 0:nt], sin, ALU.mult, [kp3, krope], [kt2])
        kr, kkr = c.nxt("f_kr")
        _tt(c, "pool", kr[:, 0:nt], t1[:, 0:nt], t2[:, 0:nt], ALU.add, [kt1, kt2], [kkr])
        return kr, kkr

    def rope_state(self, ps, kps, nt, g_ap, gkey, R, cos, sin, krope, inv_d, normalize=True):
        return dict(ps=ps, kps=kps, nt=nt, g_ap=g_ap, gkey=gkey, R=R, cos=cos, sin=sin, krope=krope, inv_d=inv_d,
                    normalize=normalize)

    def rope_fm(self, ps, kps, nt, g_ap, gkey, R, cos, sin, krope, inv_d, normalize=True):
        S = self.rope_state(ps, kps, nt, g_ap, gkey, R, cos, sin, krope, inv_d, normalize)
        self.rope_B(S)
        return self.rope_C(S)

    def alloc_f(self):
        c = self.c
        for n in ("f_kf", "f_sq", "f_rs", "f_kn", "f_t1", "f_t2", "f_kr"):
            c.pool(n, 1, [128, 512], F32)

    def phase1(self):
        c, dr, nc = self.c, self.dr, self.nc
        xall = self.din("xall", [4096, D])
        xs = self.din("xs", [NS, D])
        wkvi_d = self.din("w_in", [D, 15440])
        rope = self.din("rope", [9, 128, 4, 512])
        ko = self.dout("k_out", [TOWN, 4, 128])
        vo = self.dout("v_out", [TOWN, 4, 128])
        kio = self.dout("ki_out", [TOWN, 64])
        kso = self.dout("ks_out", [NS, 4, 128])
        vso = self.dout("vs_out", [NS, 4, 128])
        kiso = self.dout("kis_out", [NS, 64])
        c.op("pool", lambda e: e.memset(self.VE[:, :, :, 128:129], 1.0), [], ["VE1"])
        c.op("pool", lambda e: e.memset(self.VEs[:, :, 128:129], 1.0), [], ["VEs1"])
        with ExitStackLike(self) as st:
            old = c.stack
            c.stack = st
            c.pool("xblk", 2, [128, D], F32)
            c.pool("xn", 1, [128, D], BF16)
            c.pool("ss", 4, [128, 4], F32)
            hTg = c.sb("hTg", [128, KC, 512], BF16)
            W = c.sb("Wkvi", [128, KC, 1152], BF16)
            c.pool("ropeT", 1, [128, 4, 512], F32)
            self.alloc_f()
            c.pool("kout", 1, [128, 4, 128], F32)
            c.pool("vout", 1, [128, 4, 128], F32)
            c.pool("kiout", 1, [128, 64], F32)
            c.pool("pT", 2, [128, 8, 128], BF16, psum=True)
            c.pool("pA", 3, [128, 512], F32, psum=True)
            c.pool("pB", 3, [128, 512], F32, psum=True)
            wv = wkvi_d.rearrange("(kc p) n -> p kc n", p=128)
            for kc4 in range(4):
                sl = slice(kc4 * 4, kc4 * 4 + 4)
                c.dma(W[:, sl, 0:1024], wv[:, sl, C_AK:C_AK + 1024], [], [("W", 0, kc4)], ("W", 0, kc4), q="pool")
                c.dma(W[:, sl, 1024:1088], wv[:, sl, C_AKI:C_AKI + 64], [], [("W", 1, kc4)], ("W", 1, kc4), q="pool")
                c.dma(W[:, sl, 1088:1152], wv[:, sl, C_AKI:C_AKI + 64], [], [("W", 2, kc4)], ("W", 2, kc4), q="pool")
            wkeys = [("W", a, b) for a in range(3) for b in range(4)]
            import os
            LV = int(os.environ.get("DBG_LV", "9"))
            for g in (range(NG + 1) if LV >= 9 else [0]):
                samp = g == NG
                nt = NS if samp else 512
                nown = NS if samp else 128
                if samp:
                    self.norm_block(xs[:, :], NS, [(hTg[:, :, 0:NS], "hTg")])
                else:
                    for s in range(4):
                        p = 4 * g + s
                        self.norm_block(xall[p * 128:(p + 1) * 128, :], 128, [(hTg[:, :, s * 128:(s + 1) * 128], "hTg")])
                if LV < 1:
                    continue
                rt, krt = c.nxt("ropeT")
                c.dma(rt[:], rope[g], [], [krt], krt)
                kout, kko = c.nxt("kout")
                pipe = Pipe3()
                for gh in range(4):
                    def fa(gh=gh):
                        ps, kps = c.nxt("pA")
                        for kc in range(KC):
                            c.mm(ps[:, 0:nt], W[:, kc, gh * 128:(gh + 1) * 128], hTg[:, kc, 0:nt], kc == 0, kc == KC - 1,
                                 ["hTg"] + wkeys, [kps])
                        return self.rope_state(ps[:, 0:nt], kps, nt, self.gk[:, 0:1], "gk", self.R128, rt[:, 0, 0:nt],
                                               rt[:, 1, 0:nt], krt, 1.0 / 128)

                    def fc(S, gh=gh):
                        kr, kkr = self.rope_C(S)
                        if samp:
                            _copy(c, "act", self.KTs[:, gh, :], kr[:, 0:nt], [kkr], [("KTs", gh)])
                        else:
                            _copy(c, "act", self.KT[:, gh, g * 512:(g + 1) * 512], kr[:, 0:nt], [kkr], [("KT", gh, g)])
                        po, kpo = c.nxt("pB")
                        c.tr(po[0:nown, 0:128], kr[:, 0:nown], self.identf[:, :], [kkr, "identf"], [kpo])
                        _copy(c, "act", kout[0:nown, gh, :], po[0:nown, 0:128], [kpo], [(kko, gh)])
                        if gh == 3:
                            dst = kso[:, :, :] if samp else ko[g * 128:(g + 1) * 128, :, :]
                            c.dma(dst, kout[0:nown, :, :], [(kko, i) for i in range(4)], [], kko, q="act")
                    pipe.push(fa, self.rope_B, fc)

                def fa_i():
                    ps, kps = c.nxt("pA")
                    for kc in range(KC):
                        c.mm(ps[:, 0:nt], W[:, kc, 1024:1152], hTg[:, kc, 0:nt], kc == 0, kc == KC - 1, ["hTg"] + wkeys, [kps])
                    return self.rope_state(ps[:, 0:nt], kps, nt, None, None, self.R64, rt[:, 2, 0:nt], rt[:, 3, 0:nt], krt,
                                           0.0, normalize=False)

                def fc_i(S):
                    kr, kkr = self.rope_C(S)
                    if samp:
                        _copy(c, "act", self.kiTs[:, :], kr[:, 0:nt], [kkr], ["kiTs"])
                    else:
                        _copy(c, "act", self.kiT[:, g * 512:(g + 1) * 512], kr[:, 0:nt], [kkr], [("kiT", g)])
                    po, kpo = c.nxt("pB")
                    c.tr(po[0:nown, 0:64], kr[0:64, 0:nown], self.identf[0:64, 0:64], [kkr, "identf"], [kpo])
                    kiout, kkio = c.nxt("kiout")
                    _copy(c, "act", kiout[0:nown, :], po[0:nown, 0:64], [kpo], [kkio])
                    c.dma(kiso[:, :] if samp else kio[g * 128:(g + 1) * 128, :], kiout[0:nown, :], [kkio], [], kkio, q="act")
                pipe.push(fa_i, self.rope_B, fc_i)
                pipe.flush()
                if LV < 6:
                    continue
                for s in range(1 if samp else 4):
                    n1 = NS if samp else 128
                    ps, kps = c.nxt("pA")
                    for kc in range(KC):
                        c.mm(ps[0:n1, :], hTg[:, kc, s * 128:s * 128 + n1], W[:, kc, 512:1024], kc == 0, kc == KC - 1,
                             ["hTg"] + wkeys, [kps])
                    if samp:
                        _copy(c, "act", self.VEs[0:NS, :, 0:128], ps[0:NS, :].rearrange("p (g d) -> p g d", g=4), [kps],
                              ["VEs"])
                    else:
                        _copy(c, "act", self.VE[:, 4 * g + s, :, 0:128], ps[:, :].rearrange("p (g d) -> p g d", g=4),
                              [kps], [("VE", 4 * g + s)])
                    if s == 0 and LV >= 7:
                        vout, kvo = c.nxt("vout")
                        _copy(c, "act", vout[0:n1, :, :], ps[0:n1, :].rearrange("p (g d) -> p g d", g=4), [kps], [kvo])
                        c.dma(vso[:, :, :] if samp else vo[g * 128:(g + 1) * 128, :, :], vout[0:n1, :, :], [kvo], [], kvo, q="act")
            if LV >= 9:
                self.extras(hTg, W, wkeys)
            c.stack = old

    def extras(self, hTg, W, wkeys):
        c = self.c
        w_in = self.dr["w_in"].rearrange("(kc p) n -> p kc n", p=128)
        wm = self.din("w_mem_kv", [D, 2048]).rearrange("(kc p) n -> p kc n", p=128)
        memx = self.din("memx", [256, D])
        cvo = self.dout("cvs_out", [NS, 1024])
        mko = self.dout("mk_out", [256, 1024])
        mvo = self.dout("mv_out", [256, 1024])
        gsgu = c.sb("gsgu", [128, 1024], F32)
        gmk = c.sb("gmk", [128, 256], F32)
        gmemT = c.sb("gmemT", [128, 16], F32)
        mst = c.sb("mst", [128, 12], F32)
        c.dma(gsgu[:], self.din("gsgu_bc", [128, 1024])[:, :], [], ["gsgu"], "c8")
        c.dma(gmk[:], self.din("gmk_bc", [128, 256])[:, :], [], ["gmk"], "c9")
        c.dma(gmemT[:], self.din("gmemT", [128, 16])[:, :], [], ["gmemT"], "c10")

        def loadW(src, col0):
            for kc4 in range(4):
                sl = slice(kc4 * 4, kc4 * 4 + 4)
                c.dma(W[:, sl, 0:1024], src[:, sl, col0:col0 + 1024], [], wkeys + ["WX"], ("WX", kc4), q="pool")

        loadW(w_in, C_BV)
        ob, kob = c.nxt("xblk")
        for half in range(2):
            ps, kps = c.nxt("pA")
            for kc in range(KC):
                c.mm(ps[0:NS, :], hTg[:, kc, 0:NS], W[:, kc, half * 512:(half + 1) * 512], kc == 0, kc == KC - 1,
                     ["hTg", "WX"], [kps])
            _copy(c, "act", ob[0:NS, half * 512:(half + 1) * 512], ps[0:NS, :], [kps], [kob])
        jk, kjk = c.nxt("xn")
        _act(c, jk[0:NS, 0:1024], ob[0:NS, 0:1024], AF.Square, [kob], [kjk, "mst"], accum_out=mst[0:NS, 0:1])
        _act(c, mst[0:NS, 1:2], mst[0:NS, 0:1], AF.Sqrt, ["mst", "epsc"], ["mst"], scale=1.0 / 1024, bias=self.epsc[0:NS, :])
        _recip(c, mst[0:NS, 2:3], mst[0:NS, 1:2], ["mst"], ["mst"])
        c.op("dve", lambda e, ob=ob: e.scalar_tensor_tensor(out=ob[0:NS, 0:1024], in0=ob[0:NS, 0:1024], scalar=mst[0:NS, 2:3],
                                                     in1=gsgu[0:NS, :], op0=ALU.mult, op1=ALU.mult),
             [kob, "mst", "gsgu"], [kob])
        c.dma(cvo[:, :], ob[0:NS, 0:1024], [kob], [], kob)
        for t in range(2):
            self.norm_block(memx[t * 128:(t + 1) * 128, :], 128, [(hTg[:, :, t * 128:(t + 1) * 128], "hTg")],
                            gT=gmemT, gkey="gmemT")
        for half in range(2):
            loadW(wm, half * 1024)
            for t in range(2):
                ob, kob = c.nxt("xblk")
                for ct in range(2):
                    ps, kps = c.nxt("pA")
                    for kc in range(KC):
                        c.mm(ps[:, :], hTg[:, kc, t * 128:(t + 1) * 128], W[:, kc, ct * 512:(ct + 1) * 512], kc == 0,
                             kc == KC - 1, ["hTg", "WX"], [kps])
                    _copy(c, "act", ob[:, ct * 512:(ct + 1) * 512], ps[:, :], [kps], [kob])
                if half == 0:
                    jk, kjk = c.nxt("xn")
                    for h in range(4):
                        _act(c, jk[:, h * 256:(h + 1) * 256], ob[:, h * 256:(h + 1) * 256], AF.Square, [kob], [kjk, "mst"],
                             accum_out=mst[:, h:h + 1])
                    _act(c, mst[:, 4:8], mst[:, 0:4], AF.Sqrt, ["mst", "epsc"], ["mst"], scale=1.0 / 256, bias=self.epsc[:, :])
                    _recip(c, mst[:, 8:12], mst[:, 4:8], ["mst"], ["mst"])
                    for h in range(4):
                        self._stt(ob[:, h * 256:(h + 1) * 256], mst[:, 8 + h:9 + h], gmk[:, :], [kob, "mst", "gmk"], [kob])
                c.dma((mko if half == 0 else mvo)[t * 128:(t + 1) * 128, :], ob[:, 0:1024], [kob], [], kob)
                self.mem_pack(ob, kob, half, t, self.mkT, self.mvb, "memp")

    def mem_pack(self, ob, kob, half, t, mkT, mvb, key):
        c = self.c
        if half == 1:
            _copy(c, "act", mvb[:, t, :], ob[:, 0:1024], [kob], [key])
            return
        jb, kjb = c.nxt("xn")
        _copy(c, "act", jb[:, 0:1024], ob[:, 0:1024], [kob], [kjb])
        pt, kpt = c.nxt("pT")
        for f in range(8):
            c.tr(pt[:, f, :], jb[:, f * 128:(f + 1) * 128], self.identb[:, :], [kjb, "identb"], [kpt])
        _copy(c, "act", mkT[:, :, t * 128:(t + 1) * 128], pt[:, :, :], [kpt], [key])

    def _stt(self, io, sc, in1, reads, writes):
        self.c.op("dve", lambda e: e.scalar_tensor_tensor(out=io, in0=io, scalar=sc, in1=in1, op0=ALU.mult, op1=ALU.mult),
                  reads, writes)


class ExitStackLike:
    def __init__(self, k):
        import contextlib
        self.st = contextlib.ExitStack()

    def __enter__(self):
        return self.st.__enter__()

    def __exit__(self, *a):
        return self.st.__exit__(*a)


def _barrier(P):
    last = {}
    for op in P.ops:
        last[("e", op.eng) if op.dma is None else ("d", op.dma)] = op
    deps = list(last.values())
    for e in Prog.ENGS:
        op = P.add(e, lambda eng: eng.nop(), [], [])
        for d in deps:
            if d is not op and d not in op.deps:
                op.deps.append(d)
                d.need_inc = True


def build(phases):
    nc = bass.Bass("TRN2", target_bir_lowering=False)
    stack = contextlib.ExitStack()
    k = K(nc, stack)
    c = k.c
    k.consts()
    k.QT = c.sb("QT", [128, 8, 8, 128], BF16)
    k.QTs = c.sb("QTs", [128, 8, NS], BF16)
    k.qiTs = c.sb("qiTs", [128, 8, NS], BF16)
    k.wvs = c.sb("wvs", [64, 8], F32)
    k.KTs = c.sb("KTs", [128, 4, NS], BF16)
    k.VEs = c.sb("VEs", [128, 4, 130], BF16)
    k.kiTs = c.sb("kiTs", [128, NS], BF16)
    k.mkT = c.sb("mkT", [128, 8, 256], BF16)
    k.mvb = c.sb("mvb", [128, 2, 1024], BF16)
    with _scope(k):
        k.KT = c.sb("KT", [128, 4, 4096], BF16)
        k.VE = c.sb("VE", [128, NB, 4, 130], BF16)
        k.kiT = c.sb("kiT", [128, 4096], BF16)
        k.phase1()
        _barrier(c.P)
        k.qiT = c.sb("qiT", [128, 8, TOWN], BF16)
        k.wv = c.sb("wv", [128, 8, 2, 8], F32)
        import os
        DP = int(os.environ.get("DBG_P", "9"))
        if DP >= 1:
            phase0(k)
            _barrier(c.P)
        if DP >= 2:
            phase2(k)
            _barrier(c.P)
    k.mkTs = c.sb("mkTs", [128, 8, 256], BF16)
    k.mvbs = c.sb("mvbs", [128, 2, 1024], BF16)
    if DP >= 3:
        if os.environ.get("DBG_NOS", "0") != "1":
            phase2s(k)
            _barrier(c.P)
        phase3(k)
    else:
        oad = k.dout("oa_dbg", [128, 8 * 8 * 128], BF16)
        c.dma(oad[:, :], k.QT[:, :, :, :].rearrange("p a b q -> p (a b q)"), [("QT", j) for j in range(8)], [], "dbg0")
    c.P.emit(nc, stack)
    stack.close()
    return nc


def _rope_tables(pos):
    pos = pos.astype(np.float32)
    fa = (np.float32(10000.0) ** (-np.arange(64, dtype=np.float32) / np.float32(64))).astype(np.float32)
    fi = (np.float32(10000.0) ** (-np.arange(32, dtype=np.float32) / np.float32(32))).astype(np.float32)
    da = np.arange(128) % 64
    di = np.arange(128) % 32
    anga = (pos[None, :] * fa[da][:, None]).astype(np.float32).astype(np.float64)
    angi = (pos[None, :] * fi[di][:, None]).astype(np.float32).astype(np.float64)
    return np.stack([np.cos(anga), np.sin(anga), np.cos(angi), np.sin(angi)], axis=1).astype(np.float32)


def _consts():
    cf = np.zeros((128, 4, 128), np.float32)
    cf[:, 0, :] = np.eye(128)
    cf[:, 1, :] = 1.0
    for d in range(64):
        cf[d + 64, 2, d] = -1.0
        cf[d, 2, d + 64] = 1.0
    for base in (0, 64):
        for l in range(32):
            cf[base + l + 32, 3, base + l] = -1.0
            cf[base + l, 3, base + l + 32] = 1.0
    return cf


def _block_order(cc):
    order = []
    for g in range(NG):
        order.append(4 * g + cc)
        order.extend(4 * g + r for r in range(4) if r != cc)
    return order


_NC_CACHE = {}


def kernel(**inp):
    x_prompt = np.asarray(inp["x_prompt"], np.float32)
    x_sample = np.asarray(inp["x_sample"], np.float32)
    if "nc" not in _NC_CACHE:
        _NC_CACHE["nc"] = build(None)
    nc = _NC_CACHE["nc"]
    cf = _consts()
    w_in = np.ascontiguousarray(inp["w_in"], np.float32)
    gpreT = np.ascontiguousarray(np.asarray(inp["g_pre"], np.float32).reshape(16, 128).T)
    gq = np.ascontiguousarray(np.asarray(inp["g_q"], np.float32).reshape(128, 1))
    gk = np.ascontiguousarray(np.asarray(inp["g_k"], np.float32).reshape(128, 1))
    gsgu_bc = np.ascontiguousarray(np.broadcast_to(np.asarray(inp["g_sgu"], np.float32)[None, :], (128, 1024)))
    gmk_bc = np.ascontiguousarray(np.broadcast_to(np.asarray(inp["g_mk"], np.float32)[None, :], (128, 256)))
    gmemT = np.ascontiguousarray(np.asarray(inp["g_mem"], np.float32).reshape(16, 128).T)
    w_mem_kv = np.ascontiguousarray(inp["w_mem_kv"], np.float32)
    mem_prompt = np.asarray(inp["mem_prompt"], np.float32)
    wsT_in = np.ascontiguousarray(np.transpose(np.asarray(inp["w_s"], np.float32), (2, 0, 1)))
    trimask = np.ascontiguousarray((np.arange(128)[:, None] <= np.arange(128)[None, :]).astype(np.float32))
    brow = np.ascontiguousarray(np.asarray(inp["b_s"], np.float32).reshape(1, 1024))
    gmq = np.asarray(inp["g_mq"], np.float32)
    gmq0 = np.ascontiguousarray(gmq[0:128].reshape(128, 1))
    gmq1 = np.ascontiguousarray(gmq[128:256].reshape(128, 1))
    w_pa = np.ascontiguousarray(inp["w_pa"], np.float32)
    w_pb = np.ascontiguousarray(inp["w_pb"], np.float32)
    w_pc = np.ascontiguousarray(inp["w_pc"], np.float32)
    w_out = np.ascontiguousarray(inp["w_out"], np.float32)
    cache_mem_k = np.asarray(inp["cache_mem_k"], np.float32)
    cache_mem_v = np.asarray(inp["cache_mem_v"], np.float32)
    ckf = np.ascontiguousarray(np.asarray(inp["cache_k"], np.float32).reshape(1280 * 128, 512))
    cvf = np.ascontiguousarray(np.asarray(inp["cache_v"], np.float32).reshape(1280 * 128, 512))
    ckif = np.ascontiguousarray(np.asarray(inp["cache_kidx"], np.float32).reshape(1280 * 128, 64))
    page_table = np.asarray(inp["page_table"], np.int32)
    pidx = np.arange(128, dtype=np.float32).reshape(128, 1)
    maskbs = np.where(np.arange(NS)[None, :] <= np.arange(NS)[:, None], 0.0, NEG).astype(np.float32)
    sels = np.zeros((64, NS), np.float32)
    for jj_ in range(2):
        for q_ in range(NS):
            sels[jj_ * 32 + q_, q_] = 1.0
    ckt = np.ascontiguousarray(np.broadcast_to((0.5 ** (np.arange(32) + 1)).astype(np.float32)[None, :], (128, 32)))
    in_maps = []
    orders = []
    for core in range(8):
        b, cc = core // 4, core % 4
        order = _block_order(cc)
        orders.append(order)
        xall = np.ascontiguousarray(x_prompt[b].reshape(NB, 128, D)[order].reshape(4096, D))
        pos = (np.asarray(order)[:, None] * 128 + np.arange(128)[None, :]).reshape(NG, 512)
        rope = np.zeros((9, 128, 4, 512), np.float32)
        for g in range(NG):
            rope[g] = _rope_tables(pos[g])
        rope[8, :, :, 0:NS] = _rope_tables(16384 + np.arange(NS))
        rope_own = np.zeros((3, 128, 4, 512), np.float32)
        own_pos = ((4 * np.arange(8) + cc)[:, None] * 128 + np.arange(128)[None, :]).reshape(2, 512)
        rope_own[0] = _rope_tables(own_pos[0])
        rope_own[1] = _rope_tables(own_pos[1])
        rope_own[2, :, :, 0:NS] = _rope_tables(16384 + np.arange(NS))
        maskb = np.zeros((128, 512), np.float32)
        tq = np.arange(128)
        maskb[:, 0:128] = np.where(tq[None, :] <= tq[:, None], 0.0, NEG)
        for sl_ in range(1, 4):
            maskb[:, sl_ * 128:(sl_ + 1) * 128] = 0.0 if sl_ <= cc else NEG
        sel = np.zeros((128, 2, 128), np.float32)
        for jj_ in range(2):
            for q_ in range(64):
                for qh_ in range(2):
                    sel[jj_ * 64 + q_, qh_, qh_ * 64 + q_] = 1.0
        in_maps.append({"ckt": ckt, "ck": ckf, "cv": cvf, "cki": ckif, "pidx": pidx, "maskbs": maskbs, "sels": sels,
                        "ptab_bc": np.ascontiguousarray(np.broadcast_to(page_table[core][None, :], (128, NPG))),
                        "wsT_in": wsT_in, "trimask": trimask, "brow": brow, "gmq0": gmq0, "gmq1": gmq1, "w_pa": w_pa,
                        "w_pb": w_pb, "w_pc": w_pc, "w_out": w_out,
                        "cmk": np.ascontiguousarray(cache_mem_k[core].reshape(256, 1024)),
                        "cmv": np.ascontiguousarray(cache_mem_v[core].reshape(256, 1024)),
                        "rope_own": rope_own, "maskb": maskb, "sel": sel, "cf32": cf, "gpreT": gpreT, "gq": gq, "gk": gk, "xall": xall, "xs": np.ascontiguousarray(x_sample[core]),
                        "w_in": w_in, "rope": rope, "gsgu_bc": gsgu_bc, "gmk_bc": gmk_bc, "gmemT": gmemT,
                        "w_mem_kv": w_mem_kv, "memx": np.ascontiguousarray(mem_prompt[b])})
    res = run_bass_kernel_spmd(nc, in_maps, core_ids=list(range(8)))
    R = res.results
    k_p = np.zeros((2, 4096, 4, 128), np.float32)
    v_p = np.zeros((2, 4096, 4, 128), np.float32)
    ki_p = np.zeros((2, 4096, 64), np.float32)
    for core in range(8):
        b, cc = core // 4, core % 4
        for g in range(NG):
            blk = 4 * g + cc
            k_p[b, blk * 128:(blk + 1) * 128] = R[core]["k_out"][g * 128:(g + 1) * 128]
            v_p[b, blk * 128:(blk + 1) * 128] = R[core]["v_out"][g * 128:(g + 1) * 128]
            ki_p[b, blk * 128:(blk + 1) * 128] = R[core]["ki_out"][g * 128:(g + 1) * 128]
    k_s = np.stack([R[i]["ks_out"] for i in range(8)])
    v_s = np.stack([R[i]["vs_out"] for i in range(8)])
    ki_s = np.stack([R[i]["kis_out"] for i in range(8)])
    cv_s = np.stack([R[i]["cvs_out"].reshape(NS, 8, 128) for i in range(8)])
    mk_p = np.stack([R[4 * b]["mk_out"].reshape(256, 4, 256) for b in range(2)])
    mv_p = np.stack([R[4 * b]["mv_out"].reshape(256, 4, 256) for b in range(2)])
    _NC_CACHE["dbg"] = R
    y_p = np.zeros((2, 4096, D), np.float32)
    y_s = np.zeros((8, NS, D), np.float32)
    if "y_out" in R[0]:
        for core in range(8):
            b, cc = core // 4, core % 4
            for g in range(NG):
                blk = 4 * g + cc
                y_p[b, blk * 128:(blk + 1) * 128] = R[core]["y_out"][g * 128:(g + 1) * 128]
            y_s[core] = R[core]["ys_out"]
    return (y_p, y_s, k_p, v_p, ki_p, mk_p, mv_p, k_s, v_s, ki_s, cv_s)


import contextlib


@contextlib.contextmanager
def _scope(k):
    st = contextlib.ExitStack()
    old = k.c.stack
    k.c.stack = st
    try:
        with st:
            yield st
    finally:
        k.c.stack = old


def _loadW(k, pool, src_view, col0, ncols, nkc=16):
    c = k.c
    Wt, kW = c.nxt(pool)
    keys = []
    for kc4 in range(nkc // 4):
        sl = slice(kc4 * 4, kc4 * 4 + 4)
        key = (kW, kc4)
        c.dma(Wt[:, sl, 0:ncols], src_view[:, sl, col0:col0 + ncols], [], [key], key, q="pool")
        keys.append(key)
    return Wt, keys


def _hT_own(k, hT, hTs):
    xall, xs = k.dr["xall"], k.dr["xs"]
    for j in range(8):
        k.norm_block(xall[(4 * j) * 128:(4 * j + 1) * 128, :], 128, [(hT[:, :, j * 128:(j + 1) * 128], ("hTo", j))])
    k.norm_block(xs[:, :], NS, [(hTs[:, :, :], "hTs")])


def _chunks(hT, hTs):
    return [(hT[:, :, 0:512], 512, [("hTo", j) for j in range(4)]),
            (hT[:, :, 512:1024], 512, [("hTo", j) for j in range(4, 8)]),
            (hTs[:, :, :], NS, ["hTs"])]


def phase0(k):
    c = k.c
    w_in = k.dr["w_in"].rearrange("(kc p) n -> p kc n", p=128)
    ropeo = k.din("rope_own", [3, 128, 4, 512])
    with _scope(k):
        hT = c.sb("hT", [128, KC, TOWN], BF16)
        hTs = c.sb("hTs", [128, KC, NS], BF16)
        c.pool("ss", 4, [128, 4], F32)
        c.pool("pT", 2, [128, 8, 128], BF16, psum=True)
        with _scope(k):
            c.pool("xblk", 2, [128, D], F32)
            c.pool("xn", 1, [128, D], BF16)
            _hT_own(k, hT, hTs)
        _barrier(c.P)
        c.pool("Wt", 1, [128, KC, 512], BF16)
        c.pool("ropeT", 2, [128, 4, 512], F32)
        for n in ("f_kf", "f_sq", "f_rs", "f_kn", "f_t1", "f_t2", "f_kr"):
            c.pool(n, 1, [128, 512], F32)
        Wwi = c.sb("Wwi", [128, KC, 16], BF16)
        c.pool("hdup", 1, [128, KC, 128], BF16)
        c.pool("pA", 3, [128, 512], F32, psum=True)
        c.pool("pB", 3, [128, 512], F32, psum=True)
        chunks = _chunks(hT, hTs)
        pipe = Pipe3()
        for wt in range(4):
            isq = wt < 2
            Wt, wkeys = _loadW(k, "Wt", w_in, (C_AQ if isq else C_AQI) + (wt % 2) * 512, 512)
            for ch, (hv, nt, hkeys) in enumerate(chunks):
                rt, krt = c.nxt("ropeT")
                c.dma(rt[:], ropeo[ch], [], [krt], krt)
                for hh in range(4):
                    head = (wt % 2) * 4 + hh

                    def fa(hh=hh, Wt=Wt, wkeys=wkeys, hv=hv, nt=nt, hkeys=hkeys, rt=rt, krt=krt, isq=isq):
                        ps, kps = c.nxt("pA")
                        for kc in range(KC):
                            c.mm(ps[:, 0:nt], Wt[:, kc, hh * 128:(hh + 1) * 128], hv[:, kc, :], kc == 0, kc == KC - 1,
                                 hkeys + wkeys, [kps])
                        if isq:
                            return k.rope_state(ps[:, 0:nt], kps, nt, k.gq[:, 0:1], "gq", k.R128, rt[:, 0, 0:nt],
                                                rt[:, 1, 0:nt], krt, 1.0 / 128)
                        return k.rope_state(ps[:, 0:nt], kps, nt, None, None, k.R64, rt[:, 2, 0:nt], rt[:, 3, 0:nt],
                                            krt, 0.0, normalize=False)

                    def fc(S, head=head, ch=ch, isq=isq):
                        kr, kkr = k.rope_C(S)
                        if ch < 2:
                            if isq:
                                _copy(c, "act", k.QT[:, 4 * ch:4 * ch + 4, head, :], kr[:, 0:512].rearrange("p (j q) -> p j q", j=4),
                                      [kkr], [("QT", 4 * ch + i) for i in range(4)])
                            else:
                                _copy(c, "act", k.qiT[:, head, ch * 512:(ch + 1) * 512], kr[:, 0:512], [kkr], [("qiT", ch)])
                        else:
                            if isq:
                                _copy(c, "act", k.QTs[:, head, :], kr[:, 0:NS], [kkr], ["QTs"])
                            else:
                                _copy(c, "act", k.qiTs[:, head, :], kr[:, 0:NS], [kkr], ["qiTs"])
                    pipe.push(fa, k.rope_B, fc)
        pipe.flush()
        for kc4 in range(4):
            sl = slice(kc4 * 4, kc4 * 4 + 4)
            c.dma(Wwi[:, sl, :], w_in[:, sl, C_AWI:C_AWI + 16], [], [("Wwi", kc4)], ("Wwi", kc4), q="pool")
        wwk = [("Wwi", i) for i in range(4)]
        for j in range(8):
            for qh in range(2):
                hd, khd = c.nxt("hdup")
                src = hT[:, :, j * 128 + qh * 64:j * 128 + qh * 64 + 64]
                _copy(c, "pool", hd[:, :, 0:64], src, [("hTo", j)], [(khd, 0)])
                _copy(c, "pool", hd[:, :, 64:128], src, [("hTo", j)], [(khd, 1)])
                ps, kps = c.nxt("pB")
                for kc in range(KC):
                    c.mm(ps[:, 0:16], hd[:, kc, :], Wwi[:, kc, :], kc == 0, kc == KC - 1, [(khd, 0), (khd, 1)] + wwk, [kps])
                pv = ps[:, 0:16].rearrange("p (c two) -> p c two", two=2)
                _act(c, k.wv[0:64, j, qh, :], pv[0:64, :, 0], AF.Copy, [kps], [("wv", j, qh, 0)], scale=1.0 / 32)
                _act(c, k.wv[64:128, j, qh, :], pv[64:128, :, 1], AF.Copy, [kps], [("wv", j, qh, 1)], scale=1.0 / 32)
        hd, khd = c.nxt("hdup")
        c.op("pool", lambda e, hd=hd: e.memset(hd[:, :, :], 0.0), [], [(khd, 0), (khd, 1)])
        c.op("pool", lambda e: e.memset(k.wvs[:, :], 0.0), [], ["wvs"])
        _copy(c, "pool", hd[:, :, 0:NS], hTs[:, :, :], ["hTs"], [(khd, 0)])
        _copy(c, "pool", hd[:, :, 32:32 + NS], hTs[:, :, :], ["hTs"], [(khd, 1)])
        ps, kps = c.nxt("pB")
        for kc in range(KC):
            c.mm(ps[0:64, 0:16], hd[:, kc, 0:64], Wwi[:, kc, :], kc == 0, kc == KC - 1, [(khd, 0), (khd, 1)] + wwk, [kps])
        pv = ps[:, 0:16].rearrange("p (c two) -> p c two", two=2)
        _act(c, k.wvs[0:NS, :], pv[0:NS, :, 0], AF.Copy, [kps, "wvs"], ["wvs"], scale=1.0 / 32)
        _act(c, k.wvs[32:32 + NS, :], pv[32:32 + NS, :, 1], AF.Copy, [kps, "wvs"], ["wvs"], scale=1.0 / 32)


NIT = 18


def _threshold(k, sc, nk, st, junk, hwt, ksc="sc", kjunk="junk"):
    c = k.c
    _tt(c, "dve", st[:, 2:3], st[:, 1:2], st[:, 0:1], ALU.subtract, ["thr"], ["thr"])
    _ts(c, "dve", hwt[:, 0:NIT], k.ckt[0:hwt.shape[0], 0:NIT], st[:, 2:3], None, ALU.mult, ALU.bypass, ["thr", "ckt"], ["thr"])
    for it in range(NIT):
        _tt(c, "dve", st[:, 3:4], st[:, 0:1], hwt[:, it:it + 1], ALU.add, ["thr"], ["thr"])
        _ts(c, "dve", junk, sc, st[:, 3:4], None, ALU.is_ge, ALU.add, ["thr", ksc], [kjunk, "thr"], accum_out=st[:, 4:5])
        _ts(c, "dve", st[:, 5:6], st[:, 4:5], 255.5, hwt[:, it:it + 1], ALU.is_ge, ALU.mult, ["thr"], ["thr"])
        _tt(c, "dve", st[:, 0:1], st[:, 0:1], st[:, 5:6], ALU.add, ["thr"], ["thr"])


def _att_s1(k, nq, KTg, ktkeys, nkeys, qrhs, qkeys, MTap, mtkey, gp, mul_eng):
    c = k.c
    w2 = 2 * nq
    pst, kpst = c.nxt("pD")
    for gi in range(2):
        g = 2 * gp + gi
        c.mm(pst[0:nkeys, gi * w2:(gi + 1) * w2], KTg[g], qrhs[g], True, True, ktkeys + qkeys, [kpst])
    ex, kex = c.nxt("ex")
    _act(c, ex[0:nkeys, 0:2 * w2], pst[0:nkeys, 0:2 * w2], AF.Exp, [kpst], [kex], scale=float(128 ** -0.5))
    pm, kpm = c.nxt("pm")
    _tt(c, mul_eng, pm[0:nkeys, :, 0:nq], ex[0:nkeys, 0:2 * w2].rearrange("p (a q) -> p a q", a=4),
        MTap.unsqueeze(1).to_broadcast([nkeys, 4, nq]), ALU.mult, [kex, mtkey], [kpm])
    return pm, kpm


def _att_s2(k, nq, pm, kpm, VEg, vekeys, nkeys, gp, po, kpo, first, last):
    c = k.c
    for a in range(4):
        hd = 4 * gp + a
        g = hd // 2
        b3, off = hd // 3, (hd % 3) * 130
        c.mm(po[b3][0:nq, off:off + 130], pm[0:nkeys, a, 0:nq], VEg[g], first and hd % 3 == 0, last, [kpm] + vekeys, [kpo[b3]])


def _att_pipeline(k, steps):
    import os
    n = len(steps)
    sk = int(os.environ.get("PIPE_ATT", "2"))
    st = {}
    for i in range(min(sk, n)):
        st[i] = steps[i][0]()
    for i in range(n):
        if sk == 0:
            st[i] = steps[i][0]()
        steps[i][1](st.pop(i))
        if sk > 0 and i + sk < n:
            st[i + sk] = steps[i + sk][0]()


def _attn_finish(k, nq, po, kpo, oat, rden, dst, dkeys):
    c = k.c
    for hd in range(8):
        b3, off = hd // 3, (hd % 3) * 130
        _recip(c, rden[0:nq, hd:hd + 1], po[b3][0:nq, off + 128:off + 129], [kpo[b3]], ["rden"])
    for hd in range(8):
        b3, off = hd // 3, (hd % 3) * 130
        _act(c, oat[0:nq, hd, :], po[b3][0:nq, off:off + 128], AF.Copy, [kpo[b3], "rden"], [("oat", hd)],
             scale=rden[0:nq, hd:hd + 1])
    pt, kpt = c.nxt("pT")
    for hd in range(8):
        c.tr(pt[:, hd, 0:nq], oat[0:nq, hd, :], k.identb[0:nq, 0:nq], [("oat", hd), "identb"], [kpt])
    _copy(c, "act", dst, pt[:, :, 0:nq], [kpt], dkeys)


def phase2_old(k):
    c = k.c
    with _scope(k):
        sc = c.sb("sc", [128, 4096], F32)
        junk = c.sb("junk", [128, 4096], BF16)
        M = c.sb("M", [128, 4096], BF16)
        MT = c.sb("MT", [128, NB, 128], BF16)
        st = c.sb("thr", [128, 16], F32)
        hwt = c.sb("hwt", [128, 32], F32)
        oat = c.sb("oat", [128, 8, 128], BF16)
        rden = c.sb("rden", [128, 8], F32)
        maskb = c.sb("maskb", [128, 512], F32)
        sel = c.sb("sel", [128, 2, 128], BF16)
        c.dma(maskb[:], k.din("maskb", [128, 512])[:, :], [], ["maskb"], "c20")
        c.dma(sel[:], k.din("sel", [128, 2, 128])[:, :, :], [], ["sel"], "c21", q="pool")
        c.pool("BD", 2, [128, 2, 8, 128], BF16)
        c.pool("Wsel", 2, [128, 2, 8, 128], BF16)
        c.pool("rl", 4, [128, 512], BF16)
        c.pool("ex", 3, [128, 512], BF16)
        c.pool("pm", 3, [128, 4, 128], BF16)
        c.pool("pD", 3, [128, 512], F32, psum=True)
        c.pool("pSC", 1, [128, 512], F32, psum=True)
        c.pool("pT", 1, [128, 8, 128], BF16, psum=True)
        c.pool("po", 3, [128, 512], F32, psum=True)
        for i in range(2):
            bd = c.rot["BD"][0][i]
            c.op("pool", (lambda b: (lambda e: e.memset(b[:, :, :, :], 0.0)))(bd), [],
                 [(("BD", i), cc, jj) for cc in range(8) for jj in range(2)])
        import os
        NJ = int(os.environ.get("DBG_NJ", "8"))
        L2 = int(os.environ.get("DBG_L2", "9"))
        for j in range(NJ):
            nk = (j + 1) * 512
            nkb = 4 * (j + 1)
            BD, kBD = c.nxt("BD")
            Ws, kWs = c.nxt("Wsel")
            for cc in range(8):
                for jj in range(2):
                    rows = slice(jj * 64, (jj + 1) * 64)
                    _copy(c, "pool", BD[rows, :, cc, jj * 64:(jj + 1) * 64],
                          k.qiT[rows, cc, j * 128:(j + 1) * 128].rearrange("p (qh q) -> p qh q", qh=2),
                          [("qiT", j // 4)], [(kBD, cc, jj)])
            for qh in range(2):
                for cc in range(8):
                    _ts(c, "dve", Ws[:, qh, cc, :], sel[:, qh, :], k.wv[:, j, qh, cc:cc + 1], None, ALU.mult, ALU.bypass,
                        ["sel", ("wv", j, qh, 0), ("wv", j, qh, 1)], [(kWs, qh, cc)])
            if L2 < 2:
                continue
            for k5 in range(j + 1):
                psc, kpsc = c.nxt("pSC")
                combos = [(qh, cc) for qh in range(2) for cc in range(8)]
                rls = {}

                def stage0(i, k5=k5):
                    qh, cc = combos[i]
                    pd, kpd = c.nxt("pD")
                    c.mm(pd[:, :], BD[:, qh, cc, :], k.kiT[:, k5 * 512:(k5 + 1) * 512], True, True,
                         [(kBD, cc, 0), (kBD, cc, 1), ("kiT", k5)], [kpd])
                    rl, krl = c.nxt("rl")
                    if i % 2 == 0:
                        _act(c, rl[:, :], pd[:, :], AF.Relu, [kpd], [krl])
                    else:
                        _ts(c, "dve", rl[:, :], pd[:, :], 0.0, None, ALU.max, ALU.bypass, [kpd], [krl])
                    rls[i] = (rl, krl)

                ski = int(os.environ.get("PIPE_IDX", "2"))
                for i in range(ski):
                    stage0(i)
                for i in range(16):
                    if ski == 0:
                        stage0(i)
                    qh, cc = combos[i]
                    rl, krl = rls.pop(i)
                    c.mm(psc[:, :], Ws[:, qh, cc, :], rl[:, :], i == 0, i == 15, [krl, (kWs, qh, cc)], [kpsc])
                    if ski > 0 and i + ski < 16:
                        stage0(i + ski)
                if k5 == j:
                    c.op("dve", lambda e, psc=psc: e.tensor_reduce(out=st[:, 8:9], in_=psc[:, :], axis=AX.X, op=ALU.min),
                         [kpsc], ["thr"])
                    _tt(c, "dve", sc[:, k5 * 512:(k5 + 1) * 512], psc[:, :], maskb[:, :], ALU.add, [kpsc, "maskb"], ["sc"])
                else:
                    _copy(c, "act", sc[:, k5 * 512:(k5 + 1) * 512], psc[:, :], [kpsc], ["sc"])
            if L2 < 3:
                continue
            if j > 0:
                c.op("dve", lambda e, j=j: e.tensor_reduce(out=st[:, 7:8], in_=sc[:, 0:j * 512], axis=AX.X, op=ALU.min),
                     ["sc"], ["thr"])
                _tt(c, "dve", st[:, 0:1], st[:, 7:8], st[:, 8:9], ALU.min, ["thr"], ["thr"])
            else:
                _copy(c, "dve", st[:, 0:1], st[:, 8:9], ["thr"], ["thr"])
            c.op("dve", lambda e, nk=nk: e.tensor_reduce(out=st[:, 1:2], in_=sc[:, 0:nk], axis=AX.X, op=ALU.max), ["sc"], ["thr"])
            _threshold(k, sc[:, 0:nk], nk, st, junk[:, 0:nk], hwt)
            if L2 < 4:
                continue
            _ts(c, "dve", M[:, 0:nk], sc[:, 0:nk], st[:, 0:1], None, ALU.is_ge, ALU.bypass, ["thr", "sc"], ["M"])
            for kb8 in range(0, nkb, 8):
                n8 = min(8, nkb - kb8)
                pt, kpt = c.nxt("pT")
                for i in range(n8):
                    c.tr(pt[:, i, :], M[:, (kb8 + i) * 128:(kb8 + i + 1) * 128], k.identb[:, :], ["M", "identb"], [kpt])
                _copy(c, "act", MT[:, kb8:kb8 + n8, :], pt[:, 0:n8, :], [kpt], ["MT"])
            if L2 < 5:
                continue
            po, kpo = [], []
            for i in range(3):
                a, b = c.nxt("po")
                po.append(a)
                kpo.append(b)
            qrhs = [k.QT[:, j, 2 * g:2 * g + 2, :].rearrange("p h q -> p (h q)") for g in range(4)]
            steps = []
            for kb in range(nkb):
                for gp in range(2):
                    def s1(kb=kb, gp=gp):
                        KTg = [k.KT[:, g, kb * 128:(kb + 1) * 128] for g in range(4)]
                        return _att_s1(k, 128, KTg, [("KT", g, kb // 4) for g in range(4)], 128, qrhs, [("QT", j)],
                                       MT[:, kb, :], "MT", gp, "dve" if gp == 0 else "pool")

                    def s2(state, kb=kb, gp=gp):
                        VEg = [k.VE[:, kb, g, :] for g in range(4)]
                        _att_s2(k, 128, state[0], state[1], VEg, [("VE", kb), "VE1"], 128, gp, po, kpo, kb == 0, kb == nkb - 1)
                    steps.append((s1, s2))
            _att_pipeline(k, steps)
            if L2 < 6:
                continue
            _attn_finish(k, 128, po, kpo, oat, rden, k.QT[:, j, :, :], [("QT", j)])
            if os.environ.get("DBG_BAR", "0") == "1":
                _barrier(c.P)
        if NJ == 1:
            d1 = k.dout("dbg_sc", [128, 512])
            d2 = k.dout("dbg_st", [128, 16])
            d3 = k.dout("dbg_M", [128, 512], BF16)
            d4 = k.dout("dbg_wv", [128, 128])
            d5 = k.dout("dbg_qi", [128, 8, 128], BF16)
            d6 = k.dout("dbg_MT", [128, 4, 128], BF16)
            c.dma(d1[:, :], sc[:, 0:512], ["sc"], [], "dbg1")
            c.dma(d2[:, :], st[:, :], ["thr"], [], "dbg2")
            c.dma(d3[:, :], M[:, 0:512], ["M"], [], "dbg3")
            c.dma(d4[:, :], k.wv[:, :, :, :].rearrange("p a b c -> p (a b c)"), [], [], "dbg4")
            c.dma(d5[:, :, :], k.qiT[:, :, 0:128], [], [], "dbg5")
            c.dma(d6[:, :, :], MT[:, 0:4, :], ["MT"], [], "dbg6")


def phase2(k):
    c = k.c
    with _scope(k):
        scb = [c.sb("sc0", [128, 4096], F32), c.sb("sc1", [128, 4096], F32)]
        M = c.sb("M", [128, 4096], BF16)
        MT = c.sb("MT", [128, NB, 128], BF16)
        st = c.sb("thr", [128, 16], F32)
        hwt = c.sb("hwt", [128, 32], F32)
        oat = c.sb("oat", [128, 8, 128], BF16)
        rden = c.sb("rden", [128, 8], F32)
        maskb = c.sb("maskb", [128, 512], F32)
        sel = c.sb("sel", [128, 2, 128], BF16)
        c.dma(maskb[:], k.din("maskb", [128, 512])[:, :], [], ["maskb"], "c20")
        c.dma(sel[:], k.din("sel", [128, 2, 128])[:, :, :], [], ["sel"], "c21", q="pool")
        c.pool("BD", 2, [128, 2, 8, 128], BF16)
        c.pool("Wsel", 2, [128, 2, 8, 128], BF16)
        c.pool("rl", 4, [128, 512], BF16)
        c.pool("ex", 3, [128, 512], BF16)
        c.pool("pm", 3, [128, 4, 128], BF16)
        c.pool("pD", 3, [128, 512], F32, psum=True)
        c.pool("pSC", 1, [128, 512], F32, psum=True)
        c.pool("pT", 1, [128, 8, 128], BF16, psum=True)
        c.pool("po", 3, [128, 512], F32, psum=True)
        for i in range(2):
            bd = c.rot["BD"][0][i]
            c.op("pool", (lambda b: (lambda e: e.memset(b[:, :, :, :], 0.0)))(bd), [],
                 [(("BD", i), cc, jj) for cc in range(8) for jj in range(2)])
        combos = [(qh, cc) for qh in range(2) for cc in range(8)]

        def idx(j):
            sc, ksc = scb[j % 2], ("sc", j % 2)
            BD, kBD = c.nxt("BD")
            Ws, kWs = c.nxt("Wsel")
            for cc in range(8):
                for jj in range(2):
                    rows = slice(jj * 64, (jj + 1) * 64)
                    _copy(c, "pool", BD[rows, :, cc, jj * 64:(jj + 1) * 64],
                          k.qiT[rows, cc, j * 128:(j + 1) * 128].rearrange("p (qh q) -> p qh q", qh=2),
                          [("qiT", j // 4)], [(kBD, cc, jj)])
            for qh in range(2):
                for cc in range(8):
                    _ts(c, "dve", Ws[:, qh, cc, :], sel[:, qh, :], k.wv[:, j, qh, cc:cc + 1], None, ALU.mult, ALU.bypass,
                        ["sel", ("wv", j, qh, 0), ("wv", j, qh, 1)], [(kWs, qh, cc)])
            for k5 in range(j + 1):
                psc, kpsc = c.nxt("pSC")
                rls = {}

                def stage0(i):
                    qh, cc = combos[i]
                    pd, kpd = c.nxt("pD")
                    c.mm(pd[:, :], BD[:, qh, cc, :], k.kiT[:, k5 * 512:(k5 + 1) * 512], True, True,
                         [(kBD, cc, 0), (kBD, cc, 1), ("kiT", k5)], [kpd])
                    rl, krl = c.nxt("rl")
                    _act(c, rl[:, :], pd[:, :], AF.Relu, [kpd], [krl])
                    rls[i] = (rl, krl)

                stage0(0)
                stage0(1)
                for i in range(16):
                    qh, cc = combos[i]
                    rl, krl = rls.pop(i)
                    c.mm(psc[:, :], Ws[:, qh, cc, :], rl[:, :], i == 0, i == 15, [krl, (kWs, qh, cc)], [kpsc])
                    if i + 2 < 16:
                        stage0(i + 2)
                _copy(c, "act", sc[:, k5 * 512:(k5 + 1) * 512], psc[:, :], [kpsc], [ksc])

        def thr_att(j):
            sc, ksc = scb[j % 2], ("sc", j % 2)
            nk = (j + 1) * 512
            nkb = 4 * (j + 1)
            c.op("dve", lambda e: e.tensor_reduce(out=st[:, 0:1], in_=sc[:, 0:nk], axis=AX.X, op=ALU.min), [ksc], ["thr"])
            _tt(c, "dve", sc[:, j * 512:(j + 1) * 512], sc[:, j * 512:(j + 1) * 512], maskb[:, :], ALU.add, [ksc, "maskb"], [ksc])
            c.op("dve", lambda e: e.tensor_reduce(out=st[:, 1:2], in_=sc[:, 0:nk], axis=AX.X, op=ALU.max), [ksc], ["thr"])
            _threshold(k, sc[:, 0:nk], nk, st, M[:, 0:nk], hwt, ksc=ksc, kjunk="M")
            _ts(c, "dve", M[:, 0:nk], sc[:, 0:nk], st[:, 0:1], None, ALU.is_ge, ALU.bypass, ["thr", ksc], ["M"])
            for kb8 in range(0, nkb, 8):
                n8 = min(8, nkb - kb8)
                pt, kpt = c.nxt("pT")
                for i in range(n8):
                    c.tr(pt[:, i, :], M[:, (kb8 + i) * 128:(kb8 + i + 1) * 128], k.identb[:, :], ["M", "identb"], [kpt])
                _copy(c, "act", MT[:, kb8:kb8 + n8, :], pt[:, 0:n8, :], [kpt], ["MT"])
            po, kpo = [], []
            for i in range(3):
                a, b = c.nxt("po")
                po.append(a)
                kpo.append(b)
            qrhs = [k.QT[:, j, 2 * g:2 * g + 2, :].rearrange("p h q -> p (h q)") for g in range(4)]
            steps = []
            for kb in range(nkb):
                for gp in range(2):
                    def s1(kb=kb, gp=gp):
                        KTg = [k.KT[:, g, kb * 128:(kb + 1) * 128] for g in range(4)]
                        return _att_s1(k, 128, KTg, [("KT", g, kb // 4) for g in range(4)], 128, qrhs, [("QT", j)],
                                       MT[:, kb, :], "MT", gp, "dve" if gp == 0 else "pool")

                    def s2(state, kb=kb, gp=gp):
                        VEg = [k.VE[:, kb, g, :] for g in range(4)]
                        _att_s2(k, 128, state[0], state[1], VEg, [("VE", kb), "VE1"], 128, gp, po, kpo, kb == 0, kb == nkb - 1)
                    steps.append((s1, s2))
            _att_pipeline(k, steps)
            _attn_finish(k, 128, po, kpo, oat, rden, k.QT[:, j, :, :], [("QT", j)])

        idx(0)
        for j in range(8):
            if j + 1 < 8:
                idx(j + 1)
            thr_att(j)


def _stt(c, out, in0, scalar, in1, op0, op1, reads, writes):
    c.op("dve", lambda e: e.scalar_tensor_tensor(out=out, in0=in0, scalar=scalar, in1=in1, op0=op0, op1=op1), reads, writes)


def _proj_fm(k, Wt, wkeys, col0, hv, nt, hkeys, pool="pA", nkc=KC):
    c = k.c
    ps, kps = c.nxt(pool)
    for kc in range(nkc):
        c.mm(ps[:, 0:nt], Wt[:, kc, col0:col0 + 128], hv[:, kc, :], kc == 0, kc == nkc - 1, hkeys + wkeys, [kps])
    return ps, kps


def _tok(ch):
    return (slice(ch * 512, (ch + 1) * 512), 512) if ch < 2 else (slice(1024, 1024 + NS), NS)


def phase3(k):
    c = k.c
    w_in = k.dr["w_in"].rearrange("(kc p) n -> p kc n", p=128)
    with _scope(k):
        hT = c.sb("hT", [128, KC, TOWN], BF16)
        hTs = c.sb("hTs", [128, KC, NS], BF16)
        AT = c.sb("AT", [128, 8, TOWN + NS], BF16)
        BT = c.sb("BT", [128, 8, TOWN + NS], BF16)
        CT = c.sb("CT", [128, 8, TOWN + NS], BF16)
        c.pool("ss", 4, [128, 4], F32)
        c.pool("pT", 2, [128, 8, 128], BF16, psum=True)
        c.pool("pA", 3, [128, 512], F32, psum=True)
        c.pool("pB", 3, [128, 512], F32, psum=True)
        c.pool("t1", 2, [128, 512], F32)
        c.pool("t2", 2, [128, 512], F32)
        with _scope(k):
            c.pool("xblk", 2, [128, D], F32)
            c.pool("xn", 1, [128, D], BF16)
            _hT_own(k, hT, hTs)
        _barrier(c.P)
        chunks = _chunks(hT, hTs)
        with _scope(k):
            c.pool("Wt", 2, [128, KC, 512], BF16)
            sT = c.sb("sT", [128, 8, TOWN + NS], BF16)
            cqT = sT
            c.pool("vn", 2, [128, 1024], BF16)
            wsf = c.sb("wsf", [128, 8, 128], F32)
            wsT = c.sb("wsT", [128, 8, 128], BF16)
            trim = c.sb("trim", [128, 128], F32)
            brow = c.sb("brow", [1, 1024], F32)
            gsgu = c.sb("gsgu3", [128, 1024], F32)
            gmqT = c.sb("gmqT", [128, 2], F32)
            onesb = c.sb("onesb", [128, 128], BF16)
            jk = c.sb("jk3", [128, 512], BF16)
            c.pool("fa", 1, [128, 512], F32)
            c.pool("fb", 1, [128, 512], F32)
            c.pool("pe", 4, [128, 512], BF16)
            c.dma(wsf[:], k.din("wsT_in", [128, 8, 128])[:, :, :], [], ["wsf"], "c30")
            c.dma(trim[:], k.din("trimask", [128, 128])[:, :], [], ["trim"], "c31")
            c.dma(brow[:], k.din("brow", [1, 1024])[:, :], [], ["brow"], "c32")
            c.dma(gsgu[:], k.dr["gsgu_bc"][:, :], [], ["gsgu3"], "c33")
            c.dma(gmqT[:, 0:1], k.din("gmq0", [128, 1])[:, :], [], ["gmq0"], "c34")
            c.dma(gmqT[:, 1:2], k.din("gmq1", [128, 1])[:, :], [], ["gmq1"], "c35")
            _copy(c, "pool", onesb[:], k.onesf[:], ["onesf"], ["onesb"])
            _tt(c, "dve", wsT[:, :, :], wsf[:, :, :], trim[:, :].unsqueeze(1).to_broadcast([128, 8, 128]), ALU.mult,
                ["wsf", "trim"], ["wsT"])
            for wt in range(2):
                Wt, wkeys = _loadW(k, "Wt", w_in, C_AG + wt * 512, 512)
                for ch, (hv, nt, hkeys) in enumerate(chunks):
                    ts_, _ = _tok(ch)
                    for hh in range(4):
                        f = wt * 4 + hh
                        ps, kps = _proj_fm(k, Wt, wkeys, hh * 128, hv, nt, hkeys)
                        t1, kt1 = c.nxt("t1")
                        _act(c, t1[:, 0:nt], ps[:, 0:nt], AF.Silu, [kps], [kt1])
                        if ch < 2:
                            _tt(c, "dve", AT[:, f, ts_].rearrange("p (j q) -> p j q", j=4),
                                t1[:, 0:512].rearrange("p (j q) -> p j q", j=4), k.QT[:, 4 * ch:4 * ch + 4, f, :], ALU.mult,
                                [kt1] + [("QT", 4 * ch + i) for i in range(4)], [("AT", f, ch)])
                        else:
                            _tt(c, "dve", AT[:, f, ts_], t1[:, 0:NS], k.QTs[:, f, :], ALU.mult, [kt1, "QTs"], [("AT", f, ch)])
            W0, wk0 = _loadW(k, "Wt", w_in, C_BV, 512)
            W1, wk1 = _loadW(k, "Wt", w_in, C_BV + 512, 512)
            for blk in range(9):
                samp = blk == 8
                nt = NS if samp else 128
                lh = hTs if samp else hT[:, :, blk * 128:(blk + 1) * 128]
                hk = ["hTs"] if samp else [("hTo", blk)]
                tsl = slice(1024, 1024 + NS) if samp else slice(blk * 128, (blk + 1) * 128)
                ss, kss = c.nxt("ss")
                pss = []
                for half, (Wh, wkh) in enumerate(((W0, wk0), (W1, wk1))):
                    ps, kps = c.nxt("pA")
                    for kc in range(KC):
                        c.mm(ps[0:nt, :], lh[:, kc, :], Wh[:, kc, :], kc == 0, kc == KC - 1, hk + wkh, [kps])
                    _act(c, jk[0:nt, :], ps[0:nt, :], AF.Square, [kps], ["jk3", (kss, half)], accum_out=ss[0:nt, half:half + 1])
                    pss.append((ps, kps))
                _tt(c, "dve", ss[0:nt, 2:3], ss[0:nt, 0:1], ss[0:nt, 1:2], ALU.add, [(kss, 0), (kss, 1)], [(kss, 2)])
                _act(c, ss[0:nt, 2:3], ss[0:nt, 2:3], AF.Sqrt, [(kss, 2), "epsc"], [(kss, 2)], scale=1.0 / 1024, bias=k.epsc[0:nt, :])
                _recip(c, ss[0:nt, 3:4], ss[0:nt, 2:3], [(kss, 2)], [(kss, 3)])
                vn, kvn = c.nxt("vn")
                for half, (ps, kps) in enumerate(pss):
                    _stt(c, vn[0:nt, half * 512:(half + 1) * 512], ps[0:nt, :], ss[0:nt, 3:4], gsgu[0:nt, half * 512:(half + 1) * 512],
                         ALU.mult, ALU.mult, [kps, (kss, 3), "gsgu3"], [(kvn, half)])
                for g4 in range(2):
                    pm_, kpm_ = c.nxt("pB")
                    for gl in range(4):
                        g = g4 * 4 + gl
                        c.mm(pm_[:, gl * 128:gl * 128 + nt], vn[0:nt, g * 128:(g + 1) * 128], wsT[0:nt, g, 0:nt], True, False,
                             [(kvn, g // 4), "wsT"], [kpm_])
                        c.mm(pm_[:, gl * 128:gl * 128 + nt], k.onesf[0:1, :], brow[0:1, g * 128:g * 128 + nt], False, True,
                             ["brow", "onesf"], [kpm_])
                    _copy(c, "act", sT[:, g4 * 4:(g4 + 1) * 4, tsl], pm_[:, :].rearrange("p (g q) -> p g q", g=4)[:, :, 0:nt],
                          [kpm_], [("sT", blk)])
            stk = [("sT", b_) for b_ in range(9)]
            for which, col in ((0, C_BU), (1, C_BG)):
                for wt in range(2):
                    Wt, wkeys = _loadW(k, "Wt", w_in, col + wt * 512, 512)
                    for ch, (hv, nt, hkeys) in enumerate(chunks):
                        ts_, _ = _tok(ch)
                        for hh in range(4):
                            f = wt * 4 + hh
                            ps, kps = _proj_fm(k, Wt, wkeys, hh * 128, hv, nt, hkeys)
                            if which == 0:
                                _tt(c, "dve", BT[:, f, ts_], ps[:, 0:nt], sT[:, f, ts_], ALU.mult, [kps] + stk, [("BT", f, ch)])
                            else:
                                t1, kt1 = c.nxt("t1")
                                _act(c, t1[:, 0:nt], ps[:, 0:nt], AF.Silu, [kps], [kt1])
                                _tt(c, "dve", BT[:, f, ts_], BT[:, f, ts_], t1[:, 0:nt], ALU.mult, [kt1, ("BT", f, ch)], [("BT", f, ch)])
            _barrier(c.P)
            for wt in range(2):
                Wt, wkeys = _loadW(k, "Wt", w_in, C_CQ + wt * 512, 512)
                for ch, (hv, nt, hkeys) in enumerate(chunks):
                    ts_, _ = _tok(ch)
                    for hl in range(2):
                        hm = wt * 2 + hl
                        fs, sqs = [], []
                        for dh in range(2):
                            ps, kps = _proj_fm(k, Wt, wkeys, (hl * 2 + dh) * 128, hv, nt, hkeys)
                            fa, kfa = c.nxt("fa" if dh == 0 else "fb")
                            _copy(c, "act", fa[:, 0:nt], ps[:, 0:nt], [kps], [kfa])
                            sq, ksq = c.nxt("t1" if dh == 0 else "t2")
                            _act(c, sq[:, 0:nt], ps[:, 0:nt], AF.Square, [kps], [ksq])
                            fs.append((fa, kfa))
                            sqs.append((sq, ksq))
                        p2, kp2 = c.nxt("pB")
                        for dh in range(2):
                            c.mm(p2[:, 0:nt], k.onesf[:], sqs[dh][0][:, 0:nt], dh == 0, dh == 1, [sqs[dh][1], "onesf"], [kp2])
                        rs, krs = c.nxt("t1")
                        _act(c, rs[:, 0:nt], p2[:, 0:nt], AF.Sqrt, [kp2, "epsc"], [krs], scale=1.0 / 256, bias=k.epsc[:])
                        _recip(c, rs[:, 0:nt], rs[:, 0:nt], [krs], [krs])
                        for dh in range(2):
                            _stt(c, cqT[:, 2 * hm + dh, ts_], fs[dh][0][:, 0:nt], gmqT[:, dh:dh + 1], rs[:, 0:nt], ALU.mult, ALU.mult,
                                 [fs[dh][1], krs, "gmq0", "gmq1"], [("cqT", 2 * hm + dh, ch)])
            for ch in range(3):
                ts_, nt = _tok(ch)
                mkT, mvb, mkey = (k.mkTs, k.mvbs, "mems") if ch == 2 else (k.mkT, k.mvb, "memp")
                for hm in range(4):
                    pes = []
                    for mb in range(2):
                        pS, kpS = c.nxt("pA")
                        for dh in range(2):
                            c.mm(pS[:, 0:nt], mkT[:, 2 * hm + dh, mb * 128:(mb + 1) * 128], cqT[:, 2 * hm + dh, ts_], dh == 0, dh == 1,
                                 [mkey, ("cqT", 2 * hm + dh, ch)], [kpS])
                        pe, kpe = c.nxt("pe")
                        _act(c, pe[:, 0:nt], pS[:, 0:nt], AF.Exp, [kpS], [kpe], scale=1.0 / 16)
                        pes.append((pe, kpe))
                    pden, kpden = c.nxt("pB")
                    for mb in range(2):
                        c.mm(pden[:, 0:nt], onesb[:], pes[mb][0][:, 0:nt], mb == 0, mb == 1, [pes[mb][1], "onesb"], [kpden])
                    rd, krd = c.nxt("t2")
                    _recip(c, rd[:, 0:nt], pden[:, 0:nt], [kpden], [krd])
                    for dh in range(2):
                        po, kpo = c.nxt("pB")
                        for mb in range(2):
                            c.mm(po[:, 0:nt], mvb[:, mb, hm * 256 + dh * 128:hm * 256 + (dh + 1) * 128], pes[mb][0][:, 0:nt],
                                 mb == 0, mb == 1, [pes[mb][1], mkey], [kpo])
                        _tt(c, "dve", CT[:, 2 * hm + dh, ts_], po[:, 0:nt], rd[:, 0:nt], ALU.mult, [kpo, krd], [("CT", 2 * hm + dh, ch)])
            for wt in range(2):
                Wt, wkeys = _loadW(k, "Wt", w_in, C_CG + wt * 512, 512)
                for ch, (hv, nt, hkeys) in enumerate(chunks):
                    ts_, _ = _tok(ch)
                    for hh in range(4):
                        f = wt * 4 + hh
                        ps, kps = _proj_fm(k, Wt, wkeys, hh * 128, hv, nt, hkeys)
                        t1, kt1 = c.nxt("t1")
                        _act(c, t1[:, 0:nt], ps[:, 0:nt], AF.Silu, [kps], [kt1])
                        _tt(c, "dve", CT[:, f, ts_], CT[:, f, ts_], t1[:, 0:nt], ALU.mult, [kt1, ("CT", f, ch)], [("CT", f, ch)])
        _barrier(c.P)
        mT = c.sb("mT", [128, KC, TOWN + NS], BF16)
        with _scope(k):
            c.pool("Wp", 3, [128, 8, 256], BF16)
            c.pool("Wr", 3, [128, KC, 256], BF16)
            c.pool("acc", 2, [128, 512], F32)
            wps = [k.din(n, [1024, D]).rearrange("(kc p) n -> p kc n", p=128) for n in ("w_pa", "w_pb", "w_pc")]
            XT = (AT, BT, CT)
            xn_ = ("AT", "BT", "CT")
            for F2 in range(8):
                Wps, Wrs = [], []
                for X in range(3):
                    Wps.append(_loadW(k, "Wp", wps[X], F2 * 256, 256, nkc=8))
                    Wrs.append(_loadW(k, "Wr", w_in, (C_RA, C_RB, C_RC)[X] + F2 * 256, 256))
                for ch, (hv, nt, hkeys) in enumerate(chunks):
                    ts_, _ = _tok(ch)
                    for fl in range(2):
                        f = F2 * 2 + fl
                        acc, kacc = c.nxt("acc")
                        for X in range(3):
                            Wp, wpk = Wps[X]
                            Wr, wrk = Wrs[X]
                            ps1, kps1 = c.nxt("pA")
                            for kc in range(8):
                                c.mm(ps1[:, 0:nt], Wp[:, kc, fl * 128:(fl + 1) * 128], XT[X][:, kc, ts_], kc == 0, kc == 7,
                                     wpk + [(xn_[X], kc, ch)], [kps1])
                            ps2, kps2 = _proj_fm(k, Wr, wrk, fl * 128, hv, nt, hkeys, pool="pB")
                            sg, ksg = c.nxt("t1")
                            _act(c, sg[:, 0:nt], ps2[:, 0:nt], AF.Sigmoid, [kps2], [ksg])
                            if X == 0:
                                _tt(c, "dve", acc[:, 0:nt], ps1[:, 0:nt], sg[:, 0:nt], ALU.mult, [kps1, ksg], [kacc])
                            else:
                                tm, ktm = c.nxt("t2")
                                _tt(c, "dve", tm[:, 0:nt], ps1[:, 0:nt], sg[:, 0:nt], ALU.mult, [kps1, ksg], [ktm])
                                _tt(c, "pool", acc[:, 0:nt], acc[:, 0:nt], tm[:, 0:nt], ALU.add, [kacc, ktm], [kacc])
                        _copy(c, "act", mT[:, f, ts_], acc[:, 0:nt], [kacc], [("mT", f, ch)])
        _barrier(c.P)
        with _scope(k):
            c.pool("Wt", 2, [128, KC, 512], BF16)
            c.pool("xq", 3, [128, 512], F32)
            c.pool("yq", 3, [128, 512], F32)
            w_out = k.din("w_out", [D, D]).rearrange("(kc p) n -> p kc n", p=128)
            yo = k.dout("y_out", [TOWN, D])
            yso = k.dout("ys_out", [NS, D])
            xall, xs = k.dr["xall"], k.dr["xs"]
            for ct in range(4):
                Wt, wkeys = _loadW(k, "Wt", w_out, ct * 512, 512)
                for blk in range(9):
                    samp = blk == 8
                    nt = NS if samp else 128
                    tsl = slice(1024, 1024 + NS) if samp else slice(blk * 128, (blk + 1) * 128)
                    ch = 2 if samp else blk // 4
                    xq, kxq = c.nxt("xq")
                    src = xs[:, ct * 512:(ct + 1) * 512] if samp else xall[(4 * blk) * 128:(4 * blk + 1) * 128, ct * 512:(ct + 1) * 512]
                    c.dma(xq[0:nt, :], src, [], [kxq], kxq)
                    ps, kps = c.nxt("pA")
                    for kc in range(KC):
                        c.mm(ps[0:nt, :], mT[:, kc, tsl], Wt[:, kc, :], kc == 0, kc == KC - 1, wkeys + [("mT", kc, ch)], [kps])
                    yq, kyq = c.nxt("yq")
                    _tt(c, "dve", yq[0:nt, :], ps[0:nt, :], xq[0:nt, :], ALU.add, [kps, kxq], [kyq])
                    dst = yso[:, ct * 512:(ct + 1) * 512] if samp else yo[blk * 128:(blk + 1) * 128, ct * 512:(ct + 1) * 512]
                    c.dma(dst, yq[0:nt, :], [kyq], [], kyq, q="pool")


LS = NPG * 128 + NS


def _igather(k, out, src, idx_col, reads, writes, key):
    k.c.P.add("pool", lambda e: e.indirect_dma_start(out=out, out_offset=None, in_=src,
                                                     in_offset=bass.IndirectOffsetOnAxis(ap=idx_col, axis=0)),
              reads, writes, dma=key)


def phase2s(k):
    c = k.c
    cki = k.din("cki", [1280 * 128, 64])
    ck = k.din("ck", [1280 * 128, 512])
    cv = k.din("cv", [1280 * 128, 512])
    with _scope(k):
        scs = c.sb("scs", [NS, LS], F32)
        Ms = c.sb("Ms", [NS, LS], BF16)
        junk = Ms
        st = c.sb("thrs", [128, 16], F32)
        hwts = c.sb("hwts", [128, 32], F32)
        oat = c.sb("oats", [128, 8, 128], BF16)
        rden = c.sb("rdens", [128, 8], F32)
        ptb = c.sb("ptb", [128, NPG], I32)
        idx = c.sb("idx", [128, NPG], I32)
        pidx = c.sb("pidx", [128, 1], F32)
        mbs = c.sb("mbs", [NS, NS], F32)
        sels = c.sb("sels", [64, NS], BF16)
        BDs = c.sb("BDs", [128, 8, 64], BF16)
        Wss = c.sb("Wss", [64, 8, NS], BF16)
        MTn = c.sb("MTn", [NS, NS], BF16)
        c.pool("kid", 3, [128, 128], F32)
        c.pool("kiTp", 2, [128, 512], BF16)
        c.pool("rl", 3, [64, 512], BF16)
        c.pool("Kp", 3, [128, 512], F32)
        c.pool("Vp", 3, [128, 512], F32)
        c.pool("VEp", 3, [128, 4, 130], BF16)
        c.pool("KTp", 2, [128, 4, 128], BF16)
        c.pool("MTp", 2, [128, NS], BF16)
        c.pool("ex", 3, [128, 512], BF16)
        c.pool("pm", 3, [128, 4, 128], BF16)
        c.pool("pD", 2, [128, 512], F32, psum=True)
        c.pool("pSC", 1, [128, 512], F32, psum=True)
        c.pool("pT", 1, [128, 8, 128], BF16, psum=True)
        c.pool("pTf", 1, [128, 4, 128], F32, psum=True)
        c.pool("po", 3, [128, 512], F32, psum=True)
        c.pool("xblk", 1, [128, 1024], F32)
        c.pool("xn", 1, [128, 1024], BF16)
        cmk = k.din("cmk", [256, 1024])
        cmv = k.din("cmv", [256, 1024])
        for half in range(2):
            for t in range(2):
                ob, kob = c.nxt("xblk")
                c.dma(ob[:, 0:1024], (cmk if half == 0 else cmv)[t * 128:(t + 1) * 128, :], [], [kob], kob)
                k.mem_pack(ob, kob, half, t, k.mkTs, k.mvbs, "mems")
        c.dma(ptb[:], k.din("ptab_bc", [128, NPG], I32)[:, :], [], ["ptb"], "c40")
        c.dma(pidx[:], k.din("pidx", [128, 1])[:, :], [], ["pidx"], "c41")
        c.dma(mbs[:], k.din("maskbs", [NS, NS])[:, :], [], ["mbs"], "c42")
        c.dma(sels[:], k.din("sels", [64, NS])[:, :], [], ["sels"], "c43", q="pool")
        _ts(c, "dve", idx[:, :], ptb[:, :], 128.0, pidx[:, 0:1], ALU.mult, ALU.add, ["ptb", "pidx"], ["idx"])
        for i in range(3):
            ve = c.rot["VEp"][0][i]
            c.op("pool", (lambda b: (lambda e: e.memset(b[:, :, 128:129], 1.0)))(ve), [], [(("VEp", i), "one")])
        c.op("pool", lambda e: e.memset(BDs[:, :, :], 0.0), [], ["BDs"])
        for cc in range(8):
            for jj in range(2):
                rows = slice(jj * 64, (jj + 1) * 64)
                _copy(c, "pool", BDs[rows, cc, jj * 32:jj * 32 + NS], k.qiTs[rows, cc, :], ["qiTs", "BDs"], ["BDs"])
            _ts(c, "pool", Wss[:, cc, :], sels[:, :], k.wvs[:, cc:cc + 1], None, ALU.mult, ALU.bypass, ["sels", "wvs"], ["Wss"])

        def index_chunk(rhs, n, rkeys, col0, last):
            psc, kpsc = c.nxt("pSC")
            rls = {}

            def stage0(cc):
                pd, kpd = c.nxt("pD")
                c.mm(pd[0:64, 0:n], BDs[:, cc, :], rhs, True, True, ["BDs"] + rkeys, [kpd])
                rl, krl = c.nxt("rl")
                _act(c, rl[:, 0:n], pd[0:64, 0:n], AF.Relu, [kpd], [krl])
                rls[cc] = (rl, krl)

            stage0(0)
            for cc in range(8):
                if cc + 1 < 8:
                    stage0(cc + 1)
                rl, krl = rls.pop(cc)
                c.mm(psc[0:NS, 0:n], Wss[:, cc, :], rl[:, 0:n], cc == 0, cc == 7, [krl, "Wss"], [kpsc])
            if last:
                _tt(c, "dve", scs[:, col0:col0 + n], psc[0:NS, 0:n], mbs[:, :], ALU.add, [kpsc, "mbs"], ["scs"])
            else:
                _copy(c, "act", scs[:, col0:col0 + n], psc[0:NS, 0:n], [kpsc], ["scs"])

        for c4 in range(NPG // 4):
            pt, kpt = c.nxt("pTf")
            for i in range(4):
                pg = c4 * 4 + i
                kid, kkid = c.nxt("kid")
                _igather(k, kid[:, 0:64], cki[:, :], idx[:, pg:pg + 1], ["idx"], [(kkid, 0)], (kkid, 0))
                _copy(c, "dve", kid[:, 64:128], kid[:, 0:64], [(kkid, 0)], [(kkid, 1)])
                c.tr(pt[:, i, :], kid[:, :], k.identf[:, :], [(kkid, 0), (kkid, 1), "identf"], [kpt])
            kiTp, kkt = c.nxt("kiTp")
            _copy(c, "act", kiTp[:, :].rearrange("p (a q) -> p a q", a=4), pt[:, 0:4, :], [kpt], [kkt])
            index_chunk(kiTp[:, :], 512, [kkt], c4 * 512, False)
        index_chunk(k.kiTs[:, :], NS, ["kiTs"], NPG * 128, True)
        st4 = st[0:NS, :]
        c.op("dve", lambda e: e.tensor_reduce(out=st4[:, 0:1], in_=scs[:, 0:NPG * 128], axis=AX.X, op=ALU.min), ["scs"], ["thr"])
        c.op("dve", lambda e: e.tensor_reduce(out=st4[:, 1:2], in_=scs[:, :], axis=AX.X, op=ALU.max), ["scs"], ["thr"])
        _threshold(k, scs[:, :], LS, st4, junk[:, :], hwts[0:NS, :])
        _ts(c, "dve", Ms[:, :], scs[:, :], st4[:, 0:1], None, ALU.is_ge, ALU.bypass, ["thr", "scs"], ["Ms"])
        po, kpo = [], []
        for i in range(3):
            a, b = c.nxt("po")
            po.append(a)
            kpo.append(b)
        qrhs = [k.QTs[:, 2 * g:2 * g + 2, :].rearrange("p h q -> p (h q)") for g in range(4)]
        def prep(pg):
            Kp, kKp = c.nxt("Kp")
            _igather(k, Kp[:, :], ck[:, :], idx[:, pg:pg + 1], ["idx"], [kKp], kKp)
            Vp, kVf = c.nxt("Vp")
            _igather(k, Vp[:, :], cv[:, :], idx[:, pg:pg + 1], ["idx"], [kVf], kVf)
            VEp, kVp = c.nxt("VEp")
            _copy(c, "act", VEp[:, :, 0:128], Vp[:, :].rearrange("p (g d) -> p g d", g=4), [kVf], [kVp])
            ptf, kptf = c.nxt("pTf")
            for g in range(4):
                c.tr(ptf[:, g, :], Kp[:, g * 128:(g + 1) * 128], k.identf[:, :], [kKp, "identf"], [kptf])
            KTp, kKT = c.nxt("KTp")
            _copy(c, "act", KTp[:, :, :], ptf[:, :, :], [kptf], [kKT])
            pt, kpt = c.nxt("pT")
            c.tr(pt[:, 4, 0:NS], Ms[:, pg * 128:(pg + 1) * 128], k.identb[0:NS, 0:NS], ["Ms", "identb"], [kpt])
            MTp, kMT = c.nxt("MTp")
            _copy(c, "act", MTp[:, :], pt[:, 4, 0:NS], [kpt], [kMT])
            return dict(KTg=[KTp[:, g, :] for g in range(4)], ktk=[kKT], VEg=[VEp[:, g, :] for g in range(4)],
                        vek=[kVp, (kVp, "one")], nkeys=128, MT=MTp[:, :], mtk=kMT)

        def prep_new():
            pt, kpt = c.nxt("pT")
            c.tr(pt[0:NS, 0, 0:NS], Ms[:, NPG * 128:LS], k.identb[0:NS, 0:NS], ["Ms", "identb"], [kpt])
            _copy(c, "act", MTn[:, :], pt[0:NS, 0, 0:NS], [kpt], ["MTn"])
            return dict(KTg=[k.KTs[:, g, :] for g in range(4)], ktk=[("KTs", g) for g in range(4)],
                        VEg=[k.VEs[0:NS, g, :] for g in range(4)], vek=["VEs", "VEs1"], nkeys=NS, MT=MTn[:, :], mtk="MTn")

        def s1(P, gp):
            return _att_s1(k, NS, P["KTg"], P["ktk"], P["nkeys"], qrhs, ["QTs"], P["MT"], P["mtk"], gp, "dve")

        def s2(P, gp, stt_, first, last):
            _att_s2(k, NS, stt_[0], stt_[1], P["VEg"], P["vek"], P["nkeys"], gp, po, kpo, first, last)

        import os
        if os.environ.get("PIPE_S", "0") == "1":
            cur = prep(0)
            sts = [s1(cur, 0), s1(cur, 1)]
            for pg in range(NPG + 1):
                nxtP = prep(pg + 1) if pg + 1 < NPG else (prep_new() if pg + 1 == NPG else None)
                nsts = []
                for gp in range(2):
                    s2(cur, gp, sts[gp], pg == 0, pg == NPG)
                    if nxtP is not None:
                        nsts.append(s1(nxtP, gp))
                cur, sts = nxtP, nsts
        else:
            for pg in range(NPG + 1):
                cur = prep(pg) if pg < NPG else prep_new()
                for gp in range(2):
                    s2(cur, gp, s1(cur, gp), pg == 0, pg == NPG)
        _attn_finish(k, NS, po, kpo, oat, rden, k.QTs[:, :, :], ["QTs"])
```

```python
import numpy as np
import concourse.bass as bass
import concourse.mybir as mybir
from concourse.bass_utils import run_bass_kernel_spmd

F32 = mybir.dt.float32
BF16 = mybir.dt.bfloat16
I32 = mybir.dt.int32
AF = mybir.ActivationFunctionType
ALU = mybir.AluOpType
AX = mybir.AxisListType

D = 2048
KC = 16
NB = 32
NG = 8
TOWN = 1024
NS = 4
EPS = 1e-6
NEG = -1.0e30
NPG = 128
C_AQ, C_AK, C_AV, C_AQI, C_AKI, C_AWI, C_AG = 0, 1024, 1536, 2048, 3072, 3136, 3152
C_BU, C_BV, C_BG, C_CQ, C_CG, C_RA, C_RB, C_RC = 4176, 5200, 6224, 7248, 8272, 9296, 11344, 13392


class Op:
    __slots__ = ("eng", "fn", "reads", "writes", "dma", "deps", "need_inc", "tick", "idx")


class Prog:
    ENGS = ("pe", "act", "dve", "pool", "sp")

    def __init__(self):
        self.ops = []
        self.last_w = {}
        self.readers = {}

    def add(self, eng, fn, reads=(), writes=(), dma=None):
        op = Op()
        op.eng, op.fn, op.dma = eng, fn, dma
        op.reads, op.writes = tuple(reads), tuple(writes)
        op.need_inc = dma is not None
        op.tick = None
        op.idx = len(self.ops)
        deps = []
        for r in op.reads:
            w = self.last_w.get(r)
            if w is not None:
                deps.append(w)
        for w_ in op.writes:
            w = self.last_w.get(w_)
            if w is not None:
                deps.append(w)
            deps.extend(self.readers.get(w_, ()))
        seen = set()
        op.deps = []
        for d in deps:
            if d.idx in seen or d is op:
                continue
            seen.add(d.idx)
            if d.dma is None and d.eng == "pe" and eng == "pe" and dma is None:
                continue
            op.deps.append(d)
            d.need_inc = True
        for r in op.reads:
            lst = self.readers.setdefault(r, [])
            if dma is None:
                lst[:] = [o for o in lst if not (o.dma is None and o.eng == eng)]
            lst.append(op)
        for w_ in op.writes:
            self.last_w[w_] = op
            self.readers[w_] = []
        self.ops.append(op)
        return op

    def emit(self, nc, stack):
        CH = 20000
        cnt = {e: 0 for e in self.ENGS}
        dma_cnt = {}
        for op in self.ops:
            if op.dma is not None:
                dma_cnt[op.dma] = dma_cnt.get(op.dma, 0) + 1
                op.tick = (("dma", op.dma), dma_cnt[op.dma] * 16)
            elif op.need_inc:
                cnt[op.eng] += 1
                t = cnt[op.eng] - 1
                op.tick = ((op.eng, t // CH), t % CH + 1)
        sems = {}

        def sem(key):
            if key not in sems:
                sems[key] = stack.enter_context(nc.semaphore("s%d" % len(sems)))
            return sems[key]

        for op in self.ops:
            if op.tick is not None:
                sem(op.tick[0])
        final_waits = [(("dma", k), v * 16) for k, v in dma_cnt.items()]
        block = stack.enter_context(nc.Block())
        prog = self

        def run(engname, eng):
            waited = {}
            for op in prog.ops:
                if op.eng != engname:
                    continue
                for d in op.deps:
                    key, val = d.tick
                    if waited.get(key, 0) >= val:
                        continue
                    waited[key] = val
                    eng.wait_ge(sems[key], val)
                ins = op.fn(eng)
                if op.tick is not None:
                    ins.then_inc(sems[op.tick[0]], 16 if op.dma is not None else 1)
            if engname == "sp":
                for key, val in final_waits:
                    if waited.get(key, 0) < val:
                        eng.wait_ge(sems[key], val)

        @block.sync
        def _(e):
            run("sp", e)

        @block.gpsimd
        def _(e):
            run("pool", e)

        @block.vector
        def _(e):
            run("dve", e)

        @block.scalar
        def _(e):
            run("act", e)

        @block.tensor
        def _(e):
            run("pe", e)
        print("ops:", len(self.ops), "sems:", len(sems), {e: cnt[e] for e in cnt})


class Ctx:
    def __init__(self, nc, stack):
        self.nc, self.stack = nc, stack
        self.P = Prog()
        self.rot = {}
        self.nuniq = 0

    def sb(self, name, shape, dt):
        self.nuniq += 1
        return self.stack.enter_context(self.nc.sbuf_tensor("s%d_%s" % (self.nuniq, name), list(shape), dt))

    def ps(self, name, shape, dt):
        self.nuniq += 1
        return self.stack.enter_context(self.nc.psum_tensor("p%d_%s" % (self.nuniq, name), list(shape), dt))

    def pool(self, name, n, shape, dt, psum=False):
        bufs = [(self.ps if psum else self.sb)("%s%d" % (name, i), shape, dt) for i in range(n)]
        self.rot[name] = [bufs, 0]

    def nxt(self, name):
        bufs, i = self.rot[name]
        self.rot[name][1] = i + 1
        k = i % len(bufs)
        return bufs[k], (name, k)

    def dma(self, out, in_, reads, writes, key, q="sp"):
        self.P.add(q, lambda e: e.dma_start(out=out, in_=in_), reads, writes, dma=key)

    def mm(self, out, lhsT, rhs, start, stop, reads, writes):
        self.P.add("pe", lambda e: e.matmul(out, lhsT, rhs, start=start, stop=stop), reads, writes)

    def tr(self, out, in_, ident, reads, writes):
        self.P.add("pe", lambda e: e.transpose(out, in_, ident), reads, writes)

    def op(self, eng, fn, reads, writes):
        self.P.add(eng, fn, reads, writes)


def _act(c, out, in_, func, reads, writes, **kw):
    c.op("act", lambda e: e.activation(out=out, in_=in_, func=func, **kw), reads, writes)


def _tt(c, eng, out, a, b, op, reads, writes):
    c.op(eng, lambda e: e.tensor_tensor(out=out, in0=a, in1=b, op=op), reads, writes)


def _ts(c, eng, out, a, s1, s2, op0, op1, reads, writes, accum_out=None):
    if accum_out is None:
        c.op(eng, lambda e: e.tensor_scalar(out=out, in0=a, scalar1=s1, scalar2=s2, op0=op0, op1=op1), reads, writes)
    else:
        c.op(eng, lambda e: e.tensor_scalar(out=out, in0=a, scalar1=s1, scalar2=s2, op0=op0, op1=op1,
                                            accum_out=accum_out), reads, writes)


def _copy(c, eng, out, in_, reads, writes):
    if eng == "act":
        c.op("act", lambda e: e.copy(out=out, in_=in_), reads, writes)
    else:
        c.op(eng, lambda e: e.tensor_copy(out=out, in_=in_), reads, writes)


def _recip(c, out, in_, reads, writes):
    c.op("dve", lambda e: e.reciprocal(out=out, in_=in_), reads, writes)


def _bc(ap2d, n):
    return ap2d.unsqueeze(2).to_broadcast([ap2d.shape[0], ap2d.shape[1], n])


class Pipe3:
    def __init__(self):
        import os
        self.on = os.environ.get("ROPE_PIPE", "1") == "1"
        self.items = []

    def push(self, fa, fb, fc):
        S = fa()
        if not self.on:
            fb(S)
            fc(S)
            return
        self.items.append((S, fb, fc))
        n = len(self.items)
        if n >= 3:
            S2, _, fc2 = self.items[n - 3]
            fc2(S2)
        if n >= 2:
            S1, fb1, _ = self.items[n - 2]
            fb1(S1)

    def flush(self):
        if not self.on:
            return
        n = len(self.items)
        if n >= 2:
            S2, _, fc2 = self.items[n - 2]
            fc2(S2)
        if n >= 1:
            S1, fb1, fc1 = self.items[n - 1]
            fb1(S1)
            fc1(S1)
        self.items = []

class K:
    def __init__(self, nc, stack):
        self.c = Ctx(nc, stack)
        self.nc = nc
        self.stack = stack
        self.dr = {}

    def din(self, name, shape, dt=F32):
        self.dr[name] = self.nc.dram_tensor(name, list(shape), dt, kind="ExternalInput").ap()
        return self.dr[name]

    def dout(self, name, shape, dt=F32):
        self.dr[name] = self.nc.dram_tensor(name, list(shape), dt, kind="ExternalOutput").ap()
        return self.dr[name]

    def consts(self):
        c, dr = self.c, self.dr
        self.identb = c.sb("identb", [128, 128], BF16)
        self.identf = c.sb("identf", [128, 128], F32)
        self.onesf = c.sb("onesf", [128, 128], F32)
        self.R128 = c.sb("R128", [128, 128], F32)
        self.R64 = c.sb("R64", [128, 128], F32)
        self.gpreT = c.sb("gpreT", [128, 16], F32)
        self.gk = c.sb("gk", [128, 1], F32)
        self.gq = c.sb("gq", [128, 1], F32)
        self.epsc = c.sb("epsc", [128, 1], F32)
        self.ckt = c.sb("ckt", [128, 32], F32)
        c.dma(self.ckt[:], self.din("ckt", [128, 32])[:, :], [], ["ckt"], "c11")
        cf = self.din("cf32", [128, 4, 128])
        c.dma(self.identf[:], cf[:, 0, :], [], ["identf"], "c0")
        c.dma(self.onesf[:], cf[:, 1, :], [], ["onesf"], "c1")
        c.dma(self.R128[:], cf[:, 2, :], [], ["R128"], "c2")
        c.dma(self.R64[:], cf[:, 3, :], [], ["R64"], "c3")
        c.dma(self.identb[:], cf[:, 0, :], [], ["identb"], "c4", q="pool")
        c.dma(self.gpreT[:], self.din("gpreT", [128, 16])[:, :], [], ["gpreT"], "c5")
        c.dma(self.gq[:], self.din("gq", [128, 1])[:, :], [], ["gq"], "c6")
        c.dma(self.gk[:], self.din("gk", [128, 1])[:, :], [], ["gk"], "c7")
        c.op("dve", lambda e: e.memset(self.epsc[:], EPS), [], ["epsc"])

    def norm_block(self, xrows, nt, dst_list, gT=None, gkey="gpreT"):
        c = self.c
        gT = self.gpreT if gT is None else gT
        xb, kx = c.nxt("xblk")
        c.dma(xb[0:nt, :], xrows, [], [kx], kx)
        ss, kss = c.nxt("ss")
        xn, kxn = c.nxt("xn")
        _act(c, xn[0:nt, :], xb[0:nt, :], AF.Square, [kx], [kxn, kss], accum_out=ss[0:nt, 0:1])
        _act(c, ss[0:nt, 1:2], ss[0:nt, 0:1], AF.Sqrt, [kss, "epsc"], [(kss, 1)], scale=1.0 / D, bias=self.epsc[0:nt, :])
        _recip(c, ss[0:nt, 2:3], ss[0:nt, 1:2], [(kss, 1)], [(kss, 2)])
        _ts(c, "dve", xn[0:nt, :], xb[0:nt, :], ss[0:nt, 2:3], None, ALU.mult, ALU.bypass, [kx, (kss, 2)], [kxn])
        for h in range(2):
            pt, kpt = c.nxt("pT")
            for k8 in range(8):
                kc = h * 8 + k8
                c.tr(pt[:, k8, 0:nt], xn[0:nt, kc * 128:(kc + 1) * 128], self.identb[0:nt, 0:nt],
                     [kxn, "identb"], [kpt])
            for (dst, kd) in dst_list:
                _tt(c, "dve", dst[:, h * 8:(h + 1) * 8, :], pt[:, :, 0:nt], _bc(gT[:, h * 8:(h + 1) * 8], nt),
                    ALU.mult, [kpt, gkey], [kd])

    def rope_B(self, S):
        c = self.c
        nt, ps, kps = S["nt"], S["ps"], S["kps"]
        kf, kkf = c.nxt("f_kf")
        _copy(c, "act", kf[:, 0:nt], ps, [kps], [kkf])
        S["kf"], S["kkf"] = kf, kkf
        if S["normalize"]:
            sq, ksq = c.nxt("f_sq")
            _act(c, sq[:, 0:nt], ps, AF.Square, [kps], [ksq])
            p2, kp2 = c.nxt("pB")
            c.mm(p2[:, 0:nt], self.onesf[:], sq[:, 0:nt], True, True, [ksq, "onesf"], [kp2])
            S["p2"], S["kp2"] = p2, kp2

    def rope_C(self, S):
        c = self.c
        nt, kf, kkf = S["nt"], S["kf"], S["kkf"]
        g_ap, gkey, R, cos, sin, krope, inv_d = S["g_ap"], S["gkey"], S["R"], S["cos"], S["sin"], S["krope"], S["inv_d"]
        if S["normalize"]:
            p2, kp2 = S["p2"], S["kp2"]
            rs, krs = c.nxt("f_rs")
            _act(c, rs[:, 0:nt], p2[:, 0:nt], AF.Sqrt, [kp2, "epsc"], [krs], scale=inv_d, bias=self.epsc[:])
            _recip(c, rs[:, 0:nt], rs[:, 0:nt], [krs], [krs])
            kn, kkn = c.nxt("f_kn")
            c.op("dve", lambda e: e.scalar_tensor_tensor(out=kn[:, 0:nt], in0=kf[:, 0:nt], scalar=g_ap, in1=rs[:, 0:nt],
                                                         op0=ALU.mult, op1=ALU.mult), [kkf, krs, gkey], [kkn])
        else:
            kn, kkn = kf, kkf
        p3, kp3 = c.nxt("pB")
        c.mm(p3[:, 0:nt], R[:], kn[:, 0:nt], True, True, [kkn, "R128", "R64"], [kp3])
        t1, kt1 = c.nxt("f_t1")
        _tt(c, "pool", t1[:, 0:nt], kn[:, 0:nt], cos, ALU.mult, [kkn, krope], [kt1])
        t2, kt2 = c.nxt("f_t2")
        _tt(c, "dve", t2[:, 0:nt], p3[:, 0:nt], sin, ALU.mult, [kp3, krope], [kt2])
        kr, kkr = c.nxt("f_kr")
        _tt(c, "pool", kr[:, 0:nt], t1[:, 0:nt], t2[:, 0:nt], ALU.add, [kt1, kt2], [kkr])
        return kr, kkr

    def rope_state(self, ps, kps, nt, g_ap, gkey, R, cos, sin, krope, inv_d, normalize=True):
        return dict(ps=ps, kps=kps, nt=nt, g_ap=g_ap, gkey=gkey, R=R, cos=cos, sin=sin, krope=krope, inv_d=inv_d,
                    normalize=normalize)

    def rope_fm(self, ps, kps, nt, g_ap, gkey, R, cos, sin, krope, inv_d, normalize=True):
        S = self.rope_state(ps, kps, nt, g_ap, gkey, R, cos, sin, krope, inv_d, normalize)
        self.rope_B(S)
        return self.rope_C(S)

    def alloc_f(self):
        c = self.c
        for n in ("f_kf", "f_sq", "f_rs", "f_kn", "f_t1", "f_t2", "f_kr"):
            c.pool(n, 1, [128, 512], F32)

    def phase1(self):
        c, dr, nc = self.c, self.dr, self.nc
        xall = self.din("xall", [4096, D])
        xs = self.din("xs", [NS, D])
        wkvi_d = self.din("w_in", [D, 15440])
        rope = self.din("rope", [9, 128, 4, 512])
        ko = self.dout("k_out", [TOWN, 4, 128])
        vo = self.dout("v_out", [TOWN, 4, 128])
        kio = self.dout("ki_out", [TOWN, 64])
        kso = self.dout("ks_out", [NS, 4, 128])
        vso = self.dout("vs_out", [NS, 4, 128])
        kiso = self.dout("kis_out", [NS, 64])
        c.op("pool", lambda e: e.memset(self.VE[:, :, :, 128:129], 1.0), [], ["VE1"])
        c.op("pool", lambda e: e.memset(self.VEs[:, :, 128:129], 1.0), [], ["VEs1"])
        with ExitStackLike(self) as st:
            old = c.stack
            c.stack = st
            c.pool("xblk", 2, [128, D], F32)
            c.pool("xn", 1, [128, D], BF16)
            c.pool("ss", 4, [128, 4], F32)
            hTg = c.sb("hTg", [128, KC, 512], BF16)
            W = c.sb("Wkvi", [128, KC, 1152], BF16)
            c.pool("ropeT", 1, [128, 4, 512], F32)
            self.alloc_f()
            c.pool("kout", 1, [128, 4, 128], F32)
            c.pool("vout", 1, [128, 4, 128], F32)
            c.pool("kiout", 1, [128, 64], F32)
            c.pool("pT", 2, [128, 8, 128], BF16, psum=True)
            c.pool("pA", 3, [128, 512], F32, psum=True)
            c.pool("pB", 3, [128, 512], F32, psum=True)
            wv = wkvi_d.rearrange("(kc p) n -> p kc n", p=128)
            for kc4 in range(4):
                sl = slice(kc4 * 4, kc4 * 4 + 4)
                c.dma(W[:, sl, 0:1024], wv[:, sl, C_AK:C_AK + 1024], [], [("W", 0, kc4)], ("W", 0, kc4), q="pool")
                c.dma(W[:, sl, 1024:1088], wv[:, sl, C_AKI:C_AKI + 64], [], [("W", 1, kc4)], ("W", 1, kc4), q="pool")
                c.dma(W[:, sl, 1088:1152], wv[:, sl, C_AKI:C_AKI + 64], [], [("W", 2, kc4)], ("W", 2, kc4), q="pool")
            wkeys = [("W", a, b) for a in range(3) for b in range(4)]
            import os
            LV = int(os.environ.get("DBG_LV", "9"))
            for g in (range(NG + 1) if LV >= 9 else [0]):
                samp = g == NG
                nt = NS if samp else 512
                nown = NS if samp else 128
                if samp:
                    self.norm_block(xs[:, :], NS, [(hTg[:, :, 0:NS], "hTg")])
                else:
                    for s in range(4):
                        p = 4 * g + s
                        self.norm_block(xall[p * 128:(p + 1) * 128, :], 128, [(hTg[:, :, s * 128:(s + 1) * 128], "hTg")])
                if LV < 1:
                    continue
                rt, krt = c.nxt("ropeT")
                c.dma(rt[:], rope[g], [], [krt], krt)
                kout, kko = c.nxt("kout")
                pipe = Pipe3()
                for gh in range(4):
                    def fa(gh=gh):
                        ps, kps = c.nxt("pA")
                        for kc in range(KC):
                            c.mm(ps[:, 0:nt], W[:, kc, gh * 128:(gh + 1) * 128], hTg[:, kc, 0:nt], kc == 0, kc == KC - 1,
                                 ["hTg"] + wkeys, [kps])
                        return self.rope_state(ps[:, 0:nt], kps, nt, self.gk[:, 0:1], "gk", self.R128, rt[:, 0, 0:nt],
                                               rt[:, 1, 0:nt], krt, 1.0 / 128)

                    def fc(S, gh=gh):
                        kr, kkr = self.rope_C(S)
                        if samp:
                            _copy(c, "act", self.KTs[:, gh, :], kr[:, 0:nt], [kkr], [("KTs", gh)])
                        else:
                            _copy(c, "act", self.KT[:, gh, g * 512:(g + 1) * 512], kr[:, 0:nt], [kkr], [("KT", gh, g)])
                        po, kpo = c.nxt("pB")
                        c.tr(po[0:nown, 0:128], kr[:, 0:nown], self.identf[:, :], [kkr, "identf"], [kpo])
                        _copy(c, "act", kout[0:nown, gh, :], po[0:nown, 0:128], [kpo], [(kko, gh)])
                        if gh == 3:
                            dst = kso[:, :, :] if samp else ko[g * 128:(g + 1) * 128, :, :]
                            c.dma(dst, kout[0:nown, :, :], [(kko, i) for i in range(4)], [], kko, q="act")
                    pipe.push(fa, self.rope_B, fc)

                def fa_i():
                    ps, kps = c.nxt("pA")
                    for kc in range(KC):
                        c.mm(ps[:, 0:nt], W[:, kc, 1024:1152], hTg[:, kc, 0:nt], kc == 0, kc == KC - 1, ["hTg"] + wkeys, [kps])
                    return self.rope_state(ps[:, 0:nt], kps, nt, None, None, self.R64, rt[:, 2, 0:nt], rt[:, 3, 0:nt], krt,
                                           0.0, normalize=False)

                def fc_i(S):
                    kr, kkr = self.rope_C(S)
                    if samp:
                        _copy(c, "act", self.kiTs[:, :], kr[:, 0:nt], [kkr], ["kiTs"])
                    else:
                        _copy(c, "act", self.kiT[:, g * 512:(g + 1) * 512], kr[:, 0:nt], [kkr], [("kiT", g)])
                    po, kpo = c.nxt("pB")
                    c.tr(po[0:nown, 0:64], kr[0:64, 0:nown], self.identf[0:64, 0:64], [kkr, "identf"], [kpo])
                    kiout, kkio = c.nxt("kiout")
                    _copy(c, "act", kiout[0:nown, :], po[0:nown, 0:64], [kpo], [kkio])
                    c.dma(kiso[:, :] if samp else kio[g * 128:(g + 1) * 128, :], kiout[0:nown, :], [kkio], [], kkio, q="act")
                pipe.push(fa_i, self.rope_B, fc_i)
                pipe.flush()
                if LV < 6:
                    continue
                for s in range(1 if samp else 4):
                    n1 = NS if samp else 128
                    ps, kps = c.nxt("pA")
                    for kc in range(KC):
                        c.mm(ps[0:n1, :], hTg[:, kc, s * 128:s * 128 + n1], W[:, kc, 512:1024], kc == 0, kc == KC - 1,
                             ["hTg"] + wkeys, [kps])
                    if samp:
                        _copy(c, "act", self.VEs[0:NS, :, 0:128], ps[0:NS, :].rearrange("p (g d) -> p g d", g=4), [kps],
                              ["VEs"])
                    else:
                        _copy(c, "act", self.VE[:, 4 * g + s, :, 0:128], ps[:, :].rearrange("p (g d) -> p g d", g=4),
                              [kps], [("VE", 4 * g + s)])
                    if s == 0 and LV >= 7:
                        vout, kvo = c.nxt("vout")
                        _copy(c, "act", vout[0:n1, :, :], ps[0:n1, :].rearrange("p (g d) -> p g d", g=4), [kps], [kvo])
                        c.dma(vso[:, :, :] if samp else vo[g * 128:(g + 1) * 128, :, :], vout[0:n1, :, :], [kvo], [], kvo, q="act")
            if LV >= 9:
                self.extras(hTg, W, wkeys)
            c.stack = old

    def extras(self, hTg, W, wkeys):
        c = self.c
        w_in = self.dr["w_in"].rearrange("(kc p) n -> p kc n", p=128)
        wm = self.din("w_mem_kv", [D, 2048]).rearrange("(kc p) n -> p kc n", p=128)
        memx = self.din("memx", [256, D])
        cvo = self.dout("cvs_out", [NS, 1024])
        mko = self.dout("mk_out", [256, 1024])
        mvo = self.dout("mv_out", [256, 1024])
        gsgu = c.sb("gsgu", [128, 1024], F32)
        gmk = c.sb("gmk", [128, 256], F32)
        gmemT = c.sb("gmemT", [128, 16], F32)
        mst = c.sb("mst", [128, 12], F32)
        c.dma(gsgu[:], self.din("gsgu_bc", [128, 1024])[:, :], [], ["gsgu"], "c8")
        c.dma(gmk[:], self.din("gmk_bc", [128, 256])[:, :], [], ["gmk"], "c9")
        c.dma(gmemT[:], self.din("gmemT", [128, 16])[:, :], [], ["gmemT"], "c10")

        def loadW(src, col0):
            for kc4 in range(4):
                sl = slice(kc4 * 4, kc4 * 4 + 4)
                c.dma(W[:, sl, 0:1024], src[:, sl, col0:col0 + 1024], [], wkeys + ["WX"], ("WX", kc4), q="pool")

        loadW(w_in, C_BV)
        ob, kob = c.nxt("xblk")
        for half in range(2):
            ps, kps = c.nxt("pA")
            for kc in range(KC):
                c.mm(ps[0:NS, :], hTg[:, kc, 0:NS], W[:, kc, half * 512:(half + 1) * 512], kc == 0, kc == KC - 1,
                     ["hTg", "WX"], [kps])
            _copy(c, "act", ob[0:NS, half * 512:(half + 1) * 512], ps[0:NS, :], [kps], [kob])
        jk, kjk = c.nxt("xn")
        _act(c, jk[0:NS, 0:1024], ob[0:NS, 0:1024], AF.Square, [kob], [kjk, "mst"], accum_out=mst[0:NS, 0:1])
        _act(c, mst[0:NS, 1:2], mst[0:NS, 0:1], AF.Sqrt, ["mst", "epsc"], ["mst"], scale=1.0 / 1024, bias=self.epsc[0:NS, :])
        _recip(c, mst[0:NS, 2:3], mst[0:NS, 1:2], ["mst"], ["mst"])
        c.op("dve", lambda e, ob=ob: e.scalar_tensor_tensor(out=ob[0:NS, 0:1024], in0=ob[0:NS, 0:1024], scalar=mst[0:NS, 2:3],
                                                     in1=gsgu[0:NS, :], op0=ALU.mult, op1=ALU.mult),
             [kob, "mst", "gsgu"], [kob])
        c.dma(cvo[:, :], ob[0:NS, 0:1024], [kob], [], kob)
        for t in range(2):
            self.norm_block(memx[t * 128:(t + 1) * 128, :], 128, [(hTg[:, :, t * 128:(t + 1) * 128], "hTg")],
                            gT=gmemT, gkey="gmemT")
        for half in range(2):
            loadW(wm, half * 1024)
            for t in range(2):
                ob, kob = c.nxt("xblk")
                for ct in range(2):
                    ps, kps = c.nxt("pA")
                    for kc in range(KC):
                        c.mm(ps[:, :], hTg[:, kc, t * 128:(t + 1) * 128], W[:, kc, ct * 512:(ct + 1) * 512], kc == 0,
                             kc == KC - 1, ["hTg", "WX"], [kps])
                    _copy(c, "act", ob[:, ct * 512:(ct + 1) * 512], ps[:, :], [kps], [kob])
                if half == 0:
                    jk, kjk = c.nxt("xn")
                    for h in range(4):
                        _act(c, jk[:, h * 256:(h + 1) * 256], ob[:, h * 256:(h + 1) * 256], AF.Square, [kob], [kjk, "mst"],
                             accum_out=mst[:, h:h + 1])
                    _act(c, mst[:, 4:8], mst[:, 0:4], AF.Sqrt, ["mst", "epsc"], ["mst"], scale=1.0 / 256, bias=self.epsc[:, :])
                    _recip(c, mst[:, 8:12], mst[:, 4:8], ["mst"], ["mst"])
                    for h in range(4):
                        self._stt(ob[:, h * 256:(h + 1) * 256], mst[:, 8 + h:9 + h], gmk[:, :], [kob, "mst", "gmk"], [kob])
                c.dma((mko if half == 0 else mvo)[t * 128:(t + 1) * 128, :], ob[:, 0:1024], [kob], [], kob)
                self.mem_pack(ob, kob, half, t, self.mkT, self.mvb, "memp")

    def mem_pack(self, ob, kob, half, t, mkT, mvb, key):
        c = self.c
        if half == 1:
            _copy(c, "act", mvb[:, t, :], ob[:, 0:1024], [kob], [key])
            return
        jb, kjb = c.nxt("xn")
        _copy(c, "act", jb[:, 0:1024], ob[:, 0:1024], [kob], [kjb])
        pt, kpt = c.nxt("pT")
        for f in range(8):
            c.tr(pt[:, f, :], jb[:, f * 128:(f + 1) * 128], self.identb[:, :], [kjb, "identb"], [kpt])
        _copy(c, "act", mkT[:, :, t * 128:(t + 1) * 128], pt[:, :, :], [kpt], [key])

    def _stt(self, io, sc, in1, reads, writes):
        self.c.op("dve", lambda e: e.scalar_tensor_tensor(out=io, in0=io, scalar=sc, in1=in1, op0=ALU.mult, op1=ALU.mult),
                  reads, writes)


class ExitStackLike:
    def __init__(self, k):
        import contextlib
        self.st = contextlib.ExitStack()

    def __enter__(self):
        return self.st.__enter__()

    def __exit__(self, *a):
        return self.st.__exit__(*a)


def _barrier(P):
    last = {}
    for op in P.ops:
        last[("e", op.eng) if op.dma is None else ("d", op.dma)] = op
    deps = list(last.values())
    for e in Prog.ENGS:
        op = P.add(e, lambda eng: eng.nop(), [], [])
        for d in deps:
            if d is not op and d not in op.deps:
                op.deps.append(d)
                d.need_inc = True


def build(phases):
    nc = bass.Bass("TRN2", target_bir_lowering=False)
    stack = contextlib.ExitStack()
    k = K(nc, stack)
    c = k.c
    k.consts()
    k.QT = c.sb("QT", [128, 8, 8, 128], BF16)
    k.QTs = c.sb("QTs", [128, 8, NS], BF16)
    k.qiTs = c.sb("qiTs", [128, 8, NS], BF16)
    k.wvs = c.sb("wvs", [64, 8], F32)
    k.KTs = c.sb("KTs", [128, 4, NS], BF16)
    k.VEs = c.sb("VEs", [128, 4, 130], BF16)
    k.kiTs = c.sb("kiTs", [128, NS], BF16)
    k.mkT = c.sb("mkT", [128, 8, 256], BF16)
    k.mvb = c.sb("mvb", [128, 2, 1024], BF16)
    with _scope(k):
        k.KT = c.sb("KT", [128, 4, 4096], BF16)
        k.VE = c.sb("VE", [128, NB, 4, 130], BF16)
        k.kiT = c.sb("kiT", [128, 4096], BF16)
        k.phase1()
        _barrier(c.P)
        k.qiT = c.sb("qiT", [128, 8, TOWN], BF16)
        k.wv = c.sb("wv", [128, 8, 2, 8], F32)
        import os
        DP = int(os.environ.get("DBG_P", "9"))
        if DP >= 1:
            phase0(k)
            _barrier(c.P)
        if DP >= 2:
            phase2(k)
            _barrier(c.P)
    k.mkTs = c.sb("mkTs", [128, 8, 256], BF16)
    k.mvbs = c.sb("mvbs", [128, 2, 1024], BF16)
    if DP >= 3:
        if os.environ.get("DBG_NOS", "0") != "1":
            phase2s(k)
            _barrier(c.P)
        phase3(k)
    else:
        oad = k.dout("oa_dbg", [128, 8 * 8 * 128], BF16)
        c.dma(oad[:, :], k.QT[:, :, :, :].rearrange("p a b q -> p (a b q)"), [("QT", j) for j in range(8)], [], "dbg0")
    c.P.emit(nc, stack)
    stack.close()
    return nc


def _rope_tables(pos):
    pos = pos.astype(np.float32)
    fa = (np.float32(10000.0) ** (-np.arange(64, dtype=np.float32) / np.float32(64))).astype(np.float32)
    fi = (np.float32(10000.0) ** (-np.arange(32, dtype=np.float32) / np.float32(32))).astype(np.float32)
    da = np.arange(128) % 64
    di = np.arange(128) % 32
    anga = (pos[None, :] * fa[da][:, None]).astype(np.float32).astype(np.float64)
    angi = (pos[None, :] * fi[di][:, None]).astype(np.float32).astype(np.float64)
    return np.stack([np.cos(anga), np.sin(anga), np.cos(angi), np.sin(angi)], axis=1).astype(np.float32)


def _consts():
    cf = np.zeros((128, 4, 128), np.float32)
    cf[:, 0, :] = np.eye(128)
    cf[:, 1, :] = 1.0
    for d in range(64):
        cf[d + 64, 2, d] = -1.0
        cf[d, 2, d + 64] = 1.0
    for base in (0, 64):
        for l in range(32):
            cf[base + l + 32, 3, base + l] = -1.0
            cf[base + l, 3, base + l + 32] = 1.0
    return cf


def _block_order(cc):
    order = []
    for g in range(NG):
        order.append(4 * g + cc)
        order.extend(4 * g + r for r in range(4) if r != cc)
    return order


_NC_CACHE = {}


def kernel(**inp):
    x_prompt = np.asarray(inp["x_prompt"], np.float32)
    x_sample = np.asarray(inp["x_sample"], np.float32)
    if "nc" not in _NC_CACHE:
        _NC_CACHE["nc"] = build(None)
    nc = _NC_CACHE["nc"]
    cf = _consts()
    w_in = np.ascontiguousarray(inp["w_in"], np.float32)
    gpreT = np.ascontiguousarray(np.asarray(inp["g_pre"], np.float32).reshape(16, 128).T)
    gq = np.ascontiguousarray(np.asarray(inp["g_q"], np.float32).reshape(128, 1))
    gk = np.ascontiguousarray(np.asarray(inp["g_k"], np.float32).reshape(128, 1))
    gsgu_bc = np.ascontiguousarray(np.broadcast_to(np.asarray(inp["g_sgu"], np.float32)[None, :], (128, 1024)))
    gmk_bc = np.ascontiguousarray(np.broadcast_to(np.asarray(inp["g_mk"], np.float32)[None, :], (128, 256)))
    gmemT = np.ascontiguousarray(np.asarray(inp["g_mem"], np.float32).reshape(16, 128).T)
    w_mem_kv = np.ascontiguousarray(inp["w_mem_kv"], np.float32)
    mem_prompt = np.asarray(inp["mem_prompt"], np.float32)
    wsT_in = np.ascontiguousarray(np.transpose(np.asarray(inp["w_s"], np.float32), (2, 0, 1)))
    trimask = np.ascontiguousarray((np.arange(128)[:, None] <= np.arange(128)[None, :]).astype(np.float32))
    brow = np.ascontiguousarray(np.asarray(inp["b_s"], np.float32).reshape(1, 1024))
    gmq = np.asarray(inp["g_mq"], np.float32)
    gmq0 = np.ascontiguousarray(gmq[0:128].reshape(128, 1))
    gmq1 = np.ascontiguousarray(gmq[128:256].reshape(128, 1))
    w_pa = np.ascontiguousarray(inp["w_pa"], np.float32)
    w_pb = np.ascontiguousarray(inp["w_pb"], np.float32)
    w_pc = np.ascontiguousarray(inp["w_pc"], np.float32)
    w_out = np.ascontiguousarray(inp["w_out"], np.float32)
    cache_mem_k = np.asarray(inp["cache_mem_k"], np.float32)
    cache_mem_v = np.asarray(inp["cache_mem_v"], np.float32)
    ckf = np.ascontiguousarray(np.asarray(inp["cache_k"], np.float32).reshape(1280 * 128, 512))
    cvf = np.ascontiguousarray(np.asarray(inp["cache_v"], np.float32).reshape(1280 * 128, 512))
    ckif = np.ascontiguousarray(np.asarray(inp["cache_kidx"], np.float32).reshape(1280 * 128, 64))
    page_table = np.asarray(inp["page_table"], np.int32)
    pidx = np.arange(128, dtype=np.float32).reshape(128, 1)
    maskbs = np.where(np.arange(NS)[None, :] <= np.arange(NS)[:, None], 0.0, NEG).astype(np.float32)
    sels = np.zeros((64, NS), np.float32)
    for jj_ in range(2):
        for q_ in range(NS):
            sels[jj_ * 32 + q_, q_] = 1.0
    ckt = np.ascontiguousarray(np.broadcast_to((0.5 ** (np.arange(32) + 1)).astype(np.float32)[None, :], (128, 32)))
    in_maps = []
    orders = []
    for core in range(8):
        b, cc = core // 4, core % 4
        order = _block_order(cc)
        orders.append(order)
        xall = np.ascontiguousarray(x_prompt[b].reshape(NB, 128, D)[order].reshape(4096, D))
        pos = (np.asarray(order)[:, None] * 128 + np.arange(128)[None, :]).reshape(NG, 512)
        rope = np.zeros((9, 128, 4, 512), np.float32)
        for g in range(NG):
            rope[g] = _rope_tables(pos[g])
        rope[8, :, :, 0:NS] = _rope_tables(16384 + np.arange(NS))
        rope_own = np.zeros((3, 128, 4, 512), np.float32)
        own_pos = ((4 * np.arange(8) + cc)[:, None] * 128 + np.arange(128)[None, :]).reshape(2, 512)
        rope_own[0] = _rope_tables(own_pos[0])
        rope_own[1] = _rope_tables(own_pos[1])
        rope_own[2, :, :, 0:NS] = _rope_tables(16384 + np.arange(NS))
        maskb = np.zeros((128, 512), np.float32)
        tq = np.arange(128)
        maskb[:, 0:128] = np.where(tq[None, :] <= tq[:, None], 0.0, NEG)
        for sl_ in range(1, 4):
            maskb[:, sl_ * 128:(sl_ + 1) * 128] = 0.0 if sl_ <= cc else NEG
        sel = np.zeros((128, 2, 128), np.float32)
        for jj_ in range(2):
            for q_ in range(64):
                for qh_ in range(2):
                    sel[jj_ * 64 + q_, qh_, qh_ * 64 + q_] = 1.0
        in_maps.append({"ckt": ckt, "ck": ckf, "cv": cvf, "cki": ckif, "pidx": pidx, "maskbs": maskbs, "sels": sels,
                        "ptab_bc": np.ascontiguousarray(np.broadcast_to(page_table[core][None, :], (128, NPG))),
                        "wsT_in": wsT_in, "trimask": trimask, "brow": brow, "gmq0": gmq0, "gmq1": gmq1, "w_pa": w_pa,
                        "w_pb": w_pb, "w_pc": w_pc, "w_out": w_out,
                        "cmk": np.ascontiguousarray(cache_mem_k[core].reshape(256, 1024)),
                        "cmv": np.ascontiguousarray(cache_mem_v[core].reshape(256, 1024)),
                        "rope_own": rope_own, "maskb": maskb, "sel": sel, "cf32": cf, "gpreT": gpreT, "gq": gq, "gk": gk, "xall": xall, "xs": np.ascontiguousarray(x_sample[core]),
                        "w_in": w_in, "rope": rope, "gsgu_bc": gsgu_bc, "gmk_bc": gmk_bc, "gmemT": gmemT,
                        "w_mem_kv": w_mem_kv, "memx": np.ascontiguousarray(mem_prompt[b])})
    res = run_bass_kernel_spmd(nc, in_maps, core_ids=list(range(8)))
    R = res.results
    k_p = np.zeros((2, 4096, 4, 128), np.float32)
    v_p = np.zeros((2, 4096, 4, 128), np.float32)
    ki_p = np.zeros((2, 4096, 64), np.float32)
    for core in range(8):
        b, cc = core // 4, core % 4
        for g in range(NG):
            blk = 4 * g + cc
            k_p[b, blk * 128:(blk + 1) * 128] = R[core]["k_out"][g * 128:(g + 1) * 128]
            v_p[b, blk * 128:(blk + 1) * 128] = R[core]["v_out"][g * 128:(g + 1) * 128]
            ki_p[b, blk * 128:(blk + 1) * 128] = R[core]["ki_out"][g * 128:(g + 1) * 128]
    k_s = np.stack([R[i]["ks_out"] for i in range(8)])
    v_s = np.stack([R[i]["vs_out"] for i in range(8)])
    ki_s = np.stack([R[i]["kis_out"] for i in range(8)])
    cv_s = np.stack([R[i]["cvs_out"].reshape(NS, 8, 128) for i in range(8)])
    mk_p = np.stack([R[4 * b]["mk_out"].reshape(256, 4, 256) for b in range(2)])
    mv_p = np.stack([R[4 * b]["mv_out"].reshape(256, 4, 256) for b in range(2)])
    _NC_CACHE["dbg"] = R
    y_p = np.zeros((2, 4096, D), np.float32)
    y_s = np.zeros((8, NS, D), np.float32)
    if "y_out" in R[0]:
        for core in range(8):
            b, cc = core // 4, core % 4
            for g in range(NG):
                blk = 4 * g + cc
                y_p[b, blk * 128:(blk + 1) * 128] = R[core]["y_out"][g * 128:(g + 1) * 128]
            y_s[core] = R[core]["ys_out"]
    return (y_p, y_s, k_p, v_p, ki_p, mk_p, mv_p, k_s, v_s, ki_s, cv_s)


import contextlib


@contextlib.contextmanager
def _scope(k):
    st = contextlib.ExitStack()
    old = k.c.stack
    k.c.stack = st
    try:
        with st:
            yield st
    finally:
        k.c.stack = old


def _loadW(k, pool, src_view, col0, ncols, nkc=16):
    c = k.c
    Wt, kW = c.nxt(pool)
    keys = []
    for kc4 in range(nkc // 4):
        sl = slice(kc4 * 4, kc4 * 4 + 4)
        key = (kW, kc4)
        c.dma(Wt[:, sl, 0:ncols], src_view[:, sl, col0:col0 + ncols], [], [key], key, q="pool")
        keys.append(key)
    return Wt, keys


def _hT_own(k, hT, hTs):
    xall, xs = k.dr["xall"], k.dr["xs"]
    for j in range(8):
        k.norm_block(xall[(4 * j) * 128:(4 * j + 1) * 128, :], 128, [(hT[:, :, j * 128:(j + 1) * 128], ("hTo", j))])
    k.norm_block(xs[:, :], NS, [(hTs[:, :, :], "hTs")])


def _chunks(hT, hTs):
    return [(hT[:, :, 0:512], 512, [("hTo", j) for j in range(4)]),
            (hT[:, :, 512:1024], 512, [("hTo", j) for j in range(4, 8)]),
            (hTs[:, :, :], NS, ["hTs"])]


def phase0(k):
    c = k.c
    w_in = k.dr["w_in"].rearrange("(kc p) n -> p kc n", p=128)
    ropeo = k.din("rope_own", [3, 128, 4, 512])
    with _scope(k):
        hT = c.sb("hT", [128, KC, TOWN], BF16)
        hTs = c.sb("hTs", [128, KC, NS], BF16)
        c.pool("ss", 4, [128, 4], F32)
        c.pool("pT", 2, [128, 8, 128], BF16, psum=True)
        with _scope(k):
            c.pool("xblk", 2, [128, D], F32)
            c.pool("xn", 1, [128, D], BF16)
            _hT_own(k, hT, hTs)
        _barrier(c.P)
        c.pool("Wt", 1, [128, KC, 512], BF16)
        c.pool("ropeT", 2, [128, 4, 512], F32)
        for n in ("f_kf", "f_sq", "f_rs", "f_kn", "f_t1", "f_t2", "f_kr"):
            c.pool(n, 1, [128, 512], F32)
        Wwi = c.sb("Wwi", [128, KC, 16], BF16)
        c.pool("hdup", 1, [128, KC, 128], BF16)
        c.pool("pA", 3, [128, 512], F32, psum=True)
        c.pool("pB", 3, [128, 512], F32, psum=True)
        chunks = _chunks(hT, hTs)
        pipe = Pipe3()
        for wt in range(4):
            isq = wt < 2
            Wt, wkeys = _loadW(k, "Wt", w_in, (C_AQ if isq else C_AQI) + (wt % 2) * 512, 512)
            for ch, (hv, nt, hkeys) in enumerate(chunks):
                rt, krt = c.nxt("ropeT")
                c.dma(rt[:], ropeo[ch], [], [krt], krt)
                for hh in range(4):
                    head = (wt % 2) * 4 + hh

                    def fa(hh=hh, Wt=Wt, wkeys=wkeys, hv=hv, nt=nt, hkeys=hkeys, rt=rt, krt=krt, isq=isq):
                        ps, kps = c.nxt("pA")
                        for kc in range(KC):
                            c.mm(ps[:, 0:nt], Wt[:, kc, hh * 128:(hh + 1) * 128], hv[:, kc, :], kc == 0, kc == KC - 1,
                                 hkeys + wkeys, [kps])
                        if isq:
                            return k.rope_state(ps[:, 0:nt], kps, nt, k.gq[:, 0:1], "gq", k.R128, rt[:, 0, 0:nt],
                                                rt[:, 1, 0:nt], krt, 1.0 / 128)
                        return k.rope_state(ps[:, 0:nt], kps, nt, None, None, k.R64, rt[:, 2, 0:nt], rt[:, 3, 0:nt],
                                            krt, 0.0, normalize=False)

                    def fc(S, head=head, ch=ch, isq=isq):
                        kr, kkr = k.rope_C(S)
                        if ch < 2:
                            if isq:
                                _copy(c, "act", k.QT[:, 4 * ch:4 * ch + 4, head, :], kr[:, 0:512].rearrange("p (j q) -> p j q", j=4),
                                      [kkr], [("QT", 4 * ch + i) for i in range(4)])
                            else:
                                _copy(c, "act", k.qiT[:, head, ch * 512:(ch + 1) * 512], kr[:, 0:512], [kkr], [("qiT", ch)])
                        else:
                            if isq:
                                _copy(c, "act", k.QTs[:, head, :], kr[:, 0:NS], [kkr], ["QTs"])
                            else:
                                _copy(c, "act", k.qiTs[:, head, :], kr[:, 0:NS], [kkr], ["qiTs"])
                    pipe.push(fa, k.rope_B, fc)
        pipe.flush()
        for kc4 in range(4):
            sl = slice(kc4 * 4, kc4 * 4 + 4)
            c.dma(Wwi[:, sl, :], w_in[:, sl, C_AWI:C_AWI + 16], [], [("Wwi", kc4)], ("Wwi", kc4), q="pool")
        wwk = [("Wwi", i) for i in range(4)]
        for j in range(8):
            for qh in range(2):
                hd, khd = c.nxt("hdup")
                src = hT[:, :, j * 128 + qh * 64:j * 128 + qh * 64 + 64]
                _copy(c, "pool", hd[:, :, 0:64], src, [("hTo", j)], [(khd, 0)])
                _copy(c, "pool", hd[:, :, 64:128], src, [("hTo", j)], [(khd, 1)])
                ps, kps = c.nxt("pB")
                for kc in range(KC):
                    c.mm(ps[:, 0:16], hd[:, kc, :], Wwi[:, kc, :], kc == 0, kc == KC - 1, [(khd, 0), (khd, 1)] + wwk, [kps])
                pv = ps[:, 0:16].rearrange("p (c two) -> p c two", two=2)
                _act(c, k.wv[0:64, j, qh, :], pv[0:64, :, 0], AF.Copy, [kps], [("wv", j, qh, 0)], scale=1.0 / 32)
                _act(c, k.wv[64:128, j, qh, :], pv[64:128, :, 1], AF.Copy, [kps], [("wv", j, qh, 1)], scale=1.0 / 32)
        hd, khd = c.nxt("hdup")
        c.op("pool", lambda e, hd=hd: e.memset(hd[:, :, :], 0.0), [], [(khd, 0), (khd, 1)])
        c.op("pool", lambda e: e.memset(k.wvs[:, :], 0.0), [], ["wvs"])
        _copy(c, "pool", hd[:, :, 0:NS], hTs[:, :, :], ["hTs"], [(khd, 0)])
        _copy(c, "pool", hd[:, :, 32:32 + NS], hTs[:, :, :], ["hTs"], [(khd, 1)])
        ps, kps = c.nxt("pB")
        for kc in range(KC):
            c.mm(ps[0:64, 0:16], hd[:, kc, 0:64], Wwi[:, kc, :], kc == 0, kc == KC - 1, [(khd, 0), (khd, 1)] + wwk, [kps])
        pv = ps[:, 0:16].rearrange("p (c two) -> p c two", two=2)
        _act(c, k.wvs[0:NS, :], pv[0:NS, :, 0], AF.Copy, [kps, "wvs"], ["wvs"], scale=1.0 / 32)
        _act(c, k.wvs[32:32 + NS, :], pv[32:32 + NS, :, 1], AF.Copy, [kps, "wvs"], ["wvs"], scale=1.0 / 32)


NIT = 18


def _threshold(k, sc, nk, st, junk, hwt, ksc="sc", kjunk="junk"):
    c = k.c
    _tt(c, "dve", st[:, 2:3], st[:, 1:2], st[:, 0:1], ALU.subtract, ["thr"], ["thr"])
    _ts(c, "dve", hwt[:, 0:NIT], k.ckt[0:hwt.shape[0], 0:NIT], st[:, 2:3], None, ALU.mult, ALU.bypass, ["thr", "ckt"], ["thr"])
    for it in range(NIT):
        _tt(c, "dve", st[:, 3:4], st[:, 0:1], hwt[:, it:it + 1], ALU.add, ["thr"], ["thr"])
        _ts(c, "dve", junk, sc, st[:, 3:4], None, ALU.is_ge, ALU.add, ["thr", ksc], [kjunk, "thr"], accum_out=st[:, 4:5])
        _ts(c, "dve", st[:, 5:6], st[:, 4:5], 255.5, hwt[:, it:it + 1], ALU.is_ge, ALU.mult, ["thr"], ["thr"])
        _tt(c, "dve", st[:, 0:1], st[:, 0:1], st[:, 5:6], ALU.add, ["thr"], ["thr"])


def _att_s1(k, nq, KTg, ktkeys, nkeys, qrhs, qkeys, MTap, mtkey, gp, mul_eng):
    c = k.c
    w2 = 2 * nq
    pst, kpst = c.nxt("pD")
    for gi in range(2):
        g = 2 * gp + gi
        c.mm(pst[0:nkeys, gi * w2:(gi + 1) * w2], KTg[g], qrhs[g], True, True, ktkeys + qkeys, [kpst])
    ex, kex = c.nxt("ex")
    _act(c, ex[0:nkeys, 0:2 * w2], pst[0:nkeys, 0:2 * w2], AF.Exp, [kpst], [kex], scale=float(128 ** -0.5))
    pm, kpm = c.nxt("pm")
    _tt(c, mul_eng, pm[0:nkeys, :, 0:nq], ex[0:nkeys, 0:2 * w2].rearrange("p (a q) -> p a q", a=4),
        MTap.unsqueeze(1).to_broadcast([nkeys, 4, nq]), ALU.mult, [kex, mtkey], [kpm])
    return pm, kpm


def _att_s2(k, nq, pm, kpm, VEg, vekeys, nkeys, gp, po, kpo, first, last):
    c = k.c
    for a in range(4):
        hd = 4 * gp + a
        g = hd // 2
        b3, off = hd // 3, (hd % 3) * 130
        c.mm(po[b3][0:nq, off:off + 130], pm[0:nkeys, a, 0:nq], VEg[g], first and hd % 3 == 0, last, [kpm] + vekeys, [kpo[b3]])


def _att_pipeline(k, steps):
    import os
    n = len(steps)
    sk = int(os.environ.get("PIPE_ATT", "2"))
    st = {}
    for i in range(min(sk, n)):
        st[i] = steps[i][0]()
    for i in range(n):
        if sk == 0:
            st[i] = steps[i][0]()
        steps[i][1](st.pop(i))
        if sk > 0 and i + sk < n:
            st[i + sk] = steps[i + sk][0]()


def _attn_finish(k, nq, po, kpo, oat, rden, dst, dkeys):
    c = k.c
    for hd in range(8):
        b3, off = hd // 3, (hd % 3) * 130
        _recip(c, rden[0:nq, hd:hd + 1], po[b3][0:nq, off + 128:off + 129], [kpo[b3]], ["rden"])
    for hd in range(8):
        b3, off = hd // 3, (hd % 3) * 130
        _act(c, oat[0:nq, hd, :], po[b3][0:nq, off:off + 128], AF.Copy, [kpo[b3], "rden"], [("oat", hd)],
             scale=rden[0:nq, hd:hd + 1])
    pt, kpt = c.nxt("pT")
    for hd in range(8):
        c.tr(pt[:, hd, 0:nq], oat[0:nq, hd, :], k.identb[0:nq, 0:nq], [("oat", hd), "identb"], [kpt])
    _copy(c, "act", dst, pt[:, :, 0:nq], [kpt], dkeys)


def phase2_old(k):
    c = k.c
    with _scope(k):
        sc = c.sb("sc", [128, 4096], F32)
        junk = c.sb("junk", [128, 4096], BF16)
        M = c.sb("M", [128, 4096], BF16)
        MT = c.sb("MT", [128, NB, 128], BF16)
        st = c.sb("thr", [128, 16], F32)
        hwt = c.sb("hwt", [128, 32], F32)
        oat = c.sb("oat", [128, 8, 128], BF16)
        rden = c.sb("rden", [128, 8], F32)
        maskb = c.sb("maskb", [128, 512], F32)
        sel = c.sb("sel", [128, 2, 128], BF16)
        c.dma(maskb[:], k.din("maskb", [128, 512])[:, :], [], ["maskb"], "c20")
        c.dma(sel[:], k.din("sel", [128, 2, 128])[:, :, :], [], ["sel"], "c21", q="pool")
        c.pool("BD", 2, [128, 2, 8, 128], BF16)
        c.pool("Wsel", 2, [128, 2, 8, 128], BF16)
        c.pool("rl", 4, [128, 512], BF16)
        c.pool("ex", 3, [128, 512], BF16)
        c.pool("pm", 3, [128, 4, 128], BF16)
        c.pool("pD", 3, [128, 512], F32, psum=True)
        c.pool("pSC", 1, [128, 512], F32, psum=True)
        c.pool("pT", 1, [128, 8, 128], BF16, psum=True)
        c.pool("po", 3, [128, 512], F32, psum=True)
        for i in range(2):
            bd = c.rot["BD"][0][i]
            c.op("pool", (lambda b: (lambda e: e.memset(b[:, :, :, :], 0.0)))(bd), [],
                 [(("BD", i), cc, jj) for cc in range(8) for jj in range(2)])
        import os
        NJ = int(os.environ.get("DBG_NJ", "8"))
        L2 = int(os.environ.get("DBG_L2", "9"))
        for j in range(NJ):
            nk = (j + 1) * 512
            nkb = 4 * (j + 1)
            BD, kBD = c.nxt("BD")
            Ws, kWs = c.nxt("Wsel")
            for cc in range(8):
                for jj in range(2):
                    rows = slice(jj * 64, (jj + 1) * 64)
                    _copy(c, "pool", BD[rows, :, cc, jj * 64:(jj + 1) * 64],
                          k.qiT[rows, cc, j * 128:(j + 1) * 128].rearrange("p (qh q) -> p qh q", qh=2),
                          [("qiT", j // 4)], [(kBD, cc, jj)])
            for qh in range(2):
                for cc in range(8):
                    _ts(c, "dve", Ws[:, qh, cc, :], sel[:, qh, :], k.wv[:, j, qh, cc:cc + 1], None, ALU.mult, ALU.bypass,
                        ["sel", ("wv", j, qh, 0), ("wv", j, qh, 1)], [(kWs, qh, cc)])
            if L2 < 2:
                continue
            for k5 in range(j + 1):
                psc, kpsc = c.nxt("pSC")
                combos = [(qh, cc) for qh in range(2) for cc in range(8)]
                rls = {}

                def stage0(i, k5=k5):
                    qh, cc = combos[i]
                    pd, kpd = c.nxt("pD")
                    c.mm(pd[:, :], BD[:, qh, cc, :], k.kiT[:, k5 * 512:(k5 + 1) * 512], True, True,
                         [(kBD, cc, 0), (kBD, cc, 1), ("kiT", k5)], [kpd])
                    rl, krl = c.nxt("rl")
                    if i % 2 == 0:
                        _act(c, rl[:, :], pd[:, :], AF.Relu, [kpd], [krl])
                    else:
                        _ts(c, "dve", rl[:, :], pd[:, :], 0.0, None, ALU.max, ALU.bypass, [kpd], [krl])
                    rls[i] = (rl, krl)

                ski = int(os.environ.get("PIPE_IDX", "2"))
                for i in range(ski):
                    stage0(i)
                for i in range(16):
                    if ski == 0:
                        stage0(i)
                    qh, cc = combos[i]
                    rl, krl = rls.pop(i)
                    c.mm(psc[:, :], Ws[:, qh, cc, :], rl[:, :], i == 0, i == 15, [krl, (kWs, qh, cc)], [kpsc])
                    if ski > 0 and i + ski < 16:
                        stage0(i + ski)
                if k5 == j:
                    c.op("dve", lambda e, psc=psc: e.tensor_reduce(out=st[:, 8:9], in_=psc[:, :], axis=AX.X, op=ALU.min),
                         [kpsc], ["thr"])
                    _tt(c, "dve", sc[:, k5 * 512:(k5 + 1) * 512], psc[:, :], maskb[:, :], ALU.add, [kpsc, "maskb"], ["sc"])
                else:
                    _copy(c, "act", sc[:, k5 * 512:(k5 + 1) * 512], psc[:, :], [kpsc], ["sc"])
            if L2 < 3:
                continue
            if j > 0:
                c.op("dve", lambda e, j=j: e.tensor_reduce(out=st[:, 7:8], in_=sc[:, 0:j * 512], axis=AX.X, op=ALU.min),
                     ["sc"], ["thr"])
                _tt(c, "dve", st[:, 0:1], st[:, 7:8], st[:, 8:9], ALU.min, ["thr"], ["thr"])
            else:
                _copy(c, "dve", st[:, 0:1], st[:, 8:9], ["thr"], ["thr"])
            c.op("dve", lambda e, nk=nk: e.tensor_reduce(out=st[:, 1:2], in_=sc[:, 0:nk], axis=AX.X, op=ALU.max), ["sc"], ["thr"])
            _threshold(k, sc[:, 0:nk], nk, st, junk[:, 0:nk], hwt)
            if L2 < 4:
                continue
            _ts(c, "dve", M[:, 0:nk], sc[:, 0:nk], st[:, 0:1], None, ALU.is_ge, ALU.bypass, ["thr", "sc"], ["M"])
            for kb8 in range(0, nkb, 8):
                n8 = min(8, nkb - kb8)
                pt, kpt = c.nxt("pT")
                for i in range(n8):
                    c.tr(pt[:, i, :], M[:, (kb8 + i) * 128:(kb8 + i + 1) * 128], k.identb[:, :], ["M", "identb"], [kpt])
                _copy(c, "act", MT[:, kb8:kb8 + n8, :], pt[:, 0:n8, :], [kpt], ["MT"])
            if L2 < 5:
                continue
            po, kpo = [], []
            for i in range(3):
                a, b = c.nxt("po")
                po.append(a)
                kpo.append(b)
            qrhs = [k.QT[:, j, 2 * g:2 * g + 2, :].rearrange("p h q -> p (h q)") for g in range(4)]
            steps = []
            for kb in range(nkb):
                for gp in range(2):
                    def s1(kb=kb, gp=gp):
                        KTg = [k.KT[:, g, kb * 128:(kb + 1) * 128] for g in range(4)]
                        return _att_s1(k, 128, KTg, [("KT", g, kb // 4) for g in range(4)], 128, qrhs, [("QT", j)],
                                       MT[:, kb, :], "MT", gp, "dve" if gp == 0 else "pool")

                    def s2(state, kb=kb, gp=gp):
                        VEg = [k.VE[:, kb, g, :] for g in range(4)]
                        _att_s2(k, 128, state[0], state[1], VEg, [("VE", kb), "VE1"], 128, gp, po, kpo, kb == 0, kb == nkb - 1)
                    steps.append((s1, s2))
            _att_pipeline(k, steps)
            if L2 < 6:
                continue
            _attn_finish(k, 128, po, kpo, oat, rden, k.QT[:, j, :, :], [("QT", j)])
            if os.environ.get("DBG_BAR", "0") == "1":
                _barrier(c.P)
        if NJ == 1:
            d1 = k.dout("dbg_sc", [128, 512])
            d2 = k.dout("dbg_st", [128, 16])
            d3 = k.dout("dbg_M", [128, 512], BF16)
            d4 = k.dout("dbg_wv", [128, 128])
            d5 = k.dout("dbg_qi", [128, 8, 128], BF16)
            d6 = k.dout("dbg_MT", [128, 4, 128], BF16)
            c.dma(d1[:, :], sc[:, 0:512], ["sc"], [], "dbg1")
            c.dma(d2[:, :], st[:, :], ["thr"], [], "dbg2")
            c.dma(d3[:, :], M[:, 0:512], ["M"], [], "dbg3")
            c.dma(d4[:, :], k.wv[:, :, :, :].rearrange("p a b c -> p (a b c)"), [], [], "dbg4")
            c.dma(d5[:, :, :], k.qiT[:, :, 0:128], [], [], "dbg5")
            c.dma(d6[:, :, :], MT[:, 0:4, :], ["MT"], [], "dbg6")


def phase2(k):
    c = k.c
    with _scope(k):
        scb = [c.sb("sc0", [128, 4096], F32), c.sb("sc1", [128, 4096], F32)]
        M = c.sb("M", [128, 4096], BF16)
        MT = c.sb("MT", [128, NB, 128], BF16)
        st = c.sb("thr", [128, 16], F32)
        hwt = c.sb("hwt", [128, 32], F32)
        oat = c.sb("oat", [128, 8, 128], BF16)
        rden = c.sb("rden", [128, 8], F32)
        maskb = c.sb("maskb", [128, 512], F32)
        sel = c.sb("sel", [128, 2, 128], BF16)
        c.dma(maskb[:], k.din("maskb", [128, 512])[:, :], [], ["maskb"], "c20")
        c.dma(sel[:], k.din("sel", [128, 2, 128])[:, :, :], [], ["sel"], "c21", q="pool")
        c.pool("BD", 2, [128, 2, 8, 128], BF16)
        c.pool("Wsel", 2, [128, 2, 8, 128], BF16)
        c.pool("rl", 4, [128, 512], BF16)
        c.pool("ex", 3, [128, 512], BF16)
        c.pool("pm", 3, [128, 4, 128], BF16)
        c.pool("pD", 3, [128, 512], F32, psum=True)
        c.pool("pSC", 1, [128, 512], F32, psum=True)
        c.pool("pT", 1, [128, 8, 128], BF16, psum=True)
        c.pool("po", 3, [128, 512], F32, psum=True)
        for i in range(2):
            bd = c.rot["BD"][0][i]
            c.op("pool", (lambda b: (lambda e: e.memset(b[:, :, :, :], 0.0)))(bd), [],
                 [(("BD", i), cc, jj) for cc in range(8) for jj in range(2)])
        combos = [(qh, cc) for qh in range(2) for cc in range(8)]

        def idx(j):
            sc, ksc = scb[j % 2], ("sc", j % 2)
            BD, kBD = c.nxt("BD")
            Ws, kWs = c.nxt("Wsel")
            for cc in range(8):
                for jj in range(2):
                    rows = slice(jj * 64, (jj + 1) * 64)
                    _copy(c, "pool", BD[rows, :, cc, jj * 64:(jj + 1) * 64],
                          k.qiT[rows, cc, j * 128:(j + 1) * 128].rearrange("p (qh q) -> p qh q", qh=2),
                          [("qiT", j // 4)], [(kBD, cc, jj)])
            for qh in range(2):
                for cc in range(8):
                    _ts(c, "dve", Ws[:, qh, cc, :], sel[:, qh, :], k.wv[:, j, qh, cc:cc + 1], None, ALU.mult, ALU.bypass,
                        ["sel", ("wv", j, qh, 0), ("wv", j, qh, 1)], [(kWs, qh, cc)])
            for k5 in range(j + 1):
                psc, kpsc = c.nxt("pSC")
                rls = {}

                def stage0(i):
                    qh, cc = combos[i]
                    pd, kpd = c.nxt("pD")
                    c.mm(pd[:, :], BD[:, qh, cc, :], k.kiT[:, k5 * 512:(k5 + 1) * 512], True, True,
                         [(kBD, cc, 0), (kBD, cc, 1), ("kiT", k5)], [kpd])
                    rl, krl = c.nxt("rl")
                    _act(c, rl[:, :], pd[:, :], AF.Relu, [kpd], [krl])
                    rls[i] = (rl, krl)

                stage0(0)
                stage0(1)
                for i in range(16):
                    qh, cc = combos[i]
                    rl, krl = rls.pop(i)
                    c.mm(psc[:, :], Ws[:, qh, cc, :], rl[:, :], i == 0, i == 15, [krl, (kWs, qh, cc)], [kpsc])
                    if i + 2 < 16:
                        stage0(i + 2)
                _copy(c, "act", sc[:, k5 * 512:(k5 + 1) * 512], psc[:, :], [kpsc], [ksc])

        def thr_att(j):
            sc, ksc = scb[j % 2], ("sc", j % 2)
            nk = (j + 1) * 512
            nkb = 4 * (j + 1)
            c.op("dve", lambda e: e.tensor_reduce(out=st[:, 0:1], in_=sc[:, 0:nk], axis=AX.X, op=ALU.min), [ksc], ["thr"])
            _tt(c, "dve", sc[:, j * 512:(j + 1) * 512], sc[:, j * 512:(j + 1) * 512], maskb[:, :], ALU.add, [ksc, "maskb"], [ksc])
            c.op("dve", lambda e: e.tensor_reduce(out=st[:, 1:2], in_=sc[:, 0:nk], axis=AX.X, op=ALU.max), [ksc], ["thr"])
            _threshold(k, sc[:, 0:nk], nk, st, M[:, 0:nk], hwt, ksc=ksc, kjunk="M")
            _ts(c, "dve", M[:, 0:nk], sc[:, 0:nk], st[:, 0:1], None, ALU.is_ge, ALU.bypass, ["thr", ksc], ["M"])
            for kb8 in range(0, nkb, 8):
                n8 = min(8, nkb - kb8)
                pt, kpt = c.nxt("pT")
                for i in range(n8):
                    c.tr(pt[:, i, :], M[:, (kb8 + i) * 128:(kb8 + i + 1) * 128], k.identb[:, :], ["M", "identb"], [kpt])
                _copy(c, "act", MT[:, kb8:kb8 + n8, :], pt[:, 0:n8, :], [kpt], ["MT"])
            po, kpo = [], []
            for i in range(3):
                a, b = c.nxt("po")
                po.append(a)
                kpo.append(b)
            qrhs = [k.QT[:, j, 2 * g:2 * g + 2, :].rearrange("p h q -> p (h q)") for g in range(4)]
            steps = []
            for kb in range(nkb):
                for gp in range(2):
                    def s1(kb=kb, gp=gp):
                        KTg = [k.KT[:, g, kb * 128:(kb + 1) * 128] for g in range(4)]
                        return _att_s1(k, 128, KTg, [("KT", g, kb // 4) for g in range(4)], 128, qrhs, [("QT", j)],
                                       MT[:, kb, :], "MT", gp, "dve" if gp == 0 else "pool")

                    def s2(state, kb=kb, gp=gp):
                        VEg = [k.VE[:, kb, g, :] for g in range(4)]
                        _att_s2(k, 128, state[0], state[1], VEg, [("VE", kb), "VE1"], 128, gp, po, kpo, kb == 0, kb == nkb - 1)
                    steps.append((s1, s2))
            _att_pipeline(k, steps)
            _attn_finish(k, 128, po, kpo, oat, rden, k.QT[:, j, :, :], [("QT", j)])

        idx(0)
        for j in range(8):
            if j + 1 < 8:
                idx(j + 1)
            thr_att(j)


def _stt(c, out, in0, scalar, in1, op0, op1, reads, writes):
    c.op("dve", lambda e: e.scalar_tensor_tensor(out=out, in0=in0, scalar=scalar, in1=in1, op0=op0, op1=op1), reads, writes)


def _proj_fm(k, Wt, wkeys, col0, hv, nt, hkeys, pool="pA", nkc=KC):
    c = k.c
    ps, kps = c.nxt(pool)
    for kc in range(nkc):
        c.mm(ps[:, 0:nt], Wt[:, kc, col0:col0 + 128], hv[:, kc, :], kc == 0, kc == nkc - 1, hkeys + wkeys, [kps])
    return ps, kps


def _tok(ch):
    return (slice(ch * 512, (ch + 1) * 512), 512) if ch < 2 else (slice(1024, 1024 + NS), NS)


def phase3(k):
    c = k.c
    w_in = k.dr["w_in"].rearrange("(kc p) n -> p kc n", p=128)
    with _scope(k):
        hT = c.sb("hT", [128, KC, TOWN], BF16)
        hTs = c.sb("hTs", [128, KC, NS], BF16)
        AT = c.sb("AT", [128, 8, TOWN + NS], BF16)
        BT = c.sb("BT", [128, 8, TOWN + NS], BF16)
        CT = c.sb("CT", [128, 8, TOWN + NS], BF16)
        c.pool("ss", 4, [128, 4], F32)
        c.pool("pT", 2, [128, 8, 128], BF16, psum=True)
        c.pool("pA", 3, [128, 512], F32, psum=True)
        c.pool("pB", 3, [128, 512], F32, psum=True)
        c.pool("t1", 2, [128, 512], F32)
        c.pool("t2", 2, [128, 512], F32)
        with _scope(k):
            c.pool("xblk", 2, [128, D], F32)
            c.pool("xn", 1, [128, D], BF16)
            _hT_own(k, hT, hTs)
        _barrier(c.P)
        chunks = _chunks(hT, hTs)
        with _scope(k):
            c.pool("Wt", 2, [128, KC, 512], BF16)
            sT = c.sb("sT", [128, 8, TOWN + NS], BF16)
            cqT = sT
            c.pool("vn", 2, [128, 1024], BF16)
            wsf = c.sb("wsf", [128, 8, 128], F32)
            wsT = c.sb("wsT", [128, 8, 128], BF16)
            trim = c.sb("trim", [128, 128], F32)
            brow = c.sb("brow", [1, 1024], F32)
            gsgu = c.sb("gsgu3", [128, 1024], F32)
            gmqT = c.sb("gmqT", [128, 2], F32)
            onesb = c.sb("onesb", [128, 128], BF16)
            jk = c.sb("jk3", [128, 512], BF16)
            c.pool("fa", 1, [128, 512], F32)
            c.pool("fb", 1, [128, 512], F32)
            c.pool("pe", 4, [128, 512], BF16)
            c.dma(wsf[:], k.din("wsT_in", [128, 8, 128])[:, :, :], [], ["wsf"], "c30")
            c.dma(trim[:], k.din("trimask", [128, 128])[:, :], [], ["trim"], "c31")
            c.dma(brow[:], k.din("brow", [1, 1024])[:, :], [], ["brow"], "c32")
            c.dma(gsgu[:], k.dr["gsgu_bc"][:, :], [], ["gsgu3"], "c33")
            c.dma(gmqT[:, 0:1], k.din("gmq0", [128, 1])[:, :], [], ["gmq0"], "c34")
            c.dma(gmqT[:, 1:2], k.din("gmq1", [128, 1])[:, :], [], ["gmq1"], "c35")
            _copy(c, "pool", onesb[:], k.onesf[:], ["onesf"], ["onesb"])
            _tt(c, "dve", wsT[:, :, :], wsf[:, :, :], trim[:, :].unsqueeze(1).to_broadcast([128, 8, 128]), ALU.mult,
                ["wsf", "trim"], ["wsT"])
            for wt in range(2):
                Wt, wkeys = _loadW(k, "Wt", w_in, C_AG + wt * 512, 512)
                for ch, (hv, nt, hkeys) in enumerate(chunks):
                    ts_, _ = _tok(ch)
                    for hh in range(4):
                        f = wt * 4 + hh
                        ps, kps = _proj_fm(k, Wt, wkeys, hh * 128, hv, nt, hkeys)
                        t1, kt1 = c.nxt("t1")
                        _act(c, t1[:, 0:nt], ps[:, 0:nt], AF.Silu, [kps], [kt1])
                        if ch < 2:
                            _tt(c, "dve", AT[:, f, ts_].rearrange("p (j q) -> p j q", j=4),
                                t1[:, 0:512].rearrange("p (j q) -> p j q", j=4), k.QT[:, 4 * ch:4 * ch + 4, f, :], ALU.mult,
                                [kt1] + [("QT", 4 * ch + i) for i in range(4)], [("AT", f, ch)])
                        else:
                            _tt(c, "dve", AT[:, f, ts_], t1[:, 0:NS], k.QTs[:, f, :], ALU.mult, [kt1, "QTs"], [("AT", f, ch)])
            W0, wk0 = _loadW(k, "Wt", w_in, C_BV, 512)
            W1, wk1 = _loadW(k, "Wt", w_in, C_BV + 512, 512)
            for blk in range(9):
                samp = blk == 8
                nt = NS if samp else 128
                lh = hTs if samp else hT[:, :, blk * 128:(blk + 1) * 128]
                hk = ["hTs"] if samp else [("hTo", blk)]
                tsl = slice(1024, 1024 + NS) if samp else slice(blk * 128, (blk + 1) * 128)
                ss, kss = c.nxt("ss")
                pss = []
                for half, (Wh, wkh) in enumerate(((W0, wk0), (W1, wk1))):
                    ps, kps = c.nxt("pA")
                    for kc in range(KC):
                        c.mm(ps[0:nt, :], lh[:, kc, :], Wh[:, kc, :], kc == 0, kc == KC - 1, hk + wkh, [kps])
                    _act(c, jk[0:nt, :], ps[0:nt, :], AF.Square, [kps], ["jk3", (kss, half)], accum_out=ss[0:nt, half:half + 1])
                    pss.append((ps, kps))
                _tt(c, "dve", ss[0:nt, 2:3], ss[0:nt, 0:1], ss[0:nt, 1:2], ALU.add, [(kss, 0), (kss, 1)], [(kss, 2)])
                _act(c, ss[0:nt, 2:3], ss[0:nt, 2:3], AF.Sqrt, [(kss, 2), "epsc"], [(kss, 2)], scale=1.0 / 1024, bias=k.epsc[0:nt, :])
                _recip(c, ss[0:nt, 3:4], ss[0:nt, 2:3], [(kss, 2)], [(kss, 3)])
                vn, kvn = c.nxt("vn")
                for half, (ps, kps) in enumerate(pss):
                    _stt(c, vn[0:nt, half * 512:(half + 1) * 512], ps[0:nt, :], ss[0:nt, 3:4], gsgu[0:nt, half * 512:(half + 1) * 512],
                         ALU.mult, ALU.mult, [kps, (kss, 3), "gsgu3"], [(kvn, half)])
                for g4 in range(2):
                    pm_, kpm_ = c.nxt("pB")
                    for gl in range(4):
                        g = g4 * 4 + gl
                        c.mm(pm_[:, gl * 128:gl * 128 + nt], vn[0:nt, g * 128:(g + 1) * 128], wsT[0:nt, g, 0:nt], True, False,
                             [(kvn, g // 4), "wsT"], [kpm_])
                        c.mm(pm_[:, gl * 128:gl * 128 + nt], k.onesf[0:1, :], brow[0:1, g * 128:g * 128 + nt], False, True,
                             ["brow", "onesf"], [kpm_])
                    _copy(c, "act", sT[:, g4 * 4:(g4 + 1) * 4, tsl], pm_[:, :].rearrange("p (g q) -> p g q", g=4)[:, :, 0:nt],
                          [kpm_], [("sT", blk)])
            stk = [("sT", b_) for b_ in range(9)]
            for which, col in ((0, C_BU), (1, C_BG)):
                for wt in range(2):
                    Wt, wkeys = _loadW(k, "Wt", w_in, col + wt * 512, 512)
                    for ch, (hv, nt, hkeys) in enumerate(chunks):
                        ts_, _ = _tok(ch)
                        for hh in range(4):
                            f = wt * 4 + hh
                            ps, kps = _proj_fm(k, Wt, wkeys, hh * 128, hv, nt, hkeys)
                            if which == 0:
                                _tt(c, "dve", BT[:, f, ts_], ps[:, 0:nt], sT[:, f, ts_], ALU.mult, [kps] + stk, [("BT", f, ch)])
                            else:
                                t1, kt1 = c.nxt("t1")
                                _act(c, t1[:, 0:nt], ps[:, 0:nt], AF.Silu, [kps], [kt1])
                                _tt(c, "dve", BT[:, f, ts_], BT[:, f, ts_], t1[:, 0:nt], ALU.mult, [kt1, ("BT", f, ch)], [("BT", f, ch)])
            _barrier(c.P)
            for wt in range(2):
                Wt, wkeys = _loadW(k, "Wt", w_in, C_CQ + wt * 512, 512)
                for ch, (hv, nt, hkeys) in enumerate(chunks):
                    ts_, _ = _tok(ch)
                    for hl in range(2):
                        hm = wt * 2 + hl
                        fs, sqs = [], []
                        for dh in range(2):
                            ps, kps = _proj_fm(k, Wt, wkeys, (hl * 2 + dh) * 128, hv, nt, hkeys)
                            fa, kfa = c.nxt("fa" if dh == 0 else "fb")
                            _copy(c, "act", fa[:, 0:nt], ps[:, 0:nt], [kps], [kfa])
                            sq, ksq = c.nxt("t1" if dh == 0 else "t2")
                            _act(c, sq[:, 0:nt], ps[:, 0:nt], AF.Square, [kps], [ksq])
                            fs.append((fa, kfa))
                            sqs.append((sq, ksq))
                        p2, kp2 = c.nxt("pB")
                        for dh in range(2):
                            c.mm(p2[:, 0:nt], k.onesf[:], sqs[dh][0][:, 0:nt], dh == 0, dh == 1, [sqs[dh][1], "onesf"], [kp2])
                        rs, krs = c.nxt("t1")
                        _act(c, rs[:, 0:nt], p2[:, 0:nt], AF.Sqrt, [kp2, "epsc"], [krs], scale=1.0 / 256, bias=k.epsc[:])
                        _recip(c, rs[:, 0:nt], rs[:, 0:nt], [krs], [krs])
                        for dh in range(2):
                            _stt(c, cqT[:, 2 * hm + dh, ts_], fs[dh][0][:, 0:nt], gmqT[:, dh:dh + 1], rs[:, 0:nt], ALU.mult, ALU.mult,
                                 [fs[dh][1], krs, "gmq0", "gmq1"], [("cqT", 2 * hm + dh, ch)])
            for ch in range(3):
                ts_, nt = _tok(ch)
                mkT, mvb, mkey = (k.mkTs, k.mvbs, "mems") if ch == 2 else (k.mkT, k.mvb, "memp")
                for hm in range(4):
                    pes = []
                    for mb in range(2):
                        pS, kpS = c.nxt("pA")
                        for dh in range(2):
                            c.mm(pS[:, 0:nt], mkT[:, 2 * hm + dh, mb * 128:(mb + 1) * 128], cqT[:, 2 * hm + dh, ts_], dh == 0, dh == 1,
                                 [mkey, ("cqT", 2 * hm + dh, ch)], [kpS])
                        pe, kpe = c.nxt("pe")
                        _act(c, pe[:, 0:nt], pS[:, 0:nt], AF.Exp, [kpS], [kpe], scale=1.0 / 16)
                        pes.append((pe, kpe))
                    pden, kpden = c.nxt("pB")
                    for mb in range(2):
                        c.mm(pden[:, 0:nt], onesb[:], pes[mb][0][:, 0:nt], mb == 0, mb == 1, [pes[mb][1], "onesb"], [kpden])
                    rd, krd = c.nxt("t2")
                    _recip(c, rd[:, 0:nt], pden[:, 0:nt], [kpden], [krd])
                    for dh in range(2):
                        po, kpo = c.nxt("pB")
                        for mb in range(2):
                            c.mm(po[:, 0:nt], mvb[:, mb, hm * 256 + dh * 128:hm * 256 + (dh + 1) * 128], pes[mb][0][:, 0:nt],
                                 mb == 0, mb == 1, [pes[mb][1], mkey], [kpo])
                        _tt(c, "dve", CT[:, 2 * hm + dh, ts_], po[:, 0:nt], rd[:, 0:nt], ALU.mult, [kpo, krd], [("CT", 2 * hm + dh, ch)])
            for wt in range(2):
                Wt, wkeys = _loadW(k, "Wt", w_in, C_CG + wt * 512, 512)
                for ch, (hv, nt, hkeys) in enumerate(chunks):
                    ts_, _ = _tok(ch)
                    for hh in range(4):
                        f = wt * 4 + hh
                        ps, kps = _proj_fm(k, Wt, wkeys, hh * 128, hv, nt, hkeys)
                        t1, kt1 = c.nxt("t1")
                        _act(c, t1[:, 0:nt], ps[:, 0:nt], AF.Silu, [kps], [kt1])
                        _tt(c, "dve", CT[:, f, ts_], CT[:, f, ts_], t1[:, 0:nt], ALU.mult, [kt1, ("CT", f, ch)], [("CT", f, ch)])
        _barrier(c.P)
        mT = c.sb("mT", [128, KC, TOWN + NS], BF16)
        with _scope(k):
            c.pool("Wp", 3, [128, 8, 256], BF16)
            c.pool("Wr", 3, [128, KC, 256], BF16)
            c.pool("acc", 2, [128, 512], F32)
            wps = [k.din(n, [1024, D]).rearrange("(kc p) n -> p kc n", p=128) for n in ("w_pa", "w_pb", "w_pc")]
            XT = (AT, BT, CT)
            xn_ = ("AT", "BT", "CT")
            for F2 in range(8):
                Wps, Wrs = [], []
                for X in range(3):
                    Wps.append(_loadW(k, "Wp", wps[X], F2 * 256, 256, nkc=8))
                    Wrs.append(_loadW(k, "Wr", w_in, (C_RA, C_RB, C_RC)[X] + F2 * 256, 256))
                for ch, (hv, nt, hkeys) in enumerate(chunks):
                    ts_, _ = _tok(ch)
                    for fl in range(2):
                        f = F2 * 2 + fl
                        acc, kacc = c.nxt("acc")
                        for X in range(3):
                            Wp, wpk = Wps[X]
                            Wr, wrk = Wrs[X]
                            ps1, kps1 = c.nxt("pA")
                            for kc in range(8):
                                c.mm(ps1[:, 0:nt], Wp[:, kc, fl * 128:(fl + 1) * 128], XT[X][:, kc, ts_], kc == 0, kc == 7,
                                     wpk + [(xn_[X], kc, ch)], [kps1])
                            ps2, kps2 = _proj_fm(k, Wr, wrk, fl * 128, hv, nt, hkeys, pool="pB")
                            sg, ksg = c.nxt("t1")
                            _act(c, sg[:, 0:nt], ps2[:, 0:nt], AF.Sigmoid, [kps2], [ksg])
                            if X == 0:
                                _tt(c, "dve", acc[:, 0:nt], ps1[:, 0:nt], sg[:, 0:nt], ALU.mult, [kps1, ksg], [kacc])
                            else:
                                tm, ktm = c.nxt("t2")
                                _tt(c, "dve", tm[:, 0:nt], ps1[:, 0:nt], sg[:, 0:nt], ALU.mult, [kps1, ksg], [ktm])
                                _tt(c, "pool", acc[:, 0:nt], acc[:, 0:nt], tm[:, 0:nt], ALU.add, [kacc, ktm], [kacc])
                        _copy(c, "act", mT[:, f, ts_], acc[:, 0:nt], [kacc], [("mT", f, ch)])
        _barrier(c.P)
        with _scope(k):
            c.pool("Wt", 2, [128, KC, 512], BF16)
            c.pool("xq", 3, [128, 512], F32)
            c.pool("yq", 3, [128, 512], F32)
            w_out = k.din("w_out", [D, D]).rearrange("(kc p) n -> p kc n", p=128)
            yo = k.dout("y_out", [TOWN, D])
            yso = k.dout("ys_out", [NS, D])
            xall, xs = k.dr["xall"], k.dr["xs"]
            for ct in range(4):
                Wt, wkeys = _loadW(k, "Wt", w_out, ct * 512, 512)
                for blk in range(9):
                    samp = blk == 8
                    nt = NS if samp else 128
                    tsl = slice(1024, 1024 + NS) if samp else slice(blk * 128, (blk + 1) * 128)
                    ch = 2 if samp else blk // 4
                    xq, kxq = c.nxt("xq")
                    src = xs[:, ct * 512:(ct + 1) * 512] if samp else xall[(4 * blk) * 128:(4 * blk + 1) * 128, ct * 512:(ct + 1) * 512]
                    c.dma(xq[0:nt, :], src, [], [kxq], kxq)
                    ps, kps = c.nxt("pA")
                    for kc in range(KC):
                        c.mm(ps[0:nt, :], mT[:, kc, tsl], Wt[:, kc, :], kc == 0, kc == KC - 1, wkeys + [("mT", kc, ch)], [kps])
                    yq, kyq = c.nxt("yq")
                    _tt(c, "dve", yq[0:nt, :], ps[0:nt, :], xq[0:nt, :], ALU.add, [kps, kxq], [kyq])
                    dst = yso[:, ct * 512:(ct + 1) * 512] if samp else yo[blk * 128:(blk + 1) * 128, ct * 512:(ct + 1) * 512]
                    c.dma(dst, yq[0:nt, :], [kyq], [], kyq, q="pool")


LS = NPG * 128 + NS


def _igather(k, out, src, idx_col, reads, writes, key):
    k.c.P.add("pool", lambda e: e.indirect_dma_start(out=out, out_offset=None, in_=src,
                                                     in_offset=bass.IndirectOffsetOnAxis(ap=idx_col, axis=0)),
              reads, writes, dma=key)


def phase2s(k):
    c = k.c
    cki = k.din("cki", [1280 * 128, 64])
    ck = k.din("ck", [1280 * 128, 512])
    cv = k.din("cv", [1280 * 128, 512])
    with _scope(k):
        scs = c.sb("scs", [NS, LS], F32)
        Ms = c.sb("Ms", [NS, LS], BF16)
        junk = Ms
        st = c.sb("thrs", [128, 16], F32)
        hwts = c.sb("hwts", [128, 32], F32)
        oat = c.sb("oats", [128, 8, 128], BF16)
        rden = c.sb("rdens", [128, 8], F32)
        ptb = c.sb("ptb", [128, NPG], I32)
        idx = c.sb("idx", [128, NPG], I32)
        pidx = c.sb("pidx", [128, 1], F32)
        mbs = c.sb("mbs", [NS, NS], F32)
        sels = c.sb("sels", [64, NS], BF16)
        BDs = c.sb("BDs", [128, 8, 64], BF16)
        Wss = c.sb("Wss", [64, 8, NS], BF16)
        MTn = c.sb("MTn", [NS, NS], BF16)
        c.pool("kid", 3, [128, 128], F32)
        c.pool("kiTp", 2, [128, 512], BF16)
        c.pool("rl", 3, [64, 512], BF16)
        c.pool("Kp", 3, [128, 512], F32)
        c.pool("Vp", 3, [128, 512], F32)
        c.pool("VEp", 3, [128, 4, 130], BF16)
        c.pool("KTp", 2, [128, 4, 128], BF16)
        c.pool("MTp", 2, [128, NS], BF16)
        c.pool("ex", 3, [128, 512], BF16)
        c.pool("pm", 3, [128, 4, 128], BF16)
        c.pool("pD", 2, [128, 512], F32, psum=True)
        c.pool("pSC", 1, [128, 512], F32, psum=True)
        c.pool("pT", 1, [128, 8, 128], BF16, psum=True)
        c.pool("pTf", 1, [128, 4, 128], F32, psum=True)
        c.pool("po", 3, [128, 512], F32, psum=True)
        c.pool("xblk", 1, [128, 1024], F32)
        c.pool("xn", 1, [128, 1024], BF16)
        cmk = k.din("cmk", [256, 1024])
        cmv = k.din("cmv", [256, 1024])
        for half in range(2):
            for t in range(2):
                ob, kob = c.nxt("xblk")
                c.dma(ob[:, 0:1024], (cmk if half == 0 else cmv)[t * 128:(t + 1) * 128, :], [], [kob], kob)
                k.mem_pack(ob, kob, half, t, k.mkTs, k.mvbs, "mems")
        c.dma(ptb[:], k.din("ptab_bc", [128, NPG], I32)[:, :], [], ["ptb"], "c40")
        c.dma(pidx[:], k.din("pidx", [128, 1])[:, :], [], ["pidx"], "c41")
        c.dma(mbs[:], k.din("maskbs", [NS, NS])[:, :], [], ["mbs"], "c42")
        c.dma(sels[:], k.din("sels", [64, NS])[:, :], [], ["sels"], "c43", q="pool")
        _ts(c, "dve", idx[:, :], ptb[:, :], 128.0, pidx[:, 0:1], ALU.mult, ALU.add, ["ptb", "pidx"], ["idx"])
        for i in range(3):
            ve = c.rot["VEp"][0][i]
            c.op("pool", (lambda b: (lambda e: e.memset(b[:, :, 128:129], 1.0)))(ve), [], [(("VEp", i), "one")])
        c.op("pool", lambda e: e.memset(BDs[:, :, :], 0.0), [], ["BDs"])
        for cc in range(8):
            for jj in range(2):
                rows = slice(jj * 64, (jj + 1) * 64)
                _copy(c, "pool", BDs[rows, cc, jj * 32:jj * 32 + NS], k.qiTs[rows, cc, :], ["qiTs", "BDs"], ["BDs"])
            _ts(c, "pool", Wss[:, cc, :], sels[:, :], k.wvs[:, cc:cc + 1], None, ALU.mult, ALU.bypass, ["sels", "wvs"], ["Wss"])

        def index_chunk(rhs, n, rkeys, col0, last):
            psc, kpsc = c.nxt("pSC")
            rls = {}

            def stage0(cc):
                pd, kpd = c.nxt("pD")
                c.mm(pd[0:64, 0:n], BDs[:, cc, :], rhs, True, True, ["BDs"] + rkeys, [kpd])
                rl, krl = c.nxt("rl")
                _act(c, rl[:, 0:n], pd[0:64, 0:n], AF.Relu, [kpd], [krl])
                rls[cc] = (rl, krl)

            stage0(0)
            for cc in range(8):
                if cc + 1 < 8:
                    stage0(cc + 1)
                rl, krl = rls.pop(cc)
                c.mm(psc[0:NS, 0:n], Wss[:, cc, :], rl[:, 0:n], cc == 0, cc == 7, [krl, "Wss"], [kpsc])
            if last:
                _tt(c, "dve", scs[:, col0:col0 + n], psc[0:NS, 0:n], mbs[:, :], ALU.add, [kpsc, "mbs"], ["scs"])
            else:
                _copy(c, "act", scs[:, col0:col0 + n], psc[0:NS, 0:n], [kpsc], ["scs"])

        for c4 in range(NPG // 4):
            pt, kpt = c.nxt("pTf")
            for i in range(4):
                pg = c4 * 4 + i
                kid, kkid = c.nxt("kid")
                _igather(k, kid[:, 0:64], cki[:, :], idx[:, pg:pg + 1], ["idx"], [(kkid, 0)], (kkid, 0))
                _copy(c, "dve", kid[:, 64:128], kid[:, 0:64], [(kkid, 0)], [(kkid, 1)])
                c.tr(pt[:, i, :], kid[:, :], k.identf[:, :], [(kkid, 0), (kkid, 1), "identf"], [kpt])
            kiTp, kkt = c.nxt("kiTp")
            _copy(c, "act", kiTp[:, :].rearrange("p (a q) -> p a q", a=4), pt[:, 0:4, :], [kpt], [kkt])
            index_chunk(kiTp[:, :], 512, [kkt], c4 * 512, False)
        index_chunk(k.kiTs[:, :], NS, ["kiTs"], NPG * 128, True)
        st4 = st[0:NS, :]
        c.op("dve", lambda e: e.tensor_reduce(out=st4[:, 0:1], in_=scs[:, 0:NPG * 128], axis=AX.X, op=ALU.min), ["scs"], ["thr"])
        c.op("dve", lambda e: e.tensor_reduce(out=st4[:, 1:2], in_=scs[:, :], axis=AX.X, op=ALU.max), ["scs"], ["thr"])
        _threshold(k, scs[:, :], LS, st4, junk[:, :], hwts[0:NS, :])
        _ts(c, "dve", Ms[:, :], scs[:, :], st4[:, 0:1], None, ALU.is_ge, ALU.bypass, ["thr", "scs"], ["Ms"])
        po, kpo = [], []
        for i in range(3):
            a, b = c.nxt("po")
            po.append(a)
            kpo.append(b)
        qrhs = [k.QTs[:, 2 * g:2 * g + 2, :].rearrange("p h q -> p (h q)") for g in range(4)]
        def prep(pg):
            Kp, kKp = c.nxt("Kp")
            _igather(k, Kp[:, :], ck[:, :], idx[:, pg:pg + 1], ["idx"], [kKp], kKp)
            Vp, kVf = c.nxt("Vp")
            _igather(k, Vp[:, :], cv[:, :], idx[:, pg:pg + 1], ["idx"], [kVf], kVf)
            VEp, kVp = c.nxt("VEp")
            _copy(c, "act", VEp[:, :, 0:128], Vp[:, :].rearrange("p (g d) -> p g d", g=4), [kVf], [kVp])
            ptf, kptf = c.nxt("pTf")
            for g in range(4):
                c.tr(ptf[:, g, :], Kp[:, g * 128:(g + 1) * 128], k.identf[:, :], [kKp, "identf"], [kptf])
            KTp, kKT = c.nxt("KTp")
            _copy(c, "act", KTp[:, :, :], ptf[:, :, :], [kptf], [kKT])
            pt, kpt = c.nxt("pT")
            c.tr(pt[:, 4, 0:NS], Ms[:, pg * 128:(pg + 1) * 128], k.identb[0:NS, 0:NS], ["Ms", "identb"], [kpt])
            MTp, kMT = c.nxt("MTp")
            _copy(c, "act", MTp[:, :], pt[:, 4, 0:NS], [kpt], [kMT])
            return dict(KTg=[KTp[:, g, :] for g in range(4)], ktk=[kKT], VEg=[VEp[:, g, :] for g in range(4)],
                        vek=[kVp, (kVp, "one")], nkeys=128, MT=MTp[:, :], mtk=kMT)

        def prep_new():
            pt, kpt = c.nxt("pT")
            c.tr(pt[0:NS, 0, 0:NS], Ms[:, NPG * 128:LS], k.identb[0:NS, 0:NS], ["Ms", "identb"], [kpt])
            _copy(c, "act", MTn[:, :], pt[0:NS, 0, 0:NS], [kpt], ["MTn"])
            return dict(KTg=[k.KTs[:, g, :] for g in range(4)], ktk=[("KTs", g) for g in range(4)],
                        VEg=[k.VEs[0:NS, g, :] for g in range(4)], vek=["VEs", "VEs1"], nkeys=NS, MT=MTn[:, :], mtk="MTn")

        def s1(P, gp):
            return _att_s1(k, NS, P["KTg"], P["ktk"], P["nkeys"], qrhs, ["QTs"], P["MT"], P["mtk"], gp, "dve")

        def s2(P, gp, stt_, first, last):
            _att_s2(k, NS, stt_[0], stt_[1], P["VEg"], P["vek"], P["nkeys"], gp, po, kpo, first, last)

        import os
        if os.environ.get("PIPE_S", "0") == "1":
            cur = prep(0)
            sts = [s1(cur, 0), s1(cur, 1)]
            for pg in range(NPG + 1):
                nxtP = prep(pg + 1) if pg + 1 < NPG else (prep_new() if pg + 1 == NPG else None)
                nsts = []
                for gp in range(2):
                    s2(cur, gp, sts[gp], pg == 0, pg == NPG)
                    if nxtP is not None:
                        nsts.append(s1(nxtP, gp))
                cur, sts = nxtP, nsts
        else:
            for pg in range(NPG + 1):
                cur = prep(pg) if pg < NPG else prep_new()
                sa = s1(cur, 0)
                sb = s1(cur, 1)
                s2(cur, 0, sa, pg == 0, pg == NPG)
                s2(cur, 1, sb, pg == 0, pg == NPG)
        _attn_finish(k, NS, po, kpo, oat, rden, k.QTs[:, :, :], ["QTs"])
```
